# Optimizing a Trainium2 kernel written in Bass

```python
import numpy as np
import jax
import jax.numpy as jnp
from jax import lax

D_MODEL = 4096
BATCH = 4
SEQ = 4096
DEPTH = 1

HEAD_DIM = 128
N_HEADS = D_MODEL // HEAD_DIM
NSA_HEADS = N_HEADS // 2
MLA_HEADS = N_HEADS - NSA_HEADS
NSA_KV_GROUPS = 4
NSA_Q_PER_KV = NSA_HEADS // NSA_KV_GROUPS
CMP_LEN = 32
CMP_STRIDE = 16
CMP_HIDDEN = 256
SEL_LEN = 64
N_SEL = 16
N_LOCAL_FORCED = 2
FORCED_BONUS = 1e4
WINDOW = 512
MLA_Q_RANK = 1536
MLA_KV_RANK = 512
MLA_NOPE = 128
MLA_ROPE = 64
MLA_V = 128
ROPE_THETA = 500000.0
PARTIAL_ROT = HEAD_DIM // 4
D_FF = 4 * D_MODEL
ALPHA = (2 * DEPTH) ** 0.25
BETA = (8 * DEPTH) ** -0.25
SEL_Q_BLOCK = 32
DENSE_Q_BLOCK = 128
NEG = -1e30
N_ADA = 6
NSA_KV_WIDTH = NSA_KV_GROUPS * HEAD_DIM
IN_SPLITS = (NSA_HEADS * HEAD_DIM, 6 * NSA_KV_WIDTH, NSA_HEADS * 3, MLA_Q_RANK, MLA_KV_RANK, MLA_ROPE)
D_IN = sum(IN_SPLITS)
D_MIX_OUT = NSA_HEADS * HEAD_DIM + MLA_HEADS * MLA_V

kernel_name = 'hybrid_nsa_mla_deepnorm_adaln_block'


def _layernorm(x, g, b, eps=1e-5):
    xf = x.astype(jnp.float32)
    mu = jnp.mean(xf, axis=-1, keepdims=True)
    var = jnp.mean(jnp.square(xf - mu), axis=-1, keepdims=True)
    return ((xf - mu) * lax.rsqrt(var + eps) * g.astype(jnp.float32) + b.astype(jnp.float32)).astype(x.dtype)


def _rmsnorm(x, g, eps=1e-6):
    xf = x.astype(jnp.float32)
    y = xf * lax.rsqrt(jnp.mean(jnp.square(xf), axis=-1, keepdims=True) + eps)
    return (y * g.astype(jnp.float32)).astype(x.dtype)


def _rope_tables(positions, rot_dim):
    inv = jnp.power(ROPE_THETA, -jnp.arange(0, rot_dim, 2, dtype=jnp.float32) / rot_dim)
    ang = positions.astype(jnp.float32)[..., None] * inv
    return jnp.cos(ang)[:, :, None, :], jnp.sin(ang)[:, :, None, :]


def _apply_rope(x, cos, sin):
    half = cos.shape[-1]
    rot = 2 * half
    cos = cos.astype(x.dtype)
    sin = sin.astype(x.dtype)
    x1, x2, rest = x[..., :half], x[..., half:rot], x[..., rot:]
    return jnp.concatenate([x1 * cos - x2 * sin, x2 * cos + x1 * sin, rest], axis=-1)


def _nsa(q, k_cmp, v_cmp, k_sel, v_sel, k_win, v_win, gates, pos_k, pos_v, w_k1, w_k2, w_v1, w_v2):
    B, S = q.shape[0], q.shape[1]
    G, R, Dh = NSA_KV_GROUPS, NSA_Q_PER_KV, HEAD_DIM
    scale = Dh ** -0.5
    qg = q.reshape(B, S, G, R, Dh)
    t = jnp.arange(S)

    ncmp = (S - CMP_LEN) // CMP_STRIDE + 1
    blk = np.arange(ncmp)[:, None] * CMP_STRIDE + np.arange(CMP_LEN)[None, :]

    def compress(kv, pos, w1, w2):
        blocks = kv[:, blk] + pos[:, None, :]
        flat = blocks.transpose(0, 3, 1, 2, 4).reshape(B, G, ncmp, CMP_LEN * Dh)
        return jax.nn.gelu(flat @ w1) @ w2

    kc = compress(k_cmp, pos_k, w_k1, w_k2)
    vc = compress(v_cmp, pos_v, w_v1, w_v2)
    cmp_end = np.arange(ncmp) * CMP_STRIDE + CMP_LEN - 1
    mask_cmp = cmp_end[None, :] <= t[:, None]
    s_cmp = jnp.einsum('bsgrd,bgnd->bgrsn', qg, kc).astype(jnp.float32) * scale
    p_cmp = jax.nn.softmax(jnp.where(mask_cmp, s_cmp, NEG), axis=-1) * mask_cmp
    o_cmp = jnp.einsum('bgrsn,bgnd->bsgrd', p_cmp.astype(vc.dtype), vc)

    nb = S // SEL_LEN
    n_sel = min(N_SEL, nb)
    c_start = np.arange(ncmp) * CMP_STRIDE
    b_start = np.arange(nb) * SEL_LEN
    overlap = ((c_start[:, None] < b_start[None, :] + SEL_LEN) &
               (c_start[:, None] + CMP_LEN > b_start[None, :])).astype(np.float32)
    imp = jnp.einsum('bgrsn,nj->bgsj', p_cmp, jnp.asarray(overlap))
    cur = (t // SEL_LEN)[:, None]
    j = jnp.arange(nb)[None, :]
    forced = (j == 0) | ((j <= cur) & (j > cur - N_LOCAL_FORCED))
    imp = jnp.where(j > cur, NEG, imp + jnp.where(forced, FORCED_BONUS, 0.0))
    _, sel_idx = lax.top_k(imp, n_sel)

    QC = min(SEL_Q_BLOCK, S)
    nqc = S // QC
    k_sel_b = k_sel.transpose(0, 2, 1, 3).reshape(B, G, nb, SEL_LEN, Dh)
    v_sel_b = v_sel.transpose(0, 2, 1, 3).reshape(B, G, nb, SEL_LEN, Dh)
    pad = ((0, 0), (0, 0), (WINDOW, 0), (0, 0))
    k_win_p = jnp.pad(k_win.transpose(0, 2, 1, 3), pad)
    v_win_p = jnp.pad(v_win.transpose(0, 2, 1, 3), pad)
    q_chunks = qg.reshape(B, nqc, QC, G, R, Dh).transpose(1, 0, 2, 3, 4, 5)
    idx_chunks = sel_idx.reshape(B, G, nqc, QC, n_sel).transpose(2, 0, 1, 3, 4)
    gather_blocks = jax.vmap(jax.vmap(lambda kb, ix: kb[ix]))

    def chunk(args):
        qc, ic, ci = args
        start = ci * QC
        tq = start + jnp.arange(QC)
        ks = gather_blocks(k_sel_b, ic)
        vs = gather_blocks(v_sel_b, ic)
        kpos = ic[..., None] * SEL_LEN + jnp.arange(SEL_LEN)
        m_sel = (kpos <= tq[None, None, :, None, None]).reshape(B, G, 1, QC, n_sel * SEL_LEN)
        s_sel = jnp.einsum('bqgrd,bgqnld->bgrqnl', qc, ks).astype(jnp.float32) * scale
        s_sel = s_sel.reshape(B, G, R, QC, n_sel * SEL_LEN)
        p_sel = jax.nn.softmax(jnp.where(m_sel, s_sel, NEG), axis=-1).reshape(B, G, R, QC, n_sel, SEL_LEN)
        o_sel = jnp.einsum('bgrqnl,bgqnld->bqgrd', p_sel.astype(vs.dtype), vs)

        kw = lax.dynamic_slice_in_dim(k_win_p, start, WINDOW + QC, axis=2)
        vw = lax.dynamic_slice_in_dim(v_win_p, start, WINDOW + QC, axis=2)
        wpos = start - WINDOW + jnp.arange(WINDOW + QC)
        diff = tq[:, None] - wpos[None, :]
        m_win = (diff >= 0) & (diff < WINDOW) & (wpos[None, :] >= 0)
        s_win = jnp.einsum('bqgrd,bgkd->bgrqk', qc, kw).astype(jnp.float32) * scale
        p_win = jax.nn.softmax(jnp.where(m_win, s_win, NEG), axis=-1)
        o_win = jnp.einsum('bgrqk,bgkd->bqgrd', p_win.astype(vw.dtype), vw)
        return o_sel, o_win

    o_sel, o_win = lax.map(chunk, (q_chunks, idx_chunks, jnp.arange(nqc)))
    o_sel = o_sel.transpose(1, 0, 2, 3, 4, 5).reshape(B, S, G, R, Dh)
    o_win = o_win.transpose(1, 0, 2, 3, 4, 5).reshape(B, S, G, R, Dh)

    g = gates.reshape(B, S, G, R, 3)
    out = g[..., 0:1] * o_cmp + g[..., 1:2] * o_sel + g[..., 2:3] * o_win
    return out.reshape(B, S, NSA_HEADS * Dh)


def _causal_attention(q, k, v, scale):
    B, S, H, Dk = q.shape
    QB = min(DENSE_Q_BLOCK, S)
    nqb = S // QB
    q_blocks = q.reshape(B, nqb, QB, H, Dk).transpose(1, 0, 2, 3, 4)
    kpos = jnp.arange(S)

    def block(args):
        qb, bi = args
        tq = bi * QB + jnp.arange(QB)
        s = jnp.einsum('bqhd,bkhd->bhqk', qb, k).astype(jnp.float32) * scale
        p = jax.nn.softmax(jnp.where(kpos[None, :] <= tq[:, None], s, NEG), axis=-1)
        return jnp.einsum('bhqk,bkhd->bqhd', p.astype(v.dtype), v)

    o = lax.map(block, (q_blocks, jnp.arange(nqb)))
    return o.transpose(1, 0, 2, 3, 4).reshape(B, S, H * v.shape[-1])


def _mla(c_q, c_kv, k_rope, q_norm, kv_norm, w_uq, w_ukv, cos, sin):
    B, S = c_q.shape[0], c_q.shape[1]
    H = MLA_HEADS
    q = (_rmsnorm(c_q, q_norm) @ w_uq).reshape(B, S, H, MLA_NOPE + MLA_ROPE)
    q = jnp.concatenate([q[..., :MLA_NOPE], _apply_rope(q[..., MLA_NOPE:], cos, sin)], axis=-1)
    kv = (_rmsnorm(c_kv, kv_norm) @ w_ukv).reshape(B, S, H, MLA_NOPE + MLA_V)
    k_nope, v = kv[..., :MLA_NOPE], kv[..., MLA_NOPE:]
    k_pe = _apply_rope(k_rope[:, :, None, :], cos, sin)
    k = jnp.concatenate([k_nope, jnp.broadcast_to(k_pe, (B, S, H, MLA_ROPE))], axis=-1)
    return _causal_attention(q, k, v, (MLA_NOPE + MLA_ROPE) ** -0.5)


def _mixer(h, w_in, pos_k, pos_v, k1, k2, v1, v2, q_norm, kv_norm, w_uq, w_ukv, w_out,
           cos_p, sin_p, cos_m, sin_m):
    B, S = h.shape[0], h.shape[1]
    proj = h @ w_in
    q_nsa, kv_nsa, gate_logits, c_q, c_kv, k_rope = jnp.split(
        proj, np.cumsum(IN_SPLITS)[:-1].tolist(), axis=-1)
    q_nsa = _apply_rope(q_nsa.reshape(B, S, NSA_HEADS, HEAD_DIM), cos_p, sin_p)
    kv = kv_nsa.reshape(B, S, 6, NSA_KV_GROUPS, HEAD_DIM)
    k_cmp = _apply_rope(kv[:, :, 0], cos_p, sin_p)
    k_sel = _apply_rope(kv[:, :, 2], cos_p, sin_p)
    k_win = _apply_rope(kv[:, :, 4], cos_p, sin_p)
    gates = jax.nn.sigmoid(gate_logits).reshape(B, S, NSA_HEADS, 3)
    o_nsa = _nsa(q_nsa, k_cmp, kv[:, :, 1], k_sel, kv[:, :, 3], k_win, kv[:, :, 5], gates,
                 pos_k, pos_v, k1, k2, v1, v2)
    o_mla = _mla(c_q, c_kv, k_rope, q_norm, kv_norm, w_uq, w_ukv, cos_m, sin_m)
    return jnp.concatenate([o_nsa, o_mla], axis=-1) @ w_out


def setup_inputs(seed: int = 0) -> dict:
    key = jax.random.key(seed)
    ks = jax.random.split(key, 24)
    f32 = jnp.float32
    L = DEPTH

    def nrm(k, shape, scale):
        return jax.random.normal(k, shape, f32) * scale

    return {
        'x': nrm(ks[0], (BATCH, SEQ, D_MODEL), 1.0),
        'c': nrm(ks[1], (BATCH, D_MODEL), 1.0),
        'positions': jnp.tile(jnp.arange(SEQ, dtype=jnp.int32)[None, :], (BATCH, 1)),
        'w_ada': nrm(ks[2], (L, D_MODEL, N_ADA * D_MODEL), 0.5 * D_MODEL ** -0.5),
        'b_ada': nrm(ks[3], (L, N_ADA * D_MODEL), 0.02),
        'w_in': nrm(ks[4], (L, D_MODEL, D_IN), D_MODEL ** -0.5),
        'nsa_pos_k': nrm(ks[5], (L, CMP_LEN, HEAD_DIM), 0.1),
        'nsa_pos_v': nrm(ks[6], (L, CMP_LEN, HEAD_DIM), 0.1),
        'nsa_cmp_k1': nrm(ks[7], (L, CMP_LEN * HEAD_DIM, CMP_HIDDEN), (CMP_LEN * HEAD_DIM) ** -0.5),
        'nsa_cmp_k2': nrm(ks[8], (L, CMP_HIDDEN, HEAD_DIM), CMP_HIDDEN ** -0.5),
        'nsa_cmp_v1': nrm(ks[9], (L, CMP_LEN * HEAD_DIM, CMP_HIDDEN), (CMP_LEN * HEAD_DIM) ** -0.5),
        'nsa_cmp_v2': nrm(ks[10], (L, CMP_HIDDEN, HEAD_DIM), CMP_HIDDEN ** -0.5),
        'mla_q_norm': 1.0 + nrm(ks[11], (L, MLA_Q_RANK), 0.02),
        'mla_kv_norm': 1.0 + nrm(ks[12], (L, MLA_KV_RANK), 0.02),
        'mla_w_uq': nrm(ks[13], (L, MLA_Q_RANK, MLA_HEADS * (MLA_NOPE + MLA_ROPE)), MLA_Q_RANK ** -0.5),
        'mla_w_ukv': nrm(ks[14], (L, MLA_KV_RANK, MLA_HEADS * (MLA_NOPE + MLA_V)), MLA_KV_RANK ** -0.5),
        'w_out': nrm(ks[15], (L, D_MIX_OUT, D_MODEL), BETA * D_MIX_OUT ** -0.5),
        'ln1_g': 1.0 + nrm(ks[16], (L, D_MODEL), 0.02),
        'ln1_b': nrm(ks[17], (L, D_MODEL), 0.02),
        'w_ff1': nrm(ks[18], (L, D_MODEL, D_FF), D_MODEL ** -0.5),
        'w_ff2': nrm(ks[19], (L, D_FF, D_MODEL), BETA * D_FF ** -0.5),
        'ln2_g': 1.0 + nrm(ks[20], (L, D_MODEL), 0.02),
        'ln2_b': nrm(ks[21], (L, D_MODEL), 0.02),
    }


def reference(x, c, positions, w_ada, b_ada, w_in, nsa_pos_k, nsa_pos_v, nsa_cmp_k1, nsa_cmp_k2,
              nsa_cmp_v1, nsa_cmp_v2, mla_q_norm, mla_kv_norm, mla_w_uq, mla_w_ukv, w_out,
              ln1_g, ln1_b, w_ff1, w_ff2, ln2_g, ln2_b):
    cos_p, sin_p = _rope_tables(positions, PARTIAL_ROT)
    cos_m, sin_m = _rope_tables(positions, MLA_ROPE)
    cond = jax.nn.silu(c)
    for layer in range(DEPTH):
        mod = (cond @ w_ada[layer] + b_ada[layer])[:, None, :]
        sh_a, sc_a, g_a, sh_m, sc_m, g_m = jnp.split(mod, N_ADA, axis=-1)
        h = x * (1.0 + sc_a) + sh_a
        a = _mixer(h, w_in[layer], nsa_pos_k[layer], nsa_pos_v[layer], nsa_cmp_k1[layer],
                   nsa_cmp_k2[layer], nsa_cmp_v1[layer], nsa_cmp_v2[layer], mla_q_norm[layer],
                   mla_kv_norm[layer], mla_w_uq[layer], mla_w_ukv[layer], w_out[layer],
                   cos_p, sin_p, cos_m, sin_m)
        x = _layernorm(ALPHA * x + (1.0 + g_a) * a, ln1_g[layer], ln1_b[layer])
        h = x * (1.0 + sc_m) + sh_m
        f = jnp.square(jax.nn.relu(h @ w_ff1[layer])) @ w_ff2[layer]
        x = _layernorm(ALPHA * x + (1.0 + g_m) * f, ln2_g[layer], ln2_b[layer])
    return x
```

```python
import math
import os
from contextlib import ExitStack
import numpy as np
import concourse.bass as bass
import concourse.mybir as mybir
from concourse.bass_utils import run_bass_kernel_spmd


F32 = mybir.dt.float32
BF16 = mybir.dt.bfloat16
I32 = mybir.dt.int32
AF = mybir.ActivationFunctionType
ALU = mybir.AluOpType
AX = mybir.AxisListType


class Buf:
    __slots__ = ("name", "last_w", "readers", "dsem", "dcount", "ctx")

    def __init__(self, ctx, name):
        self.ctx = ctx
        self.name = name
        self.last_w = None
        self.readers = {}
        self.dsem = None
        self.dcount = 0


class EngState:
    def __init__(self, name, eng, sem):
        self.name = name
        self.eng = eng
        self.sem = sem
        self.count = 0
        self.known = {}


class Ctx:
    def __init__(self, nc, stack):
        self.nc = nc
        self.stack = stack
        self.E = {}
        for name, eng in (("pe", nc.tensor), ("act", nc.scalar), ("dve", nc.vector),
                          ("pool", nc.gpsimd), ("sp", nc.sync)):
            sem = stack.enter_context(nc.semaphore("sem_" + name))
            self.E[name] = EngState(name, eng, sem)
        self.bufs = []
        self.dma_bufs = []
        self.free_dsems = []
        self.nwaits = 0
        self.ninst = 0

    def buf(self, name):
        b = Buf(self, name)
        self.bufs.append(b)
        return b

    def bufs_n(self, name, n):
        return [self.buf(f"{name}{i}") for i in range(n)]

    def _wait_token(self, es, tok):
        if tok is None:
            return
        if tok[0] == "e":
            _, en, c = tok
            if en == es.name and en == "pe":
                return
            if es.known.get(en, 0) >= c:
                return
            es.eng.wait_ge(self.E[en].sem, c)
            es.known[en] = c
            self.nwaits += 1
        else:
            _, b = tok
            c = b.dcount
            key = ("d", id(b.dsem))
            if es.known.get(key, 0) >= c:
                return
            es.eng.wait_ge(b.dsem, 16 * c)
            es.known[key] = c
            self.nwaits += 1

    def _deps(self, es, reads, writes):
        for b in reads:
            self._wait_token(es, b.last_w)
        for b in writes:
            self._wait_token(es, b.last_w)
            for tok in list(b.readers.values()):
                self._wait_token(es, tok)

    def _commit(self, tok, reads, writes):
        for b in reads:
            if tok[0] == "e":
                b.readers[tok[1]] = tok
            else:
                b.readers[("d", id(tok[1]))] = tok
        for b in writes:
            b.last_w = tok
            b.readers = {}

    def op(self, en, fns, reads=(), writes=()):
        es = self.E[en]
        if callable(fns):
            fns = [fns]
        self._deps(es, reads, writes)
        ins = None
        for f in fns:
            ins = f()
            self.ninst += 1
        es.count += 1
        ins.then_inc(es.sem, 1)
        tok = ("e", en, es.count)
        self._commit(tok, reads, writes)
        return tok

    def op_begin(self, en, reads=(), writes=()):
        es = self.E[en]
        self._deps(es, reads, writes)
        return (en, list(reads), list(writes))

    def op_piece(self, h, fn):
        fn()
        self.ninst += 1

    def op_end(self, h, fn):
        en, reads, writes = h
        es = self.E[en]
        ins = fn()
        self.ninst += 1
        es.count += 1
        ins.then_inc(es.sem, 1)
        tok = ("e", en, es.count)
        self._commit(tok, reads, writes)
        return tok

    def dma(self, q, out, in_, sb, reads=(), writes=(), **kw):
        es = self.E[q]
        if sb.dsem is None:
            if self.free_dsems:
                sb.dsem, sb.dcount = self.free_dsems.pop()
            else:
                sb.dsem = self.stack.enter_context(self.nc.semaphore("dsem%d" % len(self.dma_bufs) + "_" + sb.name))
                sb.dcount = 0
            self.dma_bufs.append(sb)
        self._deps(es, reads, writes)
        ins = es.eng.dma_start(out=out, in_=in_, **kw)
        sb.dcount += 1
        ins.then_inc(sb.dsem, 16)
        self.ninst += 1
        tok = ("d", sb)
        self._commit(tok, reads, writes)
        return tok

    def barrier(self, engines=("pe", "act", "dve", "pool", "sp"), release=True):
        for en in engines:
            es = self.E[en]
            for on, os_ in self.E.items():
                if os_.count == 0 or (on == en and en in ("pe", "sp")):
                    continue
                if es.known.get(on, 0) < os_.count:
                    es.eng.wait_ge(os_.sem, os_.count)
                    es.known[on] = os_.count
            for b in self.dma_bufs:
                key = ("d", id(b.dsem))
                if b.dcount and es.known.get(key, 0) < b.dcount:
                    es.eng.wait_ge(b.dsem, 16 * b.dcount)
                    es.known[key] = b.dcount
        if not release:
            return
        for b in self.bufs:
            b.last_w = None
            b.readers = {}
        for b in self.dma_bufs:
            self.free_dsems.append((b.dsem, b.dcount))
            b.dsem = None
            b.dcount = 0
        self.dma_bufs = []

    def final_wait(self):
        self.barrier(engines=("sp",), release=False)


D = 4096
S = 4096
NT_ALL = 8
NT_MY = 4
TT = 512
THETA = 500000.0
DIN = 7280
DFF = 16384
PI = math.pi


W_IN_SLABS = ([(2048 + (kind * 4 + gp * 2) * 128, 256) for kind in range(6) for gp in range(2)] + [(6704, 256), (6960, 256), (7216, 64)]
              + [(hp * 256, 256) for hp in range(8)] + [(5120, 48)] + [(5168 + sl * 256, 256) for sl in range(6)])
W_IN_SLAB_INDEX = {cn: i for i, cn in enumerate(W_IN_SLABS)}


def host_tile_weights(w_ada, w_in, k1, k2, v1, v2, w_uq, w_ukv, w_out, w_ff1, w_ff2):
    t = {}
    t["w_ada"] = np.ascontiguousarray(w_ada.reshape(32, 128, 48, 512).transpose(2, 1, 0, 3))
    wi = np.zeros((len(W_IN_SLABS), 128, 32, 256), np.float32)
    for i, (c0, n) in enumerate(W_IN_SLABS):
        wi[i, :, :, :n] = w_in[:, c0:c0 + n].reshape(32, 128, n).transpose(1, 0, 2)
    t["w_in"] = wi
    t["k1"] = np.ascontiguousarray(k1.reshape(32, 128, 256).transpose(1, 0, 2))
    t["v1"] = np.ascontiguousarray(v1.reshape(32, 128, 256).transpose(1, 0, 2))
    t["k2"] = np.ascontiguousarray(k2.reshape(2, 128, 128).transpose(1, 0, 2))
    t["v2"] = np.ascontiguousarray(v2.reshape(2, 128, 128).transpose(1, 0, 2))
    t["w_uq"] = np.ascontiguousarray(w_uq.reshape(12, 128, 16, 192).transpose(2, 1, 0, 3))
    t["w_ukv"] = np.ascontiguousarray(w_ukv.reshape(4, 128, 16, 256).transpose(2, 1, 0, 3))
    t["w_out"] = np.ascontiguousarray(w_out.reshape(32, 128, 8, 512).transpose(2, 1, 0, 3))
    t["w_ff1"] = np.ascontiguousarray(w_ff1.reshape(32, 128, 64, 256).transpose(2, 1, 0, 3))
    t["w_ff2"] = np.ascontiguousarray(w_ff2.reshape(64, 2, 128, 4096).transpose(0, 2, 1, 3))
    return t


class K:
    pass


def dram(nc, name, shape, dt, kind=None):
    if kind is None:
        return nc.dram_tensor(name, list(shape), dt).ap()
    return nc.dram_tensor(name, list(shape), dt, kind=kind).ap()


def sb(k, name, shape, dt, stack=None):
    st = stack if stack is not None else k.stack
    t = st.enter_context(k.nc.sbuf_tensor(name, list(shape), dt))
    return t


def ps(k, name, shape, dt, stack=None):
    st = stack if stack is not None else k.stack
    t = st.enter_context(k.nc.psum_tensor(name, list(shape), dt))
    return t


def host_constants(hf):
    c = {}
    c["ident_f"] = np.eye(128, dtype=np.float32)
    pr = np.zeros((128, 128), np.float32)
    for m in range(16):
        pr[m + 16, m] = -1.0
        pr[m, m + 16] = 1.0
    c["prot_p"] = pr
    pm = np.zeros((128, 128), np.float32)
    for m in range(32):
        pm[m + 32, m] = -1.0
        pm[m, m + 32] = 1.0
    c["prot_m"] = pm
    inv = np.zeros((128, 2), np.float32)
    for r in range(32):
        inv[r, 0] = THETA ** (-(2.0 * (r % 16)) / 32.0)
    for r in range(64):
        inv[r, 1] = THETA ** (-(2.0 * (r % 32)) / 64.0)
    c["invf"] = inv
    NEGM = -30000.0
    qq = (np.arange(4)[:, None] * 128 + np.arange(128)[None, :])
    kk = np.arange(512)

    def band(dT):
        diff = dT * 512 + qq.T[:, :, None] - kk[None, None, :]
        return np.where((diff >= 0) & (diff < 512), 0.0, NEGM).astype(np.float32)

    def causal(dT):
        diff = dT * 512 + qq.T[:, :, None] - kk[None, None, :]
        return np.where(diff >= 0, 0.0, NEGM).astype(np.float32)
    c["mA"] = causal(0) if hf == 0 else causal(1)
    c["mB"] = causal(-1) if hf == 0 else causal(0)
    if hf == 0:
        c["mW"] = np.stack([band(1), band(0), band(-1)], axis=0)
    else:
        c["mW"] = np.stack([band(2), band(1), band(0)], axis=0)
    n = np.arange(512)
    mC = np.zeros((4, 128, 4, 512), np.float32)
    vS = np.zeros((4, 128, 4, 64), np.float32)
    aS = np.zeros((4, 128, 4, 64), np.float32)
    j = np.arange(64)
    for m in range(4):
        t = (2 * m + hf) * 512 + qq.T
        ok = (16 * n[None, None, :] + 31 <= t[:, :, None]) & (n[None, None, :] < 255)
        mC[m] = np.where(ok, 0.0, NEGM)
        cur = (t // 64)[:, :, None]
        jj = j[None, None, :]
        valid = jj <= cur
        forced = (jj == 0) | ((jj <= cur) & (jj > cur - 2))
        vS[m] = valid.astype(np.float32)
        aS[m] = np.where(valid, np.where(forced, 1e4, 0.0), -1e30)
    c["mC"] = mC
    c["vS"] = vS
    c["aS"] = aS
    cs = n * 16
    bs = j * 64
    ov = ((cs[:, None] < bs[None, :] + 64) & (cs[:, None] + 32 > bs[None, :])).astype(np.float32)
    ov[255:] = 0.0
    c["ovl"] = np.ascontiguousarray(ov.reshape(4, 128, 64).transpose(1, 0, 2))
    return c


def emit_p0(k, cT, w_ada, b_ada, modrow, ncc=48):
    nc, cx = k.nc, k.cx
    with ExitStack() as st:
        cond_f = sb(k, "p0_condf", [128, 32], F32, st)
        cond_rep = sb(k, "p0_condrep", [128, 32, 128], BF16, st)
        NSL = 3
        slabs = [sb(k, f"p0_slab{i}", [128, 32, 512], BF16, st) for i in range(NSL)]
        slab_b = cx.bufs_n("p0_slab", NSL)
        brow = [sb(k, f"p0_brow{i}", [128, 512], F32, st) for i in range(NSL)]
        brow_b = cx.bufs_n("p0_brow", NSL)
        mrow = [sb(k, f"p0_mrow{i}", [128, 512], F32, st) for i in range(2)]
        mrow_b = cx.bufs_n("p0_mrow", 2)
        pacc = [ps(k, f"p0_pacc{i}", [128, 512], F32, st) for i in range(2)]
        pacc_b = cx.bufs_n("p0_pacc", 2)
        ptr = [ps(k, f"p0_ptr{i}", [128, 512], F32, st) for i in range(2)]
        ptr_b = cx.bufs_n("p0_ptr", 2)
        b_cond = cx.buf("p0_cond")
        b_crep = cx.buf("p0_crep")

        cx.dma("sp", cond_f[:], cT, b_cond, writes=[b_cond])
        cx.op("act", lambda: nc.scalar.activation(out=cond_f[:], in_=cond_f[:], func=AF.Silu),
              reads=[b_cond], writes=[b_cond])
        cx.op("dve", lambda: nc.vector.tensor_copy(
            out=cond_rep[:], in_=cond_f[:].unsqueeze(2).to_broadcast([128, 32, 128])),
            reads=[b_cond], writes=[b_crep])

        def load(cc):
            s = cc % NSL
            cx.dma("pool", slabs[s][:], w_ada[cc], slab_b[s], writes=[slab_b[s]])
            cx.dma("sp", brow[s][:], b_ada[0:1, cc * 512:(cc + 1) * 512].to_broadcast([128, 512]),
                   brow_b[s], writes=[brow_b[s]])

        PRE = NSL - 1
        for cc in range(min(PRE, ncc)):
            load(cc)
        for cc in range(ncc):
            if cc + PRE < ncc:
                load(cc + PRE)
            s = cc % NSL
            pa, pab = pacc[cc % 2], pacc_b[cc % 2]
            cx.op("pe", [(lambda j=j: nc.tensor.matmul(pa[:], cond_rep[:, j, :], slabs[s][:, j, :],
                                                        start=(j == 0), stop=(j == 31))) for j in range(32)],
                  reads=[b_crep, slab_b[s]], writes=[pab])
            v = cc // 8
            plus = 1.0 if v in (1, 2, 4, 5) else 0.0
            mr, mrb = mrow[cc % 2], mrow_b[cc % 2]
            cx.op("dve", lambda: nc.vector.scalar_tensor_tensor(
                out=mr[:], in0=pa[:], scalar=plus, in1=brow[s][:], op0=ALU.add, op1=ALU.add),
                reads=[pab, brow_b[s]], writes=[mrb])
            cx.dma("sp", modrow[v:v + 1, (cc % 8) * 512:(cc % 8 + 1) * 512], mr[0:1, :], mrb, reads=[mrb])
            pt, ptb = ptr[cc % 2], ptr_b[cc % 2]
            cx.op("pe", [(lambda i=i: nc.tensor.transpose(pt[:, i * 128:(i + 1) * 128],
                                                           mr[:, i * 128:(i + 1) * 128], k.ident_f[:]))
                         for i in range(4)],
                  reads=[mrb, k.b_const], writes=[ptb])
            cx.op("act", lambda: nc.scalar.copy(
                out=k.modT[:, cc * 4:(cc + 1) * 4],
                in_=pt[:].rearrange("p (a b) -> p a b", b=128)[:, :, 0]),
                reads=[ptb], writes=[k.b_modT])
        cx.barrier()


def rope_tables(k, st, pos_ap, ntok, tabs, tab_b, tmp, tmp_b, posi, posi_b):
    nc, cx = k.nc, k.cx
    cx.dma("sp", posi[:, 0:ntok], pos_ap.to_broadcast([64, ntok]), posi_b, writes=[posi_b])
    posf = tmp[0]
    cx.op("dve", lambda: nc.vector.tensor_copy(out=posf[:, 0:ntok], in_=posi[:, 0:ntok]),
          reads=[posi_b], writes=[tmp_b[0]])
    ang, t1, t2 = tmp[1], tmp[2], tmp[3]
    b_ang, b1, b2 = tmp_b[1], tmp_b[2], tmp_b[3]
    ti, bi = k.p1_ti, k.p1_ti_b
    N = slice(0, ntok)
    for col, (cn, sn) in ((0, ("cosP", "sinP")), (1, ("cosM", "sinM"))):
        cx.op("dve", lambda: nc.vector.tensor_scalar(
            out=ang[:, N], in0=posf[:, N], scalar1=k.invf[0:64, col:col + 1], scalar2=None,
            op0=ALU.mult), reads=[tmp_b[0], k.b_const], writes=[b_ang])
        for name, shift in ((sn, 0.0), (cn, 0.5 * PI)):
            cx.op("dve", lambda: nc.vector.tensor_scalar(out=t1[:, N], in0=ang[:, N], scalar1=shift, scalar2=None, op0=ALU.add),
                  reads=[b_ang], writes=[b1])
            cx.op("dve", lambda: nc.vector.tensor_scalar(out=t2[:, N], in0=t1[:, N], scalar1=1.0 / (2 * PI), scalar2=None, op0=ALU.mult),
                  reads=[b1], writes=[b2])
            cx.op("dve", lambda: nc.vector.tensor_copy(out=ti[:, N], in_=t2[:, N]), reads=[b2], writes=[bi])
            cx.op("dve", lambda: nc.vector.tensor_copy(out=t2[:, N], in_=ti[:, N]), reads=[bi], writes=[b2])
            cx.op("dve", lambda: nc.vector.scalar_tensor_tensor(out=t1[:, N], in0=t2[:, N], scalar=-2 * PI, in1=t1[:, N], op0=ALU.mult, op1=ALU.add),
                  reads=[b1, b2], writes=[b1])
            cx.op("dve", lambda: nc.vector.tensor_scalar(out=t2[:, N], in0=t1[:, N], scalar1=PI, scalar2=None, op0=ALU.is_gt), reads=[b1], writes=[b2])
            cx.op("dve", lambda: nc.vector.scalar_tensor_tensor(out=t1[:, N], in0=t2[:, N], scalar=-2 * PI, in1=t1[:, N], op0=ALU.mult, op1=ALU.add),
                  reads=[b1, b2], writes=[b1])
            cx.op("dve", lambda: nc.vector.tensor_scalar(out=t2[:, N], in0=t1[:, N], scalar1=-PI, scalar2=None, op0=ALU.is_lt), reads=[b1], writes=[b2])
            cx.op("dve", lambda: nc.vector.scalar_tensor_tensor(out=t1[:, N], in0=t2[:, N], scalar=2 * PI, in1=t1[:, N], op0=ALU.mult, op1=ALU.add),
                  reads=[b1, b2], writes=[b1])
            cx.op("act", lambda: nc.scalar.activation(out=tabs[name][:, N], in_=t1[:, N], func=AF.Sin),
                  reads=[b1], writes=[tab_b[name]])


def emit_p1(k, x_all, xq, pos_all, pos_q, w_in, scr, n_all=NT_ALL, n_my=NT_MY, do_kv=True, do_q=True):
    nc, cx = k.nc, k.cx
    with ExitStack() as st:
        NSL = 3
        CW = 256
        slabs = [sb(k, f"p1_slab{i}", [128, 32, CW], BF16, st) for i in range(NSL)]
        slab_b = cx.bufs_n("p1_slab", NSL)
        hT = [sb(k, f"p1_hT{i}", [128, 32, TT], BF16, st) for i in range(2)]
        hT_b = cx.bufs_n("p1_hT", 2)
        xst = [sb(k, f"p1_x{i}", [128, D], F32, st) for i in range(2)]
        xst_b = cx.bufs_n("p1_x", 2)
        tabs = {n: [sb(k, f"p1_{n}{i}", [64, TT], F32, st) for i in range(2)] for n in ("cosP", "sinP", "cosM", "sinM")}
        tab_b = {n: cx.bufs_n("p1_" + n, 2) for n in tabs}
        tmp = [sb(k, f"p1_tmp{i}", [64, TT], F32, st) for i in range(4)]
        tmp_b = cx.bufs_n("p1_tmp", 4)
        k.p1_ti = sb(k, "p1_ti", [64, TT], I32, st)
        k.p1_ti_b = cx.buf("p1_ti")
        posi = sb(k, "p1_posi", [64, TT], I32, st)
        posi_b = cx.buf("p1_posi")
        NSTG = 4
        stg = [sb(k, f"p1_stg{i}", [128, TT], BF16, st) for i in range(NSTG)]
        stg_b = cx.bufs_n("p1_stg", NSTG)
        stgf = [sb(k, f"p1_stgf{i}", [128, TT], F32, st) for i in range(2)]
        stgf_b = cx.bufs_n("p1_stgf", 2)
        qf = [sb(k, f"p1_qf{i}", [64, TT], F32, st) for i in range(2)]
        qf_b = cx.bufs_n("p1_qf", 2)
        rt = [sb(k, f"p1_rt{i}", [64, TT], F32, st) for i in range(2)]
        rt_b = cx.bufs_n("p1_rt", 2)
        ptr = [ps(k, f"p1_ptr{i}", [128, 512], F32, st) for i in range(2)]
        ptr_b = cx.bufs_n("p1_ptr", 2)
        pacc = [ps(k, f"p1_pacc{i}", [128, 512], F32, st) for i in range(3)]
        pacc_b = cx.bufs_n("p1_pacc", 3)
        ppar = [ps(k, f"p1_ppar{i}", [64, 512], F32, st) for i in range(2)]
        ppar_b = cx.bufs_n("p1_ppar", 2)
        cnt = {"x": 0, "stg": 0, "stgf": 0, "qf": 0, "pacc": 0, "ppar": 0, "sq": 0, "slab": 0, "ptr": 0}

        def nxt(name, n):
            i = cnt[name] % n
            cnt[name] += 1
            return i

        def build_hT(src, row0, slot):
            for tb in range(4):
                xi = nxt("x", 2)
                cx.dma("sp", xst[xi][:], src[row0 + tb * 128: row0 + (tb + 1) * 128, :], xst_b[xi], writes=[xst_b[xi]])
                for j4 in range(8):
                    pi = nxt("ptr", 2)
                    cx.op("pe", [(lambda i=i: nc.tensor.transpose(
                        ptr[pi][:, i * 128:(i + 1) * 128], xst[xi][:, (j4 * 4 + i) * 128:(j4 * 4 + i + 1) * 128], k.ident_f[:]))
                        for i in range(4)], reads=[xst_b[xi], k.b_const], writes=[ptr_b[pi]])
                    for i in range(4):
                        j = j4 * 4 + i
                        cx.op("act", lambda: nc.scalar.activation(
                            out=hT[slot][:, j, tb * 128:(tb + 1) * 128], in_=ptr[pi][:, i * 128:(i + 1) * 128],
                            func=AF.Identity, scale=k.modT[:, 32 + j:33 + j], bias=k.modT[:, j:j + 1]),
                            reads=[ptr_b[pi], k.b_modT], writes=[hT_b[slot]])

        def load_slab(c0, n):
            s = nxt("slab", NSL)
            cx.dma("pool", slabs[s][:, :, 0:n], w_in[W_IN_SLAB_INDEX[(c0, n)], :, :, 0:n], slab_b[s], writes=[slab_b[s]])
            return s

        def proj_fm(s, off, M, slot):
            pi = nxt("pacc", 3)
            cx.op("pe", [(lambda j=j: nc.tensor.matmul(pacc[pi][0:M, :], slabs[s][:, j, off:off + M], hT[slot][:, j, :],
                                                        start=(j == 0), stop=(j == 31))) for j in range(32)],
                  reads=[slab_b[s], hT_b[slot]], writes=[pacc_b[pi]])
            return pi

        def proj_tm(s, n, slot, tb):
            pi = nxt("pacc", 3)
            cx.op("pe", [(lambda j=j: nc.tensor.matmul(pacc[pi][:, 0:n], hT[slot][:, j, tb * 128:(tb + 1) * 128], slabs[s][:, j, 0:n],
                                                        start=(j == 0), stop=(j == 31))) for j in range(32)],
                  reads=[slab_b[s], hT_b[slot]], writes=[pacc_b[pi]])
            return pi

        def rope_evac(pi, M, R, cosn, sinn, prot, ti, dst):
            si = nxt("stg", NSTG)
            qi = nxt("qf", 2)
            cx.op("act", lambda: nc.scalar.copy(out=stg[si][0:M, :], in_=pacc[pi][0:M, :]),
                  reads=[pacc_b[pi]], writes=[stg_b[si]])
            cx.op("act", lambda: nc.scalar.copy(out=qf[qi][0:R, :], in_=pacc[pi][0:R, :]),
                  reads=[pacc_b[pi]], writes=[qf_b[qi]])
            pp = nxt("ppar", 2)
            cx.op("pe", lambda: nc.tensor.matmul(ppar[pp][0:R, :], prot[0:R, 0:R], qf[qi][0:R, :], start=True, stop=True),
                  reads=[qf_b[qi], k.b_const], writes=[ppar_b[pp]])
            cx.op("dve", lambda: nc.vector.tensor_tensor(out=rt[0][0:R, :], in0=qf[qi][0:R, :], in1=tabs[cosn][ti][0:R, :], op=ALU.mult),
                  reads=[qf_b[qi], tab_b[cosn][ti]], writes=[rt_b[0]])
            cx.op("dve", lambda: nc.vector.tensor_tensor(out=rt[1][0:R, :], in0=ppar[pp][0:R, :], in1=tabs[sinn][ti][0:R, :], op=ALU.mult),
                  reads=[ppar_b[pp], tab_b[sinn][ti]], writes=[rt_b[1]])
            cx.op("dve", lambda: nc.vector.tensor_tensor(out=stg[si][0:R, :], in0=rt[0][0:R, :], in1=rt[1][0:R, :], op=ALU.add),
                  reads=[rt_b[0], rt_b[1]], writes=[stg_b[si]])
            cx.dma("sp", dst, stg[si][0:M, :], stg_b[si], reads=[stg_b[si]])

        def plain_evac(pi, M, dst):
            si = nxt("stg", NSTG)
            cx.op("act", lambda: nc.scalar.copy(out=stg[si][0:M, :], in_=pacc[pi][0:M, :]),
                  reads=[pacc_b[pi]], writes=[stg_b[si]])
            cx.dma("sp", dst, stg[si][0:M, :], stg_b[si], reads=[stg_b[si]])

        def plain_evac_f32(pi, M, dst, func=None):
            fi = nxt("stgf", 2)
            if func is None:
                cx.op("act", lambda: nc.scalar.copy(out=stgf[fi][0:M, :], in_=pacc[pi][0:M, :]),
                      reads=[pacc_b[pi]], writes=[stgf_b[fi]])
            else:
                cx.op("act", lambda: nc.scalar.activation(out=stgf[fi][0:M, :], in_=pacc[pi][0:M, :], func=func),
                      reads=[pacc_b[pi]], writes=[stgf_b[fi]])
            cx.dma("sp", dst, stgf[fi][0:M, :], stgf_b[fi], reads=[stgf_b[fi]])

        if do_kv:
            for pair in range(0, n_all, 2):
                tiles = [t for t in (pair, pair + 1) if t < n_all]
                for li, t in enumerate(tiles):
                    build_hT(x_all, t * TT, li)
                    rope_tables(k, st, pos_all[0:1, t * TT:(t + 1) * TT], TT,
                                {n: tabs[n][li] for n in tabs}, {n: tab_b[n][li] for n in tabs}, tmp, tmp_b, posi, posi_b)
                for kind in range(6):
                    for gp in range(2):
                        c0 = 2048 + (kind * 4 + gp * 2) * 128
                        s = load_slab(c0, 256)
                        for li, t in enumerate(tiles):
                            if kind in (0, 2, 4):
                                dst = {0: scr["kcmpT"], 2: scr["kselT"], 4: scr["kwinT"]}[kind]
                                for gi in range(2):
                                    pi = proj_fm(s, gi * 128, 128, li)
                                    rope_evac(pi, 128, 32, "cosP", "sinP", k.prot_p, li, dst[gp * 2 + gi, :, t * TT:(t + 1) * TT])
                            elif kind == 1:
                                for gi in range(2):
                                    pi = proj_fm(s, gi * 128, 128, li)
                                    plain_evac(pi, 128, scr["vcmpT"][gp * 2 + gi, :, t * TT:(t + 1) * TT])
                            else:
                                dst = scr["vsel"] if kind == 3 else scr["vwin"]
                                for tb in range(4):
                                    pi = proj_tm(s, 256, li, tb)
                                    si = nxt("stg", NSTG)
                                    cx.op("act", lambda: nc.scalar.copy(out=stg[si][:, 0:256], in_=pacc[pi][:, 0:256]),
                                          reads=[pacc_b[pi]], writes=[stg_b[si]])
                                    r0 = t * TT + tb * 128
                                    cx.dma("sp", dst[r0:r0 + 128, gp * 256:(gp + 1) * 256], stg[si][:, 0:256], stg_b[si], reads=[stg_b[si]])
                for half in range(2):
                    s2 = load_slab(6704 + half * 256, 256)
                    for li, t in enumerate(tiles):
                        for ci in range(2):
                            pi = proj_fm(s2, ci * 128, 128, li)
                            plain_evac_f32(pi, 128, scr["ckvf"][half * 2 + ci, :, t * TT:(t + 1) * TT])
                s3 = load_slab(7216, 64)
                for li, t in enumerate(tiles):
                    pi = proj_fm(s3, 0, 64, li)
                    rope_evac(pi, 64, 64, "cosM", "sinM", k.prot_m, li, scr["kpeT"][:, t * TT:(t + 1) * TT])
        if do_q:
            for pair in range(0, n_my, 2):
                tiles = [t for t in (pair, pair + 1) if t < n_my]
                for li, t in enumerate(tiles):
                    build_hT(xq, t * TT, li)
                    rope_tables(k, st, pos_q[0:1, t * TT:(t + 1) * TT], TT,
                                {n: tabs[n][li] for n in tabs}, {n: tab_b[n][li] for n in tabs}, tmp, tmp_b, posi, posi_b)
                for hp in range(8):
                    s = load_slab(hp * 256, 256)
                    for li, t in enumerate(tiles):
                        for hi in range(2):
                            pi = proj_fm(s, hi * 128, 128, li)
                            rope_evac(pi, 128, 32, "cosP", "sinP", k.prot_p, li, scr["qT"][hp * 2 + hi, :, t * TT:(t + 1) * TT])
                s = load_slab(5120, 48)
                for li, t in enumerate(tiles):
                    for tb in range(4):
                        pi = proj_tm(s, 48, li, tb)
                        fi = nxt("stgf", 2)
                        cx.op("act", lambda: nc.scalar.activation(out=stgf[fi][:, 0:48], in_=pacc[pi][:, 0:48], func=AF.Sigmoid),
                              reads=[pacc_b[pi]], writes=[stgf_b[fi]])
                        r0 = t * TT + tb * 128
                        cx.dma("sp", scr["gtok"][r0:r0 + 128, :], stgf[fi][:, 0:48], stgf_b[fi], reads=[stgf_b[fi]])
                for sl in range(6):
                    s = load_slab(5168 + sl * 256, 256)
                    for li, t in enumerate(tiles):
                        for ci in range(2):
                            pi = proj_fm(s, ci * 128, 128, li)
                            plain_evac_f32(pi, 128, scr["cqf"][sl * 2 + ci, :, t * TT:(t + 1) * TT])
        cx.barrier()


class Attn:
    def __init__(self, k, st, pfx, extra_bank=False):
        cx = k.cx
        self.k = k
        self.pS = [ps(k, f"{pfx}_pS{i}", [128, 512], F32, st) for i in range(2)]
        self.pS_b = cx.bufs_n(pfx + "_pS", 2)
        self.pT = [ps(k, f"{pfx}_pT{i}", [128, 512], BF16, st) for i in range(2)]
        self.pT_b = cx.bufs_n(pfx + "_pT", 2)
        self.pObank = ps(k, f"{pfx}_pO", [128, 512], F32, st)
        self.pO = [self.pObank[:, i * 128:(i + 1) * 128] for i in range(2)]
        self.pO_b = cx.bufs_n(pfx + "_pO", 2)
        if extra_bank:
            self.pXbank = [ps(k, f"{pfx}_pX{i}", [128, 512], F32, st) for i in range(2)]
            self.pX = [self.pXbank[i][:, 0:64] for i in range(2)]
        else:
            self.pX = [None, None]
        self.pX_b = cx.bufs_n(pfx + "_pX", 2)
        self.Sm = [sb(k, f"{pfx}_Sm{i}", [128, 512], F32, st) for i in range(2)]
        self.Sm_b = cx.bufs_n(pfx + "_Sm", 2)
        self.P = [sb(k, f"{pfx}_P{i}", [128, 512], BF16, st) for i in range(2)]
        self.P_b = cx.bufs_n(pfx + "_P", 2)
        self.PT = [sb(k, f"{pfx}_PT{i}", [128, 512], BF16, st) for i in range(2)]
        self.PT_b = cx.bufs_n(pfx + "_PT", 2)
        self.ls = [sb(k, f"{pfx}_ls{i}", [128, 16], F32, st) for i in range(4)]
        self.ls_b = cx.bufs_n(pfx + "_ls", 4)
        self.rl = sb(k, pfx + "_rl", [128, 4], F32, st)
        self.rl_b = cx.buf(pfx + "_rl")
        self.jobs = []
        self.nrun = 0
        self.done_t = 0
        self.done_pv = 0

    def run(self, qk_fns, qk_reads, ntiles, width, masks, scale, v_fn, v_reads, finish, x_fn=None, x_reads=(), copy_eng="dve"):
        r = self.nrun
        self.nrun += 1
        for kt in range(ntiles):
            self.jobs.append(dict(run=r, kt=kt, nt=ntiles, width=width, qk=qk_fns(kt), qk_reads=qk_reads, masks=masks(kt), scale=scale,
                                  v_fn=v_fn, v_reads=v_reads, finish=finish, x_fn=x_fn, x_reads=list(x_reads), copy_eng=copy_eng))
            self._step()

    def _qk(self, i):
        k = self.k
        nc, cx = k.nc, k.cx
        jb = self.jobs[i]
        b = i % 2
        w = jb["width"]
        pS = self.pS[b]
        cx.op("pe", [(lambda f=f: f(pS[:, 0:w])) for f in jb["qk"]], reads=jb["qk_reads"], writes=[self.pS_b[b]])
        src, src_b = pS, self.pS_b[b]
        for (map_, mreads) in jb["masks"]:
            o_ap, i_ap = self.Sm[b][:, 0:w], src[:, 0:w]
            if len(map_.shape) == 3:
                o_ap = o_ap.rearrange("p (a b) -> p a b", b=map_.shape[2])
                i_ap = i_ap.rearrange("p (a b) -> p a b", b=map_.shape[2])
            cx.op("dve", lambda: nc.vector.tensor_tensor(out=o_ap, in0=i_ap, in1=map_, op=ALU.add),
                  reads=[src_b] + mreads, writes=[self.Sm_b[b]])
            src, src_b = self.Sm[b], self.Sm_b[b]
        li = jb["run"] % 4
        kt = jb["kt"]
        cx.op("act", lambda: nc.scalar.activation(out=self.P[b][:, 0:w], in_=src[:, 0:w], func=AF.Exp, scale=jb["scale"],
                                                  accum_out=self.ls[li][:, kt:kt + 1]),
              reads=[src_b], writes=[self.P_b[b], self.ls_b[li]])

    def _t(self, i):
        k = self.k
        nc, cx = k.nc, k.cx
        jb = self.jobs[i]
        b = i % 2
        w = jb["width"]
        nkb = w // 128
        cx.op("pe", [(lambda kb=kb: nc.tensor.transpose(self.pT[b][:, kb * 128:(kb + 1) * 128], self.P[b][:, kb * 128:(kb + 1) * 128], k.ident_b[:]))
                     for kb in range(nkb)], reads=[self.P_b[b], k.b_const], writes=[self.pT_b[b]])
        if jb["copy_eng"] == "act":
            cx.op("act", lambda: nc.scalar.copy(out=self.PT[b][:, 0:w], in_=self.pT[b][:, 0:w]), reads=[self.pT_b[b]], writes=[self.PT_b[b]])
        else:
            cx.op("dve", lambda: nc.vector.tensor_copy(out=self.PT[b][:, 0:w], in_=self.pT[b][:, 0:w]), reads=[self.pT_b[b]], writes=[self.PT_b[b]])

    def _pv(self, i):
        k = self.k
        nc, cx = k.nc, k.cx
        jb = self.jobs[i]
        b = i % 2
        w = jb["width"]
        nkb = w // 128
        oi = jb["run"] % 2
        kt, nt = jb["kt"], jb["nt"]
        cx.op("pe", [(lambda kb=kb: nc.tensor.matmul(self.pO[oi], self.PT[b][:, kb * 128:(kb + 1) * 128], jb["v_fn"](kt, kb),
                                                      start=(kt == 0 and kb == 0), stop=(kt == nt - 1 and kb == nkb - 1)))
                     for kb in range(nkb)], reads=[self.PT_b[b]] + jb["v_reads"], writes=[self.pO_b[oi]])
        if jb["x_fn"] is not None and os.environ.get("P4_X", "1") == "1":
            cx.op("pe", [(lambda kb=kb: nc.tensor.matmul(self.pX[oi], self.PT[b][:, kb * 128:(kb + 1) * 128], jb["x_fn"](kb),
                                                          start=(kb == 0), stop=(kb == nkb - 1)))
                         for kb in range(nkb)], reads=[self.PT_b[b]] + jb["x_reads"], writes=[self.pX_b[oi]])
        if kt == nt - 1:
            jb["finish"](oi, jb["run"] % 4)
        self.jobs[i] = None

    def _step(self):
        i = len(self.jobs) - 1
        self._qk(i)
        if i - 1 >= 0:
            self._t(i - 1)
            self.done_t = i
        if i - 2 >= 0:
            self._pv(i - 2)
            self.done_pv = i - 1

    def drain(self):
        n = len(self.jobs)
        for i in range(self.done_t, n):
            self._t(i)
        for i in range(self.done_pv, n):
            self._pv(i)
        self.jobs = []
        self.done_t = 0
        self.done_pv = 0

    def rsum(self, li, ntiles, col):
        k = self.k
        nc, cx = k.nc, k.cx
        cx.op("dve", lambda: nc.vector.tensor_reduce(out=self.rl[:, col:col + 1], in_=self.ls[li][:, 0:ntiles], axis=AX.X, op=ALU.add),
              reads=[self.ls_b[li]], writes=[self.rl_b])
        cx.op("dve", lambda: nc.vector.tensor_scalar(out=self.rl[:, col:col + 1], in0=self.rl[:, col:col + 1], scalar1=1e-30, scalar2=None, op0=ALU.max),
              reads=[self.rl_b], writes=[self.rl_b])
        cx.op("dve", lambda: nc.vector.reciprocal(out=self.rl[:, col:col + 1], in_=self.rl[:, col:col + 1]),
              reads=[self.rl_b], writes=[self.rl_b])


class AttnSync:
    def __init__(self, k, st, pfx):
        cx = k.cx
        self.k = k
        self.pS = [ps(k, f"{pfx}_pS{i}", [128, 512], F32, st) for i in range(2)]
        self.pS_b = cx.bufs_n(pfx + "_pS", 2)
        self.pT = [ps(k, f"{pfx}_pT{i}", [128, 512], BF16, st) for i in range(2)]
        self.pT_b = cx.bufs_n(pfx + "_pT", 2)
        self.pObank = ps(k, f"{pfx}_pO", [128, 512], F32, st)
        self.pO = [self.pObank[:, i * 128:(i + 1) * 128] for i in range(2)]
        self.pO_b = cx.bufs_n(pfx + "_pO", 2)
        self.Sm = [sb(k, f"{pfx}_Sm{i}", [128, 512], F32, st) for i in range(2)]
        self.Sm_b = cx.bufs_n(pfx + "_Sm", 2)
        self.P = [sb(k, f"{pfx}_P{i}", [128, 512], BF16, st) for i in range(2)]
        self.P_b = cx.bufs_n(pfx + "_P", 2)
        self.PT = [sb(k, f"{pfx}_PT{i}", [128, 512], BF16, st) for i in range(2)]
        self.PT_b = cx.bufs_n(pfx + "_PT", 2)
        self.ls = [sb(k, f"{pfx}_ls{i}", [128, 16], F32, st) for i in range(2)]
        self.ls_b = cx.bufs_n(pfx + "_ls", 2)
        self.rl = sb(k, pfx + "_rl", [128, 4], F32, st)
        self.rl_b = cx.buf(pfx + "_rl")
        self.c = {"S": 0, "T": 0, "O": 0, "Sm": 0, "P": 0, "PT": 0, "ls": 0}

    def nx(self, n, m=2):
        i = self.c[n] % m
        self.c[n] += 1
        return i

    def run(self, qk_fns, qk_reads, ntiles, width, masks, scale, v_fn, v_reads):
        k = self.k
        nc, cx = k.nc, k.cx
        oi = self.nx("O")
        li = self.nx("ls")
        nkb = width // 128
        for kt in range(ntiles):
            si = self.nx("S")
            pS = self.pS[si]
            cx.op("pe", [(lambda f=f: f(pS[:, 0:width])) for f in qk_fns(kt)], reads=qk_reads, writes=[self.pS_b[si]])
            src, src_b = pS, self.pS_b[si]
            ml = masks(kt)
            if ml:
                mi = self.nx("Sm")
                for (map_, mreads) in ml:
                    o_ap, i_ap = self.Sm[mi][:, 0:width], src[:, 0:width]
                    if len(map_.shape) == 3:
                        o_ap = o_ap.rearrange("p (a b) -> p a b", b=map_.shape[2])
                        i_ap = i_ap.rearrange("p (a b) -> p a b", b=map_.shape[2])
                    cx.op("dve", lambda: nc.vector.tensor_tensor(out=o_ap, in0=i_ap, in1=map_, op=ALU.add),
                          reads=[src_b] + mreads, writes=[self.Sm_b[mi]])
                    src, src_b = self.Sm[mi], self.Sm_b[mi]
            pi = self.nx("P")
            cx.op("act", lambda: nc.scalar.activation(out=self.P[pi][:, 0:width], in_=src[:, 0:width], func=AF.Exp, scale=scale,
                                                      accum_out=self.ls[li][:, kt:kt + 1]),
                  reads=[src_b], writes=[self.P_b[pi], self.ls_b[li]])
            ti = self.nx("T")
            cx.op("pe", [(lambda kb=kb: nc.tensor.transpose(self.pT[ti][:, kb * 128:(kb + 1) * 128], self.P[pi][:, kb * 128:(kb + 1) * 128], k.ident_b[:]))
                         for kb in range(nkb)], reads=[self.P_b[pi], k.b_const], writes=[self.pT_b[ti]])
            pti = self.nx("PT")
            cx.op("dve", lambda: nc.vector.tensor_copy(out=self.PT[pti][:, 0:width], in_=self.pT[ti][:, 0:width]),
                  reads=[self.pT_b[ti]], writes=[self.PT_b[pti]])
            cx.op("pe", [(lambda kb=kb: nc.tensor.matmul(self.pO[oi], self.PT[pti][:, kb * 128:(kb + 1) * 128], v_fn(kt, kb),
                                                          start=(kt == 0 and kb == 0), stop=(kt == ntiles - 1 and kb == nkb - 1)))
                         for kb in range(nkb)], reads=[self.PT_b[pti]] + v_reads, writes=[self.pO_b[oi]])
        return oi, li

    def rsum(self, li, ntiles, col):
        k = self.k
        nc, cx = k.nc, k.cx
        cx.op("dve", lambda: nc.vector.tensor_reduce(out=self.rl[:, col:col + 1], in_=self.ls[li][:, 0:ntiles], axis=AX.X, op=ALU.add),
              reads=[self.ls_b[li]], writes=[self.rl_b])
        cx.op("dve", lambda: nc.vector.tensor_scalar(out=self.rl[:, col:col + 1], in0=self.rl[:, col:col + 1], scalar1=1e-30, scalar2=None, op0=ALU.max),
              reads=[self.rl_b], writes=[self.rl_b])
        cx.op("dve", lambda: nc.vector.reciprocal(out=self.rl[:, col:col + 1], in_=self.rl[:, col:col + 1]),
              reads=[self.rl_b], writes=[self.rl_b])


class OutT:
    def __init__(self, k, st, pfx, nstg=2):
        cx = k.cx
        self.k = k
        self.ob = [sb(k, f"{pfx}_ob{i}", [128, 128], BF16, st) for i in range(2)]
        self.ob_b = cx.bufs_n(pfx + "_ob", 2)
        self.ptbank = ps(k, f"{pfx}_opt", [128, 512], BF16, st)
        self.pt = [self.ptbank[:, i * 128:(i + 1) * 128] for i in range(2)]
        self.pt_b = cx.bufs_n(pfx + "_opt", 2)
        self.stg = [sb(k, f"{pfx}_ostg{i}", [128, 512], BF16, st) for i in range(nstg)]
        self.stg_b = cx.bufs_n(pfx + "_ostg", nstg)
        self.n = 0

    def put(self, src_ap, src_reads, qb, si):
        k = self.k
        nc, cx = k.nc, k.cx
        i = self.n % 2
        self.n += 1
        cx.op("act", lambda: nc.scalar.copy(out=self.ob[i][:], in_=src_ap), reads=src_reads, writes=[self.ob_b[i]])
        cx.op("pe", lambda: nc.tensor.transpose(self.pt[i], self.ob[i][:], k.ident_b[:]), reads=[self.ob_b[i], k.b_const], writes=[self.pt_b[i]])
        cx.op("act", lambda: nc.scalar.copy(out=self.stg[si][:, qb * 128:(qb + 1) * 128], in_=self.pt[i]),
              reads=[self.pt_b[i]], writes=[self.stg_b[si]])

    def flush(self, si, dst):
        self.k.cx.dma("sp", dst, self.stg[si][:], self.stg_b[si], reads=[self.stg_b[si]])


def emit_p2(k, scr, w_uq, w_ukv, qnT_d, kvnT_d, pos_q, mA_d, mB_d, n_all=NT_ALL, n_my=NT_MY, heads=range(16)):
    nc, cx = k.nc, k.cx
    SC = 192.0 ** -0.5
    with ExitStack() as st:
        ckvn = sb(k, "p2_ckvn", [128, 4, n_all * TT], BF16, st)
        ckvn_b = cx.buf("p2_ckvn")
        kpe = sb(k, "p2_kpe", [64, n_all * TT], BF16, st)
        kpe_b = cx.buf("p2_kpe")
        qnT = sb(k, "p2_qnT", [128, 12], F32, st)
        kvnT = sb(k, "p2_kvnT", [128, 4], F32, st)
        mA = sb(k, "p2_mA", [128, 4, 512], F32, st)
        mB = sb(k, "p2_mB", [128, 4, 512], F32, st)
        b_c = cx.buf("p2_c")
        for dst, src in ((qnT, qnT_d), (kvnT, kvnT_d), (mA, mA_d), (mB, mB_d)):
            cx.dma("sp", dst[:], src, b_c, writes=[b_c])
        cx.dma("sp", kpe[:], scr["kpeT"][:, 0:n_all * TT], kpe_b, writes=[kpe_b])
        with ExitStack() as st2:
            ld = [sb(k, f"p2_ld{i}", [128, 12, TT], F32, st2) for i in range(2)]
            ld_b = cx.bufs_n("p2_ld", 2)
            sq = [sb(k, f"p2_sq{i}", [128, TT], BF16, st2) for i in range(2)]
            sq_b = cx.bufs_n("p2_sq", 2)
            rs = sb(k, "p2_rs", [128, TT], F32, st2)
            rs_b = cx.buf("p2_rs")
            cqn = [sb(k, f"p2_cqn{i}", [128, 12, TT], BF16, st2) for i in range(2)]
            cqn_b = cx.bufs_n("p2_cqn", 2)
            pss = ps(k, "p2_pss", [128, 512], F32, st2)
            pss_b = cx.buf("p2_pss")
            wq = [sb(k, f"p2_wq{i}", [128, 12, 192], BF16, st2) for i in range(2)]
            wq_b = cx.bufs_n("p2_wq", 2)
            pq = [ps(k, f"p2_pq{i}", [128, 512], F32, st2) for i in range(2)]
            pq_b = cx.bufs_n("p2_pq", 2)
            pp = [ps(k, f"p2_pp{i}", [64, 512], F32, st2) for i in range(2)]
            pp_b = cx.bufs_n("p2_pp", 2)
            pqe = [ps(k, f"p2_pqe{i}", [64, 512], F32, st2) for i in range(2)]
            pqe_b = cx.bufs_n("p2_pqe", 2)
            stg = [sb(k, f"p2_stg{i}", [128, TT], BF16, st2) for i in range(4)]
            stg_b = cx.bufs_n("p2_stg", 4)
            qf = [sb(k, f"p2_qf{i}", [64, TT], F32, st2) for i in range(2)]
            qf_b = cx.bufs_n("p2_qf", 2)
            rt = [sb(k, f"p2_rt{i}", [64, TT], F32, st2) for i in range(2)]
            rt_b = cx.bufs_n("p2_rt", 2)
            tabs = {n: sb(k, f"p2_{n}", [64, TT], F32, st2) for n in ("cosP", "sinP", "cosM", "sinM")}
            tab_b = {n: cx.buf("p2_" + n) for n in tabs}
            tmp = [sb(k, f"p2_tmp{i}", [64, TT], F32, st2) for i in range(4)]
            tmp_b = cx.bufs_n("p2_tmp", 4)
            posi = sb(k, "p2_posi", [64, TT], I32, st2)
            posi_b = cx.buf("p2_posi")
            k.p1_ti = sb(k, "p2_ti", [64, TT], I32, st2)
            k.p1_ti_b = cx.buf("p2_ti")
            cn = {"ld": 0, "sq": 0, "stg": 0}

            def rstd(src_d, nch, tok0, width, li):
                cx.dma("sp", ld[li][:, 0:nch, :], src_d[:, :, tok0:tok0 + TT].rearrange("c p t -> p c t"), ld_b[li], writes=[ld_b[li]])
                for ci in range(nch):
                    qi = cn["sq"] % 2
                    cn["sq"] += 1
                    cx.op("act", lambda: nc.scalar.activation(out=sq[qi][:], in_=ld[li][:, ci, :], func=AF.Square),
                          reads=[ld_b[li]], writes=[sq_b[qi]])
                    cx.op("pe", lambda: nc.tensor.matmul(pss[:], k.ones_bf[:], sq[qi][:], start=(ci == 0), stop=(ci == nch - 1)),
                          reads=[sq_b[qi], k.b_const], writes=[pss_b])
                cx.op("act", lambda: nc.scalar.activation(out=rs[:], in_=pss[:], func=AF.Sqrt, scale=1.0 / width, bias=k.eps6[:]),
                      reads=[pss_b, k.b_const], writes=[rs_b])
                cx.op("dve", lambda: nc.vector.reciprocal(out=rs[:], in_=rs[:]), reads=[rs_b], writes=[rs_b])

            for t in range(n_all):
                li = cn["ld"] % 2
                cn["ld"] += 1
                rstd(scr["ckvf"], 4, t * TT, 512.0, li)
                for ci in range(4):
                    cx.op("dve", lambda: nc.vector.scalar_tensor_tensor(out=ckvn[:, ci, t * TT:(t + 1) * TT], in0=ld[li][:, ci, :], scalar=kvnT[:, ci:ci + 1],
                                                                       in1=rs[:], op0=ALU.mult, op1=ALU.mult),
                          reads=[ld_b[li], rs_b, b_c], writes=[ckvn_b])
            for m in range(n_my):
                li = cn["ld"] % 2
                cn["ld"] += 1
                rstd(scr["cqf"], 12, m * TT, 1536.0, li)
                ci_ = m % 2
                for ci in range(12):
                    cx.op("dve", lambda: nc.vector.scalar_tensor_tensor(out=cqn[ci_][:, ci, :], in0=ld[li][:, ci, :], scalar=qnT[:, ci:ci + 1],
                                                                       in1=rs[:], op0=ALU.mult, op1=ALU.mult),
                          reads=[ld_b[li], rs_b, b_c], writes=[cqn_b[ci_]])
                rope_tables(k, st2, pos_q[0:1, m * TT:(m + 1) * TT], TT, tabs, tab_b, tmp, tmp_b, posi, posi_b)
                for h in heads:
                    wi = h % 2
                    cx.dma("pool", wq[wi][:], w_uq[h], wq_b[wi], writes=[wq_b[wi]])
                    qi = h % 2
                    cx.op("pe", [(lambda j=j: nc.tensor.matmul(pq[qi][:], wq[wi][:, j, 0:128], cqn[ci_][:, j, :], start=(j == 0), stop=(j == 11)))
                                 for j in range(12)], reads=[wq_b[wi], cqn_b[ci_]], writes=[pq_b[qi]])
                    si = cn["stg"] % 4
                    cn["stg"] += 1
                    cx.op("act", lambda: nc.scalar.copy(out=stg[si][:], in_=pq[qi][:]), reads=[pq_b[qi]], writes=[stg_b[si]])
                    cx.dma("sp", scr["qmnT"][h, :, m * TT:(m + 1) * TT], stg[si][:], stg_b[si], reads=[stg_b[si]])
                    cx.op("pe", [(lambda j=j: nc.tensor.matmul(pqe[qi][:], wq[wi][:, j, 128:192], cqn[ci_][:, j, :], start=(j == 0), stop=(j == 11)))
                                 for j in range(12)], reads=[wq_b[wi], cqn_b[ci_]], writes=[pqe_b[qi]])
                    cx.op("act", lambda: nc.scalar.copy(out=qf[qi][:], in_=pqe[qi][:]), reads=[pqe_b[qi]], writes=[qf_b[qi]])
                    cx.op("pe", lambda: nc.tensor.matmul(pp[qi][:], k.prot_m[0:64, 0:64], qf[qi][:], start=True, stop=True),
                          reads=[qf_b[qi], k.b_const], writes=[pp_b[qi]])
                    cx.op("dve", lambda: nc.vector.tensor_tensor(out=rt[0][:], in0=qf[qi][:], in1=tabs["cosM"][:], op=ALU.mult),
                          reads=[qf_b[qi], tab_b["cosM"]], writes=[rt_b[0]])
                    cx.op("dve", lambda: nc.vector.tensor_tensor(out=rt[1][:], in0=pp[qi][:], in1=tabs["sinM"][:], op=ALU.mult),
                          reads=[pp_b[qi], tab_b["sinM"]], writes=[rt_b[1]])
                    si = cn["stg"] % 4
                    cn["stg"] += 1
                    cx.op("dve", lambda: nc.vector.tensor_tensor(out=stg[si][0:64, :], in0=rt[0][:], in1=rt[1][:], op=ALU.add),
                          reads=[rt_b[0], rt_b[1]], writes=[stg_b[si]])
                    cx.dma("sp", scr["qmpT"][h, :, m * TT:(m + 1) * TT], stg[si][0:64, :], stg_b[si], reads=[stg_b[si]])
            cx.barrier()
        with ExitStack() as st3:
            at = Attn(k, st3, "p2a")
            ot = OutT(k, st3, "p2o")
            wkv = [sb(k, f"p2_wkv{i}", [128, 4, 256], BF16, st3) for i in range(2)]
            wkv_b = cx.bufs_n("p2_wkv", 2)
            KT = [sb(k, f"p2_KT{i}", [128, n_all * TT], BF16, st3) for i in range(2)]
            KT_b = cx.bufs_n("p2_KT", 2)
            V = [sb(k, f"p2_V{i}", [128, n_all * 4, 128], BF16, st3) for i in range(2)]
            V_b = cx.bufs_n("p2_V", 2)
            qn = [sb(k, f"p2_qn{i}", [128, n_my * TT], BF16, st3) for i in range(2)]
            qn_b = cx.bufs_n("p2_qn", 2)
            qp = [sb(k, f"p2_qp{i}", [64, n_my * TT], BF16, st3) for i in range(2)]
            qp_b = cx.bufs_n("p2_qp", 2)
            pk = [ps(k, f"p2_pk{i}", [128, 512], F32, st3) for i in range(2)]
            pk_b = cx.bufs_n("p2_pk", 2)
            on = sb(k, "p2_on", [128, 128], F32, st3)
            on_b = cx.buf("p2_on")
            for hi_, h in enumerate(heads):
                b = hi_ % 2
                cx.dma("pool", wkv[b][:], w_ukv[h], wkv_b[b], writes=[wkv_b[b]])
                cx.dma("sp", qn[b][:], scr["qmnT"][h, :, 0:n_my * TT], qn_b[b], writes=[qn_b[b]])
                cx.dma("sp", qp[b][:], scr["qmpT"][h, :, 0:n_my * TT], qp_b[b], writes=[qp_b[b]])
                for t in range(n_all):
                    cx.op("pe", [(lambda j=j: nc.tensor.matmul(pk[0][:], wkv[b][:, j, 0:128], ckvn[:, j, t * TT:(t + 1) * TT], start=(j == 0), stop=(j == 3)))
                                 for j in range(4)], reads=[wkv_b[b], ckvn_b], writes=[pk_b[0]])
                    cx.op("act", lambda: nc.scalar.copy(out=KT[b][:, t * TT:(t + 1) * TT], in_=pk[0][:]), reads=[pk_b[0]], writes=[KT_b[b]])
                    fns = []
                    for tb in range(4):
                        for j in range(4):
                            fns.append(lambda j=j, tb=tb: nc.tensor.matmul(pk[1][:, tb * 128:(tb + 1) * 128], ckvn[:, j, t * TT + tb * 128:t * TT + (tb + 1) * 128],
                                                                            wkv[b][:, j, 128:256], start=(j == 0), stop=(j == 3)))
                    cx.op("pe", fns, reads=[wkv_b[b], ckvn_b], writes=[pk_b[1]])
                    cx.op("dve", lambda: nc.vector.tensor_copy(out=V[b][:, t * 4:(t + 1) * 4, :], in_=pk[1][:].rearrange("p (a d) -> p a d", d=128)),
                          reads=[pk_b[1]], writes=[V_b[b]])
                for m in range(n_my):
                    si = (hi_ * n_my + m) % 2
                    for qb in range(4):
                        q0 = m * TT + qb * 128
                        nt = min(2 * m + 2, n_all)

                        def mk_qk(b=b, q0=q0):
                            def qk(kt):
                                return [lambda o: nc.tensor.matmul(o, qn[b][:, q0:q0 + 128], KT[b][:, kt * TT:(kt + 1) * TT], start=True, stop=False),
                                        lambda o: nc.tensor.matmul(o, qp[b][0:64, q0:q0 + 128], kpe[0:64, kt * TT:(kt + 1) * TT], start=False, stop=True)]
                            return qk

                        def mk_masks(m=m, qb=qb):
                            def masks(kt):
                                if kt == 2 * m:
                                    return [(mA[:, qb, :], [b_c])]
                                if kt == 2 * m + 1:
                                    return [(mB[:, qb, :], [b_c])]
                                return []
                            return masks

                        def mk_fin(nt=nt, qb=qb, si=si, h=h, m=m):
                            def fin(oi, li):
                                at.rsum(li, nt, 0)
                                cx.op("dve", lambda: nc.vector.tensor_scalar(out=on[:], in0=at.pO[oi], scalar1=at.rl[:, 0:1], scalar2=None, op0=ALU.mult),
                                      reads=[at.pO_b[oi], at.rl_b], writes=[on_b])
                                ot.put(on[:], [on_b], qb, si)
                                if qb == 3:
                                    ot.flush(si, scr["oT"][16 + h, :, m * TT:(m + 1) * TT])
                            return fin

                        at.run(mk_qk(), [qn_b[b], qp_b[b], KT_b[b], kpe_b], nt, 512, mk_masks(), SC,
                               (lambda b=b: (lambda kt, kb: V[b][:, kt * 4 + kb, :]))(), [V_b[b]], mk_fin())
            at.drain()
        cx.barrier()


def emit_p3(k, scr, k1_d, k2_d, v1_d, v2_d, posk_d, posv_d):
    nc, cx = k.nc, k.cx
    with ExitStack() as st:
        w1 = sb(k, "p3_w1", [128, 32, 256], BF16, st)
        w2 = sb(k, "p3_w2", [128, 2, 128], BF16, st)
        posT = sb(k, "p3_posT", [128, 32], F32, st)
        b_w = cx.buf("p3_w")
        src = [sb(k, f"p3_src{i}", [128, S], BF16, st) for i in range(2)]
        src_b = cx.bufs_n("p3_src", 2)
        kp = sb(k, "p3_kp", [128, 32, 256], BF16, st)
        kp_b = cx.buf("p3_kp")
        ppre = [ps(k, f"p3_ppre{i}", [128, 256], F32, st) for i in range(2)]
        ppre_b = cx.bufs_n("p3_ppre", 2)
        pout = ps(k, "p3_pout", [128, 256], F32, st)
        pout_b = cx.buf("p3_pout")
        xs = sb(k, "p3_xs", [128, 256], F32, st)
        xs_b = cx.buf("p3_xs")
        t1 = sb(k, "p3_t1", [128, 256], F32, st)
        t1_b = cx.buf("p3_t1")
        gl = sb(k, "p3_gl", [128, 2, 256], BF16, st)
        gl_b = cx.buf("p3_gl")
        cx.op("dve", lambda: nc.vector.memset(kp[:], 0.0), writes=[kp_b])
        cx.op("dve", lambda: nc.vector.memset(k.kcT[:], 0.0), writes=[k.kcv_b])
        cx.op("dve", lambda: nc.vector.memset(k.vc[:], 0.0), writes=[k.kcv_b])
        n_src = 0
        for which in range(2):
            w1d, w2d, posd, srcd = ((k1_d, k2_d, posk_d, scr["kcmpT"]), (v1_d, v2_d, posv_d, scr["vcmpT"]))[which]
            cx.dma("pool", w1[:], w1d, b_w, writes=[b_w])
            cx.dma("pool", w2[:], w2d, b_w, writes=[b_w])
            cx.dma("sp", posT[:], posd, b_w, writes=[b_w])
            for g in range(4):
                si = n_src % 2
                n_src += 1
                cx.dma("sp", src[si][:], srcd[g, :, :], src_b[si], writes=[src_b[si]])
                for l in range(32):
                    cx.op("dve", lambda: nc.vector.tensor_scalar(out=kp[:, l, 0:255], in0=src[si][:, l:l + 16 * 254 + 1:16], scalar1=posT[:, l:l + 1],
                                                                 scalar2=None, op0=ALU.add), reads=[src_b[si], b_w], writes=[kp_b])
                for hc in range(2):
                    pi = hc
                    cx.op("pe", [(lambda l=l: nc.tensor.matmul(ppre[pi][:], w1[:, l, hc * 128:(hc + 1) * 128], kp[:, l, :], start=(l == 0), stop=(l == 31)))
                                 for l in range(32)], reads=[b_w, kp_b], writes=[ppre_b[pi]])
                    cx.op("act", lambda: nc.scalar.copy(out=xs[:], in_=ppre[pi][:]), reads=[ppre_b[pi]], writes=[xs_b])
                    cx.op("dve", lambda: nc.vector.tensor_tensor(out=t1[:], in0=xs[:], in1=xs[:], op=ALU.mult), reads=[xs_b], writes=[t1_b])
                    cx.op("dve", lambda: nc.vector.tensor_scalar(out=t1[:], in0=t1[:], scalar1=0.044715, scalar2=1.0, op0=ALU.mult, op1=ALU.add),
                          reads=[t1_b], writes=[t1_b])
                    cx.op("dve", lambda: nc.vector.tensor_tensor(out=t1[:], in0=t1[:], in1=xs[:], op=ALU.mult), reads=[t1_b, xs_b], writes=[t1_b])
                    cx.op("act", lambda: nc.scalar.activation(out=t1[:], in_=t1[:], func=AF.Tanh, scale=0.7978845608028654), reads=[t1_b], writes=[t1_b])
                    cx.op("dve", lambda: nc.vector.tensor_scalar(out=t1[:], in0=t1[:], scalar1=0.5, scalar2=0.5, op0=ALU.mult, op1=ALU.add),
                          reads=[t1_b], writes=[t1_b])
                    cx.op("dve", lambda: nc.vector.tensor_tensor(out=gl[:, hc, :], in0=t1[:], in1=xs[:], op=ALU.mult), reads=[t1_b, xs_b], writes=[gl_b])
                if which == 0:
                    cx.op("pe", [(lambda hc=hc: nc.tensor.matmul(pout[:], w2[:, hc, :], gl[:, hc, :], start=(hc == 0), stop=(hc == 1))) for hc in range(2)],
                          reads=[b_w, gl_b], writes=[pout_b])
                    cx.op("act", lambda: nc.scalar.copy(out=k.kcT[:, g, 0:256], in_=pout[:]), reads=[pout_b], writes=[k.kcv_b])
                else:
                    for c in range(2):
                        cx.op("pe", [(lambda hc=hc: nc.tensor.matmul(pout[:, 0:128], gl[:, hc, c * 128:(c + 1) * 128], w2[:, hc, :], start=(hc == 0), stop=(hc == 1)))
                                     for hc in range(2)], reads=[b_w, gl_b], writes=[pout_b])
                        cx.op("act", lambda: nc.scalar.copy(out=k.vc[:, g, c, :], in_=pout[:, 0:128]), reads=[pout_b], writes=[k.kcv_b])
        cx.barrier()


def emit_p4(k, scr, cst, n_all=NT_ALL, n_my=NT_MY, groups=range(4), mode="cws"):
    nc, cx = k.nc, k.cx
    SC = 128.0 ** -0.5
    with ExitStack() as st:
        at = Attn(k, st, "p4a" + mode, extra_bank=True)
        ot = OutT(k, st, "p4o" + mode, 4)
        mA = sb(k, f"p4{mode}_mA", [128, 4, 512], F32, st)
        mB = sb(k, f"p4{mode}_mB", [128, 4, 512], F32, st)
        mW = sb(k, f"p4{mode}_mW", [128, 3, 4, 512], F32, st)
        mC = sb(k, f"p4{mode}_mC", [128, 4, 512], F32, st)
        vS = sb(k, f"p4{mode}_vS", [128, 4, 64], F32, st)
        aS = sb(k, f"p4{mode}_aS", [128, 4, 64], F32, st)
        ovl = sb(k, f"p4{mode}_ovl", [128, 4, 64], BF16, st)
        ovf = sb(k, f"p4{mode}_ovf", [128, 4, 64], F32, st)
        b_c = cx.buf(f"p4{mode}_c")
        b_cm = cx.buf(f"p4{mode}_cm")
        cx.dma("sp", mA[:], cst["mA"], b_c, writes=[b_c])
        cx.dma("sp", mB[:], cst["mB"], b_c, writes=[b_c])
        cx.dma("sp", mW[:], cst["mW"].rearrange("w p q k -> p w q k"), b_c, writes=[b_c])
        cx.dma("sp", ovf[:], cst["ovl"], b_c, writes=[b_c])
        cx.op("dve", lambda: nc.vector.tensor_copy(out=ovl[:], in_=ovf[:]), reads=[b_c], writes=[b_c])
        ksel = [sb(k, f"p4{mode}_ksel{i}", [128, n_all * TT], BF16, st) for i in range(1)]
        kwin = [sb(k, f"p4{mode}_kwin{i}", [128, n_all * TT], BF16, st) for i in range(1)]
        vsel = [sb(k, f"p4{mode}_vsel{i}", [128, n_all * 4, 128], BF16, st) for i in range(1)]
        vwin = [sb(k, f"p4{mode}_vwin{i}", [128, n_all * 4, 128], BF16, st) for i in range(1)]
        kv_b = cx.buf(f"p4{mode}_kv")
        qg = [sb(k, f"p4{mode}_q{i}", [128, 4, TT], BF16, st) for i in range(2)]
        qg_b = cx.bufs_n(f"p4{mode}_q", 2)
        gt = [sb(k, f"p4{mode}_gt{i}", [128, 4, 48], F32, st) for i in range(2)]
        gt_b = cx.bufs_n(f"p4{mode}_gt", 2)
        oacc_t = [sb(k, f"p4{mode}_oacc{i}", [128, 4, 128], F32, st) for i in range(2)]
        oacc_bt = cx.bufs_n(f"p4{mode}_oacc", 2)
        otmp = sb(k, f"p4{mode}_otmp", [128, 2, 128], F32, st)
        otmp_b = cx.bufs_n(f"p4{mode}_otmp", 2)
        imp = sb(k, f"p4{mode}_imp", [128, 64], F32, st)
        imp_b = cx.buf(f"p4{mode}_imp")
        wk = sb(k, f"p4{mode}_wk", [128, 64], F32, st)
        wk_b = cx.buf(f"p4{mode}_wk")
        m16 = sb(k, f"p4{mode}_m16", [128, 16], F32, st)
        m16_b = cx.buf(f"p4{mode}_m16")
        sneg_t = [sb(k, f"p4{mode}_sneg{i}", [128, 64], F32, st) for i in range(2)]
        sneg_bt = cx.bufs_n(f"p4{mode}_sneg", 2)
        nqb = 0
        gm = sb(k, f"p4{mode}_gm", [128, 4], F32, st)
        gm_b = cx.buf(f"p4{mode}_gm")
        vsel_v = scr["vsel"].rearrange("(b p) c -> p b c", p=128)
        vwin_v = scr["vwin"].rearrange("(b p) c -> p b c", p=128)
        it = 0
        for g in groups:
            nb = n_all * 4
            at.drain()
            cx.dma("sp", ksel[0][:], scr["kselT"][g, :, 0:n_all * TT], kv_b, writes=[kv_b])
            cx.dma("sp", kwin[0][:], scr["kwinT"][g, :, 0:n_all * TT], kv_b, writes=[kv_b])
            cx.dma("sp", vsel[0][:], vsel_v[:, 0:nb, g * 128:(g + 1) * 128], kv_b, writes=[kv_b])
            cx.dma("sp", vwin[0][:], vwin_v[:, 0:nb, g * 128:(g + 1) * 128], kv_b, writes=[kv_b])
            for m in range(n_my):
                at.drain()
                qi = it % 2
                it += 1
                cx.dma("sp", qg[qi][:], scr["qT"][g * 4:(g + 1) * 4, :, m * TT:(m + 1) * TT].rearrange("h p t -> p h t"), qg_b[qi], writes=[qg_b[qi]])
                cx.dma("sp", gt[qi][:], scr["gtok"][m * TT:(m + 1) * TT, :].rearrange("(b p) c -> p b c", p=128), gt_b[qi], writes=[gt_b[qi]])
                cx.dma("sp", mC[:], cst["mC"][m], b_cm, writes=[b_cm])
                cx.dma("sp", vS[:], cst["vS"][m], b_cm, writes=[b_cm])
                cx.dma("sp", aS[:], cst["aS"][m], b_cm, writes=[b_cm])
                for qb in range(4):
                    Q = slice(qb * 128, (qb + 1) * 128)
                    oacc, oacc_b = oacc_t[nqb % 2], oacc_bt[nqb % 2]
                    sneg, sneg_b = sneg_t[nqb % 2], sneg_bt[nqb % 2]
                    nqb += 1
                    r0_ = m * TT + qb * 128
                    if "c" not in mode:
                        cx.dma("sp", oacc[:], scr["ocmp"][g, r0_:r0_ + 128, :, :], oacc_b, writes=[oacc_b])
                        cx.dma("sp", sneg[:], scr["snegs"][g, r0_:r0_ + 128, :], sneg_b, writes=[sneg_b])
                    nt = min(2 * m + 2, n_all)
                    wt = [(w, 2 * m - 1 + w) for w in range(3) if 0 <= 2 * m - 1 + w < n_all]

                    def topk(qb=qb, sneg=sneg, sneg_b=sneg_b, oacc=oacc, oacc_b=oacc_b, g=g, r0_=r0_):
                        cx.op("dve", lambda: nc.vector.tensor_tensor(out=imp[:], in0=imp[:], in1=vS[:, qb, :], op=ALU.mult), reads=[imp_b, b_cm], writes=[imp_b])
                        cx.op("dve", lambda: nc.vector.tensor_tensor(out=imp[:], in0=imp[:], in1=aS[:, qb, :], op=ALU.add), reads=[imp_b, b_cm], writes=[imp_b])
                        cx.op("dve", lambda: nc.vector.max(out=m16[:, 0:8], in_=imp[:]), reads=[imp_b], writes=[m16_b])
                        cx.op("dve", lambda: nc.vector.match_replace(out=wk[:], in_to_replace=m16[:, 0:8], in_values=imp[:], imm_value=-3e38),
                              reads=[imp_b, m16_b], writes=[wk_b])
                        cx.op("dve", lambda: nc.vector.max(out=m16[:, 8:16], in_=wk[:]), reads=[wk_b], writes=[m16_b])
                        cx.op("dve", lambda: nc.vector.tensor_scalar(out=sneg[:], in0=imp[:], scalar1=m16[:, 15:16], scalar2=-30000.0, op0=ALU.is_lt, op1=ALU.mult),
                              reads=[imp_b, m16_b], writes=[sneg_b])
                        if "w" not in mode and os.environ.get("NOSTORE","0") != "1":
                            cx.dma("sp", scr["snegs"][g, r0_:r0_ + 128, :], sneg[:], sneg_b, reads=[sneg_b])
                            cx.dma("sp", scr["ocmp"][g, r0_:r0_ + 128, :, :], oacc[:], oacc_b, reads=[oacc_b])

                    BR = mode
                    for r in (range(4) if "c" in BR else []):
                        h = g * 4 + r

                        def mk_fin_c(r=r, h=h, qb=qb, qi=qi, topk=topk, oacc=oacc, oacc_b=oacc_b):
                            def fin(oi, li):
                                LV = int(os.environ.get("P4_FINLVL", "9"))
                                if LV < 1:
                                    return
                                at.rsum(li, 1, 0)
                                if LV < 2:
                                    return
                                if r == 0:
                                    cx.op("dve", lambda: nc.vector.tensor_scalar(out=imp[:], in0=at.pX[oi], scalar1=at.rl[:, 0:1], scalar2=None, op0=ALU.mult),
                                          reads=[at.pX_b[oi], at.rl_b], writes=[imp_b])
                                else:
                                    cx.op("dve", lambda: nc.vector.scalar_tensor_tensor(out=imp[:], in0=at.pX[oi], scalar=at.rl[:, 0:1], in1=imp[:], op0=ALU.mult, op1=ALU.add),
                                          reads=[at.pX_b[oi], at.rl_b, imp_b], writes=[imp_b])
                                if LV < 3:
                                    return
                                cx.op("dve", lambda: nc.vector.tensor_tensor(out=gm[:, 0:1], in0=at.rl[:, 0:1], in1=gt[qi][:, qb, 3 * h:3 * h + 1], op=ALU.mult),
                                      reads=[at.rl_b, gt_b[qi]], writes=[gm_b])
                                if LV < 4:
                                    return
                                cx.op("act", lambda: nc.scalar.activation(out=oacc[:, r, :], in_=at.pO[oi], func=AF.Copy, scale=gm[:, 0:1]),
                                      reads=[at.pO_b[oi], gm_b], writes=[oacc_b])
                                if r == 3 and os.environ.get("P4_NOTOPK", "0") != "1":
                                    topk()
                            return fin

                        at.run((lambda r=r, qi=qi, Q=Q, g=g: (lambda kt: [lambda o: nc.tensor.matmul(o, qg[qi][:, r, Q], k.kcT[:, g, 0:256], start=True, stop=True)]))(),
                               [qg_b[qi], k.kcv_b], 1, 256, (lambda qb=qb: (lambda kt: [(mC[:, qb, 0:256], [b_cm])]))(), SC,
                               (lambda g=g: (lambda kt, kb: k.vc[:, g, kb, :]))(), [k.kcv_b], mk_fin_c(),
                               x_fn=(lambda kb: ovl[:, kb, :]), x_reads=[b_c])
                    for r in (range(4) if "w" in BR else []):
                        h = g * 4 + r

                        def mk_fin_w(r=r, h=h, qb=qb, qi=qi, nw=len(wt), oacc=oacc, oacc_b=oacc_b):
                            def fin(oi, li):
                                if os.environ.get("P4_NOFINW", "0") == "1":
                                    return
                                at.rsum(li, nw, 2)
                                cx.op("dve", lambda: nc.vector.tensor_tensor(out=gm[:, 2:3], in0=at.rl[:, 2:3], in1=gt[qi][:, qb, 3 * h + 2:3 * h + 3], op=ALU.mult),
                                      reads=[at.rl_b, gt_b[qi]], writes=[gm_b])
                                ti_ = oi
                                cx.op("act", lambda: nc.scalar.activation(out=otmp[:, ti_, :], in_=at.pO[oi], func=AF.Copy, scale=gm[:, 2:3]),
                                      reads=[at.pO_b[oi], gm_b], writes=[otmp_b[ti_]])
                                cx.op("dve", lambda: nc.vector.tensor_tensor(out=oacc[:, r, :], in0=oacc[:, r, :], in1=otmp[:, ti_, :], op=ALU.add),
                                      reads=[otmp_b[ti_], oacc_b], writes=[oacc_b])
                            return fin

                        at.run((lambda r=r, qi=qi, Q=Q, wt=wt: (lambda i: [lambda o: nc.tensor.matmul(o, qg[qi][:, r, Q], kwin[0][:, wt[i][1] * TT:(wt[i][1] + 1) * TT], start=True, stop=True)]))(),
                               [qg_b[qi], kv_b], len(wt), 512, (lambda wt=wt, qb=qb: (lambda i: [(mW[:, wt[i][0], qb, :], [b_c])]))(), SC,
                               (lambda wt=wt: (lambda i, kb: vwin[0][:, wt[i][1] * 4 + kb, :]))(), [kv_b], mk_fin_w())
                    for r in (range(4) if "s" in BR else []):
                        h = g * 4 + r

                        def mk_masks_s(m=m, qb=qb, sneg=sneg, sneg_b=sneg_b):
                            def masks(kt):
                                ml = [(sneg[:, kt * 8:(kt + 1) * 8].unsqueeze(2).to_broadcast([128, 8, 64]), [sneg_b])]
                                if kt == 2 * m:
                                    ml.append((mA[:, qb, :], [b_c]))
                                if kt == 2 * m + 1:
                                    ml.append((mB[:, qb, :], [b_c]))
                                return ml
                            return masks

                        def mk_fin_s(r=r, h=h, qb=qb, qi=qi, nt=nt, g=g, m=m, oacc=oacc, oacc_b=oacc_b):
                            def fin(oi, li):
                                at.rsum(li, nt, 1)
                                cx.op("dve", lambda: nc.vector.tensor_tensor(out=gm[:, 1:2], in0=at.rl[:, 1:2], in1=gt[qi][:, qb, 3 * h + 1:3 * h + 2], op=ALU.mult),
                                      reads=[at.rl_b, gt_b[qi]], writes=[gm_b])
                                ti_ = oi
                                cx.op("act", lambda: nc.scalar.activation(out=otmp[:, ti_, :], in_=at.pO[oi], func=AF.Copy, scale=gm[:, 1:2]),
                                      reads=[at.pO_b[oi], gm_b], writes=[otmp_b[ti_]])
                                cx.op("dve", lambda: nc.vector.tensor_tensor(out=oacc[:, r, :], in0=oacc[:, r, :], in1=otmp[:, ti_, :], op=ALU.add),
                                      reads=[otmp_b[ti_], oacc_b], writes=[oacc_b])
                                if r == 3:
                                    for r2 in range(4):
                                        ot.put(oacc[:, r2, :], [oacc_b], qb, r2)
                                    if qb == 3:
                                        for r2 in range(4):
                                            ot.flush(r2, scr["oT"][g * 4 + r2, :, m * TT:(m + 1) * TT])
                            return fin

                        at.run((lambda r=r, qi=qi, Q=Q: (lambda kt: [lambda o: nc.tensor.matmul(o, qg[qi][:, r, Q], ksel[0][:, kt * TT:(kt + 1) * TT], start=True, stop=True)]))(),
                               [qg_b[qi], kv_b], nt, 512, mk_masks_s(), SC, (lambda kt, kb: vsel[0][:, kt * 4 + kb, :]), [kv_b], mk_fin_s(), copy_eng="act")
            at.drain()
        cx.barrier()


def emit_p4_sync(k, scr, cst, n_all=NT_ALL, n_my=NT_MY, groups=range(4), skip_cmp=False):
    nc, cx = k.nc, k.cx
    SC = 128.0 ** -0.5
    with ExitStack() as st:
        at = AttnSync(k, st, "p4ya")
        ot = OutT(k, st, "p4yo", 4)
        mA = sb(k, "p4y_mA", [128, 4, 512], F32, st)
        mB = sb(k, "p4y_mB", [128, 4, 512], F32, st)
        mW = sb(k, "p4y_mW", [128, 3, 4, 512], F32, st)
        mC = sb(k, "p4y_mC", [128, 4, 512], F32, st)
        vS = sb(k, "p4y_vS", [128, 4, 64], F32, st)
        aS = sb(k, "p4y_aS", [128, 4, 64], F32, st)
        ovl = sb(k, "p4y_ovl", [128, 4, 64], BF16, st)
        ovf = sb(k, "p4y_ovf", [128, 4, 64], F32, st)
        b_c = cx.buf("p4y_c")
        b_cm = cx.buf("p4y_cm")
        cx.dma("sp", mA[:], cst["mA"], b_c, writes=[b_c])
        cx.dma("sp", mB[:], cst["mB"], b_c, writes=[b_c])
        cx.dma("sp", mW[:], cst["mW"].rearrange("w p q k -> p w q k"), b_c, writes=[b_c])
        cx.dma("sp", ovf[:], cst["ovl"], b_c, writes=[b_c])
        cx.op("dve", lambda: nc.vector.tensor_copy(out=ovl[:], in_=ovf[:]), reads=[b_c], writes=[b_c])
        ksel = [sb(k, f"p4y_ksel{i}", [128, n_all * TT], BF16, st) for i in range(1)]
        kwin = [sb(k, f"p4y_kwin{i}", [128, n_all * TT], BF16, st) for i in range(1)]
        vsel = [sb(k, f"p4y_vsel{i}", [128, n_all * 4, 128], BF16, st) for i in range(1)]
        vwin = [sb(k, f"p4y_vwin{i}", [128, n_all * 4, 128], BF16, st) for i in range(1)]
        kv_b = cx.buf("p4y_kv")
        qg = [sb(k, f"p4y_q{i}", [128, 4, TT], BF16, st) for i in range(2)]
        qg_b = cx.bufs_n("p4y_q", 2)
        gt = [sb(k, f"p4y_gt{i}", [128, 4, 48], F32, st) for i in range(2)]
        gt_b = cx.bufs_n("p4y_gt", 2)
        oacc = sb(k, "p4y_oacc", [128, 4, 128], F32, st)
        oacc_b = cx.buf("p4y_oacc")
        pn = sb(k, "p4y_pn", [128, 256], BF16, st)
        pn_b = cx.buf("p4y_pn")
        pimp = ps(k, "p4y_pimp", [128, 64], F32, st)
        pimp_b = cx.buf("p4y_pimp")
        imp = sb(k, "p4y_imp", [128, 64], F32, st)
        imp_b = cx.buf("p4y_imp")
        wk = sb(k, "p4y_wk", [128, 64], F32, st)
        wk_b = cx.buf("p4y_wk")
        m16 = sb(k, "p4y_m16", [128, 16], F32, st)
        m16_b = cx.buf("p4y_m16")
        sneg = sb(k, "p4y_sneg", [128, 64], F32, st)
        sneg_b = cx.buf("p4y_sneg")
        gm = sb(k, "p4y_gm", [128, 4], F32, st)
        gm_b = cx.buf("p4y_gm")
        vsel_v = scr["vsel"].rearrange("(b p) c -> p b c", p=128)
        vwin_v = scr["vwin"].rearrange("(b p) c -> p b c", p=128)
        it = 0
        for g in groups:
            nb = n_all * 4
            cx.dma("sp", ksel[0][:], scr["kselT"][g, :, 0:n_all * TT], kv_b, writes=[kv_b])
            cx.dma("sp", kwin[0][:], scr["kwinT"][g, :, 0:n_all * TT], kv_b, writes=[kv_b])
            cx.dma("sp", vsel[0][:], vsel_v[:, 0:nb, g * 128:(g + 1) * 128], kv_b, writes=[kv_b])
            cx.dma("sp", vwin[0][:], vwin_v[:, 0:nb, g * 128:(g + 1) * 128], kv_b, writes=[kv_b])
            for m in range(n_my):
                qi = it % 2
                it += 1
                cx.dma("sp", qg[qi][:], scr["qT"][g * 4:(g + 1) * 4, :, m * TT:(m + 1) * TT].rearrange("h p t -> p h t"), qg_b[qi], writes=[qg_b[qi]])
                cx.dma("sp", gt[qi][:], scr["gtok"][m * TT:(m + 1) * TT, :].rearrange("(b p) c -> p b c", p=128), gt_b[qi], writes=[gt_b[qi]])
                cx.dma("sp", mC[:], cst["mC"][m], b_cm, writes=[b_cm])
                cx.dma("sp", vS[:], cst["vS"][m], b_cm, writes=[b_cm])
                cx.dma("sp", aS[:], cst["aS"][m], b_cm, writes=[b_cm])
                for qb in range(4):
                    Q = slice(qb * 128, (qb + 1) * 128)
                    if skip_cmp:
                        r0_ = m * TT + qb * 128
                        cx.dma("sp", oacc[:], scr["ocmp"][g, r0_:r0_ + 128, :, :], oacc_b, writes=[oacc_b])
                        cx.dma("sp", sneg[:], scr["snegs"][g, r0_:r0_ + 128, :], sneg_b, writes=[sneg_b])
                    else:
                        for r in range(4):
                            h = g * 4 + r
                            oi, li = at.run(lambda kt: [lambda o: nc.tensor.matmul(o, qg[qi][:, r, Q], k.kcT[:, g, 0:256], start=True, stop=True)],
                                            [qg_b[qi], k.kcv_b], 1, 256, lambda kt: [(mC[:, qb, 0:256], [b_cm])], SC,
                                            lambda kt, kb: k.vc[:, g, kb, :], [k.kcv_b])
                            at.rsum(li, 1, 0)
                            pidx = (at.c["P"] - 1) % 2
                            cx.op("dve", lambda: nc.vector.tensor_scalar(out=pn[:], in0=at.P[pidx][:, 0:256], scalar1=at.rl[:, 0:1], scalar2=None, op0=ALU.mult),
                                  reads=[at.P_b[pidx], at.rl_b], writes=[pn_b])
                            ti = at.nx("T")
                            cx.op("pe", [(lambda c=c: nc.tensor.transpose(at.pT[ti][:, c * 128:(c + 1) * 128], pn[:, c * 128:(c + 1) * 128], k.ident_b[:])) for c in range(2)],
                                  reads=[pn_b, k.b_const], writes=[at.pT_b[ti]])
                            pti = at.nx("PT")
                            cx.op("dve", lambda: nc.vector.tensor_copy(out=at.PT[pti][:, 0:256], in_=at.pT[ti][:, 0:256]), reads=[at.pT_b[ti]], writes=[at.PT_b[pti]])
                            cx.op("pe", [(lambda c=c: nc.tensor.matmul(pimp[:], at.PT[pti][:, c * 128:(c + 1) * 128], ovl[:, c, :], start=(r == 0 and c == 0), stop=(r == 3 and c == 1)))
                                         for c in range(2)], reads=[at.PT_b[pti], b_c], writes=[pimp_b])
                            cx.op("dve", lambda: nc.vector.tensor_tensor(out=gm[:, 0:1], in0=at.rl[:, 0:1], in1=gt[qi][:, qb, 3 * h:3 * h + 1], op=ALU.mult),
                                  reads=[at.rl_b, gt_b[qi]], writes=[gm_b])
                            cx.op("dve", lambda: nc.vector.tensor_scalar(out=oacc[:, r, :], in0=at.pO[oi], scalar1=gm[:, 0:1], scalar2=None, op0=ALU.mult),
                                  reads=[at.pO_b[oi], gm_b], writes=[oacc_b])
                        cx.op("dve", lambda: nc.vector.tensor_tensor(out=imp[:], in0=pimp[:], in1=vS[:, qb, :], op=ALU.mult), reads=[pimp_b, b_cm], writes=[imp_b])
                        cx.op("dve", lambda: nc.vector.tensor_tensor(out=imp[:], in0=imp[:], in1=aS[:, qb, :], op=ALU.add), reads=[imp_b, b_cm], writes=[imp_b])
                        cx.op("dve", lambda: nc.vector.max(out=m16[:, 0:8], in_=imp[:]), reads=[imp_b], writes=[m16_b])
                        cx.op("dve", lambda: nc.vector.match_replace(out=wk[:], in_to_replace=m16[:, 0:8], in_values=imp[:], imm_value=-3e38),
                              reads=[imp_b, m16_b], writes=[wk_b])
                        cx.op("dve", lambda: nc.vector.max(out=m16[:, 8:16], in_=wk[:]), reads=[wk_b], writes=[m16_b])
                        cx.op("dve", lambda: nc.vector.tensor_scalar(out=sneg[:], in0=imp[:], scalar1=m16[:, 15:16], scalar2=30000.0, op0=ALU.is_lt, op1=ALU.mult),
                              reads=[imp_b, m16_b], writes=[sneg_b])
                        cx.op("dve", lambda: nc.vector.tensor_scalar(out=sneg[:], in0=sneg[:], scalar1=-1.0, scalar2=None, op0=ALU.mult), reads=[sneg_b], writes=[sneg_b])
                    nt = min(2 * m + 2, n_all)
                    for r in range(4):
                        h = g * 4 + r

                        def masks(kt):
                            ml = [(sneg[:, kt * 8:(kt + 1) * 8].unsqueeze(2).to_broadcast([128, 8, 64]), [sneg_b])]
                            if kt == 2 * m:
                                ml.append((mA[:, qb, :], [b_c]))
                            if kt == 2 * m + 1:
                                ml.append((mB[:, qb, :], [b_c]))
                            return ml
                        oi, li = at.run(lambda kt: [lambda o: nc.tensor.matmul(o, qg[qi][:, r, Q], ksel[0][:, kt * TT:(kt + 1) * TT], start=True, stop=True)],
                                        [qg_b[qi], kv_b], nt, 512, masks, SC, lambda kt, kb: vsel[0][:, kt * 4 + kb, :], [kv_b])
                        at.rsum(li, nt, 1)
                        cx.op("dve", lambda: nc.vector.tensor_tensor(out=gm[:, 1:2], in0=at.rl[:, 1:2], in1=gt[qi][:, qb, 3 * h + 1:3 * h + 2], op=ALU.mult),
                              reads=[at.rl_b, gt_b[qi]], writes=[gm_b])
                        cx.op("dve", lambda: nc.vector.scalar_tensor_tensor(out=oacc[:, r, :], in0=at.pO[oi], scalar=gm[:, 1:2], in1=oacc[:, r, :], op0=ALU.mult, op1=ALU.add),
                              reads=[at.pO_b[oi], gm_b, oacc_b], writes=[oacc_b])
                    wt = [(w, 2 * m - 1 + w) for w in range(3) if 0 <= 2 * m - 1 + w < n_all]
                    for r in range(4):
                        h = g * 4 + r
                        oi, li = at.run(lambda i: [lambda o: nc.tensor.matmul(o, qg[qi][:, r, Q], kwin[0][:, wt[i][1] * TT:(wt[i][1] + 1) * TT], start=True, stop=True)],
                                        [qg_b[qi], kv_b], len(wt), 512, lambda i: [(mW[:, wt[i][0], qb, :], [b_c])], SC,
                                        lambda i, kb: vwin[0][:, wt[i][1] * 4 + kb, :], [kv_b])
                        at.rsum(li, len(wt), 2)
                        cx.op("dve", lambda: nc.vector.tensor_tensor(out=gm[:, 2:3], in0=at.rl[:, 2:3], in1=gt[qi][:, qb, 3 * h + 2:3 * h + 3], op=ALU.mult),
                              reads=[at.rl_b, gt_b[qi]], writes=[gm_b])
                        cx.op("dve", lambda: nc.vector.scalar_tensor_tensor(out=oacc[:, r, :], in0=at.pO[oi], scalar=gm[:, 2:3], in1=oacc[:, r, :], op0=ALU.mult, op1=ALU.add),
                              reads=[at.pO_b[oi], gm_b, oacc_b], writes=[oacc_b])
                    for r in range(4):
                        ot.put(oacc[:, r, :], [oacc_b], qb, r)
                for r in range(4):
                    ot.flush(r, scr["oT"][g * 4 + r, :, m * TT:(m + 1) * TT])
        cx.barrier()


class LNorm:
    def __init__(self, k, st, pfx, g_d, b_d):
        cx = k.cx
        self.k = k
        self.g_d, self.b_d = g_d, b_d
        self.stt = sb(k, pfx + "_st", [128, 8], F32, st)
        self.stt_b = cx.buf(pfx + "_st")
        self.rows = [sb(k, f"{pfx}_row{i}", [128, 2, 512], F32, st) for i in range(2)]
        self.rows_b = cx.bufs_n(pfx + "_row", 2)
        self.n = 0

    def run(self, y, y_b, junk, junk_b):
        k = self.k
        nc, cx = k.nc, k.cx
        s = self.stt
        sb_ = self.stt_b
        cx.op("act", lambda: nc.scalar.activation(out=junk, in_=y, func=AF.Identity, accum_out=s[:, 0:1]), reads=[y_b], writes=[junk_b, sb_])
        cx.op("act", lambda: nc.scalar.activation(out=junk, in_=y, func=AF.Square, accum_out=s[:, 1:2]), reads=[y_b], writes=[junk_b, sb_])
        cx.op("dve", lambda: nc.vector.tensor_scalar(out=s[:, 2:3], in0=s[:, 0:1], scalar1=1.0 / D, scalar2=None, op0=ALU.mult), reads=[sb_], writes=[sb_])
        cx.op("dve", lambda: nc.vector.tensor_tensor(out=s[:, 3:4], in0=s[:, 2:3], in1=s[:, 2:3], op=ALU.mult), reads=[sb_], writes=[sb_])
        cx.op("dve", lambda: nc.vector.scalar_tensor_tensor(out=s[:, 4:5], in0=s[:, 1:2], scalar=1.0 / D, in1=s[:, 3:4], op0=ALU.mult, op1=ALU.subtract),
              reads=[sb_], writes=[sb_])
        cx.op("act", lambda: nc.scalar.activation(out=s[:, 5:6], in_=s[:, 4:5], func=AF.Sqrt, bias=k.eps5[:], scale=1.0), reads=[sb_, k.b_const], writes=[sb_])
        cx.op("dve", lambda: nc.vector.reciprocal(out=s[:, 5:6], in_=s[:, 5:6]), reads=[sb_], writes=[sb_])
        cx.op("dve", lambda: nc.vector.tensor_scalar(out=y, in0=y, scalar1=s[:, 2:3], scalar2=s[:, 5:6], op0=ALU.subtract, op1=ALU.mult),
              reads=[y_b, sb_], writes=[y_b])
        for fc in range(8):
            ri = self.n % 2
            self.n += 1
            C = slice(fc * 512, (fc + 1) * 512)
            cx.dma("sp", self.rows[ri][:, 0, :], self.g_d[0:1, C].to_broadcast([128, 512]), self.rows_b[ri], writes=[self.rows_b[ri]])
            cx.dma("sp", self.rows[ri][:, 1, :], self.b_d[0:1, C].to_broadcast([128, 512]), self.rows_b[ri], writes=[self.rows_b[ri]])
            cx.op("dve", lambda: nc.vector.tensor_tensor(out=y[:, C], in0=y[:, C], in1=self.rows[ri][:, 0, :], op=ALU.mult), reads=[y_b, self.rows_b[ri]], writes=[y_b])
            cx.op("dve", lambda: nc.vector.tensor_tensor(out=y[:, C], in0=y[:, C], in1=self.rows[ri][:, 1, :], op=ALU.add), reads=[y_b, self.rows_b[ri]], writes=[y_b])


def emit_p5(k, scr, xq, w_out, modrow, ln_g, ln_b, alpha, n_my=NT_MY):
    nc, cx = k.nc, k.cx
    with ExitStack() as st:
        oT = sb(k, "p5_oT", [128, 32, TT], BF16, st)
        oT_b = cx.buf("p5_oT")
        yb = sb(k, "p5_y", [128, 4, D], F32, st)
        y_b = cx.bufs_n("p5_y", 4)
        slabs = [sb(k, f"p5_slab{i}", [128, 32, 512], BF16, st) for i in range(2)]
        slab_b = cx.bufs_n("p5_slab", 2)
        grow = [sb(k, f"p5_grow{i}", [128, 512], F32, st) for i in range(2)]
        grow_b = cx.bufs_n("p5_grow", 2)
        tmp = [sb(k, f"p5_tmp{i}", [128, 512], F32, st) for i in range(2)]
        tmp_b = cx.bufs_n("p5_tmp", 2)
        junk = sb(k, "p5_junk", [128, D], BF16, st)
        junk_b = cx.buf("p5_junk")
        hst = [sb(k, f"p5_hst{i}", [128, 32, 128], BF16, st) for i in range(2)]
        hst_b = cx.bufs_n("p5_hst", 2)
        pa = [ps(k, f"p5_pa{i}", [128, 512], F32, st) for i in range(3)]
        pa_b = cx.bufs_n("p5_pa", 3)
        ptr = [ps(k, f"p5_ptr{i}", [128, 512], F32, st) for i in range(2)]
        ptr_b = cx.bufs_n("p5_ptr", 2)
        ln = LNorm(k, st, "p5_ln", ln_g, ln_b)
        n_sl = 0
        n_pa = 0
        n_tr = 0
        n_h = 0
        for m in range(n_my):
            cx.dma("sp", oT[:], scr["oT"][:, :, m * TT:(m + 1) * TT].rearrange("c p t -> p c t"), oT_b, writes=[oT_b])
            for tb in range(4):
                r0 = m * TT + tb * 128
                cx.dma("sp", yb[:, tb, :], xq[r0:r0 + 128, :], y_b[tb], writes=[y_b[tb]])
            for fc in range(8):
                si = n_sl % 2
                n_sl += 1
                C = slice(fc * 512, (fc + 1) * 512)
                cx.dma("pool", slabs[si][:], w_out[fc], slab_b[si], writes=[slab_b[si]])
                cx.dma("sp", grow[si][:], modrow[2:3, C].to_broadcast([128, 512]), grow_b[si], writes=[grow_b[si]])
                for tb in range(4):
                    pi = n_pa % 3
                    n_pa += 1
                    cx.op("pe", [(lambda j=j: nc.tensor.matmul(pa[pi][:], oT[:, j, tb * 128:(tb + 1) * 128], slabs[si][:, j, :], start=(j == 0), stop=(j == 31)))
                                 for j in range(32)], reads=[oT_b, slab_b[si]], writes=[pa_b[pi]])
                    ti = pi % 2
                    cx.op("dve", lambda: nc.vector.tensor_tensor(out=tmp[ti][:], in0=pa[pi][:], in1=grow[si][:], op=ALU.mult),
                          reads=[pa_b[pi], grow_b[si]], writes=[tmp_b[ti]])
                    cx.op("dve", lambda: nc.vector.scalar_tensor_tensor(out=yb[:, tb, C], in0=yb[:, tb, C], scalar=alpha, in1=tmp[ti][:], op0=ALU.mult, op1=ALU.add),
                          reads=[y_b[tb], tmp_b[ti]], writes=[y_b[tb]])
            for tb in range(4):
                r0 = m * TT + tb * 128
                ln.run(yb[:, tb, :], y_b[tb], junk[:], junk_b)
                cx.dma("sp", scr["x1"][r0:r0 + 128, :], yb[:, tb, :], y_b[tb], reads=[y_b[tb]])
                hi = n_h % 2
                n_h += 1
                for j4 in range(8):
                    pi = n_tr % 2
                    n_tr += 1
                    cx.op("pe", [(lambda i=i: nc.tensor.transpose(ptr[pi][:, i * 128:(i + 1) * 128], yb[:, tb, (j4 * 4 + i) * 128:(j4 * 4 + i + 1) * 128], k.ident_f[:]))
                                 for i in range(4)], reads=[y_b[tb], k.b_const], writes=[ptr_b[pi]])
                    for i in range(4):
                        j = j4 * 4 + i
                        cx.op("act", lambda: nc.scalar.activation(out=hst[hi][:, j, :], in_=ptr[pi][:, i * 128:(i + 1) * 128], func=AF.Identity,
                                                                  scale=k.modT[:, 128 + j:129 + j], bias=k.modT[:, 96 + j:97 + j]),
                              reads=[ptr_b[pi], k.b_modT], writes=[hst_b[hi]])
                cx.dma("sp", scr["h2T"][:, :, r0:r0 + 128].rearrange("c p t -> p c t"), hst[hi][:], hst_b[hi], reads=[hst_b[hi]])
        cx.barrier()


def emit_p6(k, scr, w1_d, w2_d, modrow, ln_g, ln_b, alpha, out_d, n_my=NT_MY, n_sc=DFF // 256):
    nc, cx = k.nc, k.cx
    with ExitStack() as st:
        h2 = sb(k, "p6_h2", [128, 32, TT], BF16, st)
        h2_b = cx.buf("p6_h2")
        acc = sb(k, "p6_acc", [128, 4, D], F32, st)
        acc_b = cx.bufs_n("p6_acc", 4)
        accf_b = [[cx.buf(f"p6_acc{tb}_{fc}") for fc in range(8)] for tb in range(4)]
        etmp = [sb(k, f"p6_etmp{i}", [128, 512], F32, st) for i in range(3)]
        etmp_b = cx.bufs_n("p6_etmp", 3)
        n_e = 0
        s1 = [sb(k, f"p6_s1{i}", [128, 32, 256], BF16, st) for i in range(2)]
        s1_b = cx.bufs_n("p6_s1", 2)
        s2 = [sb(k, f"p6_s2{i}", [128, 2, D], BF16, st) for i in range(2)]
        s2_b = cx.bufs_n("p6_s2", 2)
        uT = [sb(k, f"p6_uT{i}", [128, 2, TT], BF16, st) for i in range(2)]
        uT_b = cx.bufs_n("p6_uT", 2)
        rl = [sb(k, f"p6_rl{i}", [128, TT], F32, st) for i in range(2)]
        rl_b = cx.bufs_n("p6_rl", 2)
        xr = sb(k, "p6_xr", [128, D], F32, st)
        xr_b = cx.buf("p6_xr")
        grow = [sb(k, f"p6_grow{i}", [128, 512], F32, st) for i in range(2)]
        grow_b = cx.bufs_n("p6_grow", 2)
        pu = [ps(k, f"p6_pu{i}", [128, 512], F32, st) for i in range(2)]
        pu_b = cx.bufs_n("p6_pu", 2)
        po = [ps(k, f"p6_po{i}", [128, 512], F32, st) for i in range(4)]
        po_b = cx.bufs_n("p6_po", 4)
        ln = LNorm(k, st, "p6_ln", ln_g, ln_b)
        n_u = 0
        n_o = 0
        n_g = 0
        tot = n_my * n_sc

        def load1(idx):
            if idx < tot:
                cx.dma("pool", s1[idx % 2][:], w1_d[idx % n_sc], s1_b[idx % 2], writes=[s1_b[idx % 2]])

        def load2(idx):
            if idx < tot:
                cx.dma("pool", s2[idx % 2][:], w2_d[idx % n_sc], s2_b[idx % 2], writes=[s2_b[idx % 2]])

        def ff1_steps(idx):
            nonlocal n_u
            si = idx % 2
            ui = idx % 2
            for c in range(2):
                pi = n_u % 2
                n_u += 1
                h = cx.op_begin("pe", reads=[s1_b[si], h2_b], writes=[pu_b[pi]])
                for j in range(32):
                    fn = (lambda j=j: nc.tensor.matmul(pu[pi][:], s1[si][:, j, c * 128:(c + 1) * 128], h2[:, j, :], start=(j == 0), stop=(j == 31)))
                    if j < 31:
                        cx.op_piece(h, fn)
                    else:
                        cx.op_end(h, fn)
                        cx.op("act", lambda: nc.scalar.activation(out=rl[pi][:], in_=pu[pi][:], func=AF.Relu), reads=[pu_b[pi]], writes=[rl_b[pi]])
                        cx.op("pool", lambda: nc.gpsimd.tensor_tensor(out=uT[ui][:, c, :], in0=rl[pi][:], in1=rl[pi][:], op=ALU.mult),
                              reads=[rl_b[pi]], writes=[uT_b[ui]])
                    yield

        def ff1(idx):
            for _ in ff1_steps(idx):
                pass

        load1(0)
        load2(0)
        load1(1)
        load2(1)
        def start_tile(m):
            cx.dma("sp", h2[:], scr["h2T"][:, :, m * TT:(m + 1) * TT].rearrange("c p t -> p c t"), h2_b, writes=[h2_b])
            ff1(m * n_sc)

        for m in range(n_my):
            start_tile(m)
            for sc in range(n_sc):
                idx = m * n_sc + sc
                si = idx % 2
                ui = idx % 2
                load1(idx + 2)
                gen = ff1_steps(idx + 1) if sc + 1 < n_sc else None
                for tb in range(4):
                    for fc in range(8):
                        if gen is not None:
                            next(gen, None)
                            next(gen, None)
                        oi = n_o % 4
                        n_o += 1
                        C = slice(fc * 512, (fc + 1) * 512)
                        cx.op("pe", [(lambda c=c: nc.tensor.matmul(po[oi][:], uT[ui][:, c, tb * 128:(tb + 1) * 128], s2[si][:, c, C], start=(c == 0), stop=(c == 1)))
                                     for c in range(2)], reads=[uT_b[ui], s2_b[si]], writes=[po_b[oi]])
                        ab = accf_b[tb][fc]
                        if sc == 0:
                            cx.op("dve", lambda: nc.vector.tensor_copy(out=acc[:, tb, C], in_=po[oi][:]), reads=[po_b[oi]], writes=[ab, acc_b[tb]])
                        elif n_o % 2 == 0:
                            cx.op("dve", lambda: nc.vector.tensor_tensor(out=acc[:, tb, C], in0=acc[:, tb, C], in1=po[oi][:], op=ALU.add),
                                  reads=[po_b[oi], ab], writes=[ab])
                        else:
                            ei = n_e % 3
                            n_e += 1
                            cx.op("act", lambda: nc.scalar.copy(out=etmp[ei][:], in_=po[oi][:]), reads=[po_b[oi]], writes=[etmp_b[ei]])
                            cx.op("pool", lambda: nc.gpsimd.tensor_tensor(out=acc[:, tb, C], in0=acc[:, tb, C], in1=etmp[ei][:], op=ALU.add),
                                  reads=[etmp_b[ei], ab], writes=[ab])
                if gen is not None:
                    for _ in gen:
                        pass
                load2(idx + 2)
            for tb in range(4):
                r0 = m * TT + tb * 128
                cx.dma("sp", xr[:], scr["x1"][r0:r0 + 128, :], xr_b, writes=[xr_b])
                cx.op("dve", lambda: nc.vector.tensor_copy(out=acc[:, tb, 0:1], in_=acc[:, tb, 0:1]), reads=accf_b[tb], writes=[acc_b[tb]])
                for fc in range(8):
                    gi = n_g % 2
                    n_g += 1
                    C = slice(fc * 512, (fc + 1) * 512)
                    cx.dma("sp", grow[gi][:], modrow[5:6, C].to_broadcast([128, 512]), grow_b[gi], writes=[grow_b[gi]])
                    cx.op("dve", lambda: nc.vector.tensor_tensor(out=acc[:, tb, C], in0=acc[:, tb, C], in1=grow[gi][:], op=ALU.mult),
                          reads=[acc_b[tb], grow_b[gi]], writes=[acc_b[tb]])
                cx.op("dve", lambda: nc.vector.scalar_tensor_tensor(out=acc[:, tb, :], in0=xr[:], scalar=alpha, in1=acc[:, tb, :], op0=ALU.mult, op1=ALU.add),
                      reads=[xr_b, acc_b[tb]], writes=[acc_b[tb]])
                ln.run(acc[:, tb, :], acc_b[tb], xr[:], xr_b)
                cx.dma("sp", out_d[r0:r0 + 128, :], acc[:, tb, :], acc_b[tb], reads=[acc_b[tb]])
        cx.barrier()


ALPHA_C = 2.0 ** 0.25
CONST_NAMES = ("ident_f", "prot_p", "prot_m", "invf", "mA", "mB", "mW", "mC", "vS", "aS", "ovl")


def setup(nc, stack):
    k = K()
    k.nc = nc
    k.stack = stack
    k.cx = Ctx(nc, stack)
    cx = k.cx
    I = "ExternalInput"
    k.cd = {}
    shapes = {"ident_f": [128, 128], "prot_p": [128, 128], "prot_m": [128, 128], "invf": [128, 2], "mA": [128, 4, 512], "mB": [128, 4, 512],
              "mW": [3, 128, 4, 512], "mC": [4, 128, 4, 512], "vS": [4, 128, 4, 64], "aS": [4, 128, 4, 64], "ovl": [128, 4, 64]}
    for n in CONST_NAMES:
        k.cd[n] = dram(nc, "c_" + n, shapes[n], F32, I)
    k.ident_f = sb(k, "ident_f", [128, 128], F32)
    k.prot_p = sb(k, "prot_p", [128, 128], F32)
    k.prot_m = sb(k, "prot_m", [128, 128], F32)
    k.invf = sb(k, "invf", [128, 2], F32)
    k.modT = sb(k, "modT", [128, 192], F32)
    k.negpi = sb(k, "negpi", [128, 1], F32)
    k.ones_bf = sb(k, "ones_bf", [128, 128], BF16)
    k.ident_b = sb(k, "ident_b", [128, 128], BF16)
    k.eps6 = sb(k, "eps6", [128, 1], F32)
    k.eps5 = sb(k, "eps5", [128, 1], F32)
    k.kcv_b = cx.buf("kcv")
    k.b_const = cx.buf("const")
    k.b_modT = cx.buf("modT")
    for dst, src in ((k.ident_f, "ident_f"), (k.prot_p, "prot_p"), (k.prot_m, "prot_m"), (k.invf, "invf")):
        cx.dma("sp", dst[:], k.cd[src], k.b_const, writes=[k.b_const])
    cx.op("dve", lambda: nc.vector.memset(k.negpi[:], -PI), writes=[k.b_const])
    cx.op("dve", lambda: nc.vector.memset(k.ones_bf[:], 1.0), writes=[k.b_const])
    cx.op("dve", lambda: nc.vector.memset(k.eps6[:], 1e-6), writes=[k.b_const])
    cx.op("dve", lambda: nc.vector.memset(k.eps5[:], 1e-5), writes=[k.b_const])
    cx.op("dve", lambda: nc.vector.tensor_copy(out=k.ident_b[:], in_=k.ident_f[:]), reads=[k.b_const], writes=[k.b_const])
    return k


def build_program():
    nc = bass.Bass("TRN2", target_bir_lowering=False)
    with ExitStack() as st:
        k = setup(nc, st)
        I = "ExternalInput"
        x_all = dram(nc, "x_all", [S, D], F32, I)
        xq = dram(nc, "xq", [NT_MY * TT, D], F32, I)
        cT = dram(nc, "cT", [128, 32], F32, I)
        pos_all = dram(nc, "pos_all", [1, S], I32, I)
        pos_q = dram(nc, "pos_q", [1, NT_MY * TT], I32, I)
        w_ada = dram(nc, "w_ada", [48, 128, 32, 512], F32, I)
        b_ada = dram(nc, "b_ada", [1, 6 * D], F32, I)
        w_in = dram(nc, "w_in", [len(W_IN_SLABS), 128, 32, 256], F32, I)
        k1 = dram(nc, "k1", [128, 32, 256], F32, I)
        k2 = dram(nc, "k2", [128, 2, 128], F32, I)
        v1 = dram(nc, "v1", [128, 32, 256], F32, I)
        v2 = dram(nc, "v2", [128, 2, 128], F32, I)
        posk = dram(nc, "posk", [128, 32], F32, I)
        posv = dram(nc, "posv", [128, 32], F32, I)
        qnT = dram(nc, "qnT", [128, 12], F32, I)
        kvnT = dram(nc, "kvnT", [128, 4], F32, I)
        w_uq = dram(nc, "w_uq", [16, 128, 12, 192], F32, I)
        w_ukv = dram(nc, "w_ukv", [16, 128, 4, 256], F32, I)
        w_out = dram(nc, "w_out", [8, 128, 32, 512], F32, I)
        l1g = dram(nc, "l1g", [1, D], F32, I)
        l1b = dram(nc, "l1b", [1, D], F32, I)
        l2g = dram(nc, "l2g", [1, D], F32, I)
        l2b = dram(nc, "l2b", [1, D], F32, I)
        w1 = dram(nc, "w_ff1", [64, 128, 32, 256], F32, I)
        w2 = dram(nc, "w_ff2", [64, 128, 2, 4096], F32, I)
        out = dram(nc, "out", [NT_MY * TT, D], F32, "ExternalOutput")
        modrow = dram(nc, "s_modrow", [6, D], F32)
        NQ = NT_MY * TT
        scr = dict(
            kcmpT=dram(nc, "s_kcmpT", [4, 128, S], BF16), vcmpT=dram(nc, "s_vcmpT", [4, 128, S], BF16),
            kselT=dram(nc, "s_kselT", [4, 128, S], BF16), kwinT=dram(nc, "s_kwinT", [4, 128, S], BF16),
            vsel=dram(nc, "s_vsel", [S, 512], BF16), vwin=dram(nc, "s_vwin", [S, 512], BF16),
            ckvf=dram(nc, "s_ckvf", [4, 128, S], F32), kpeT=dram(nc, "s_kpeT", [64, S], BF16),
            qT=dram(nc, "s_qT", [16, 128, NQ], BF16), gtok=dram(nc, "s_gtok", [NQ, 48], F32), cqf=dram(nc, "s_cqf", [12, 128, NQ], F32),
            qmnT=dram(nc, "s_qmnT", [16, 128, NQ], BF16), qmpT=dram(nc, "s_qmpT", [16, 64, NQ], BF16),
            ocmp=dram(nc, "s_ocmp", [4, NQ, 4, 128], F32), snegs=dram(nc, "s_snegs", [4, NQ, 64], F32),
            oT=dram(nc, "s_oT", [32, 128, NQ], BF16), x1=dram(nc, "s_x1", [NQ, D], F32), h2T=dram(nc, "s_h2T", [32, 128, NQ], BF16))
        emit_p0(k, cT, w_ada, b_ada, modrow)
        emit_p1(k, x_all, xq, pos_all, pos_q, w_in, scr)
        emit_p2(k, scr, w_uq, w_ukv, qnT, kvnT, pos_q, k.cd["mA"], k.cd["mB"])
        with ExitStack() as st34:
            k.kcT = sb(k, "kcT", [128, 4, 512], BF16, st34)
            k.vc = sb(k, "vc", [128, 4, 4, 128], BF16, st34)
            emit_p3(k, scr, k1, k2, v1, v2, posk, posv)
            emit_p4_sync(k, scr, k.cd)
        emit_p5(k, scr, xq, w_out, modrow, l1g, l1b, ALPHA_C)
        emit_p6(k, scr, w1, w2, modrow, l2g, l2b, ALPHA_C, out)
        k.cx.final_wait()
    return nc


def kernel(x, c, positions, w_ada, b_ada, w_in, nsa_pos_k, nsa_pos_v, nsa_cmp_k1, nsa_cmp_k2,
           nsa_cmp_v1, nsa_cmp_v2, mla_q_norm, mla_kv_norm, mla_w_uq, mla_w_ukv, w_out,
           ln1_g, ln1_b, w_ff1, w_ff2, ln2_g, ln2_b):
    f32 = np.float32
    A = lambda a: np.asarray(a)
    x = A(x); c = A(c); positions = A(positions)
    nc = build_program()
    tw = host_tile_weights(A(w_ada)[0], A(w_in)[0], A(nsa_cmp_k1)[0], A(nsa_cmp_k2)[0], A(nsa_cmp_v1)[0], A(nsa_cmp_v2)[0],
                           A(mla_w_uq)[0], A(mla_w_ukv)[0], A(w_out)[0], A(w_ff1)[0], A(w_ff2)[0])
    shared = {
        "b_ada": A(b_ada).reshape(1, -1),
        "posk": np.ascontiguousarray(A(nsa_pos_k)[0].T), "posv": np.ascontiguousarray(A(nsa_pos_v)[0].T),
        "qnT": np.ascontiguousarray(A(mla_q_norm)[0].reshape(12, 128).T), "kvnT": np.ascontiguousarray(A(mla_kv_norm)[0].reshape(4, 128).T),
        "l1g": A(ln1_g).reshape(1, -1), "l1b": A(ln1_b).reshape(1, -1), "l2g": A(ln2_g).reshape(1, -1), "l2b": A(ln2_b).reshape(1, -1),
    }
    shared.update(tw)
    consts = [host_constants(0), host_constants(1)]
    in_maps = []
    rows_of = []
    for core in range(8):
        b, hf = core // 2, core % 2
        rows = np.concatenate([np.arange((2 * m + hf) * TT, (2 * m + hf + 1) * TT) for m in range(NT_MY)])
        rows_of.append((b, rows))
        im = dict(shared)
        im["x_all"] = x[b]
        im["xq"] = np.ascontiguousarray(x[b][rows])
        im["cT"] = np.ascontiguousarray(c[b].reshape(32, 128).T)
        im["pos_all"] = np.ascontiguousarray(positions[b].reshape(1, -1)).astype(np.int32)
        im["pos_q"] = np.ascontiguousarray(positions[b][rows].reshape(1, -1)).astype(np.int32)
        for n in CONST_NAMES:
            im["c_" + n] = consts[hf][n]
        in_maps.append(im)
    res = run_bass_kernel_spmd(nc, in_maps, core_ids=list(range(8)))
    outp = np.empty((4, S, D), dtype=f32)
    for core in range(8):
        b, rows = rows_of[core]
        outp[b, rows] = np.asarray(res.results[core]["out"], dtype=f32)
    return outp
```

```python
import math
import os
from contextlib import ExitStack
import numpy as np
import concourse.bass as bass
import concourse.mybir as mybir
from concourse.bass_utils import run_bass_kernel_spmd


F32 = mybir.dt.float32
BF16 = mybir.dt.bfloat16
I32 = mybir.dt.int32
AF = mybir.ActivationFunctionType
ALU = mybir.AluOpType
AX = mybir.AxisListType


class Buf:
    __slots__ = ("name", "last_w", "readers", "dsem", "dcount", "ctx")

    def __init__(self, ctx, name):
        self.ctx = ctx
        self.name = name
        self.last_w = None
        self.readers = {}
        self.dsem = None
        self.dcount = 0


class EngState:
    def __init__(self, name, eng, sem):
        self.name = name
        self.eng = eng
        self.sem = sem
        self.count = 0
        self.known = {}


class Ctx:
    def __init__(self, nc, stack):
        self.nc = nc
        self.stack = stack
        self.E = {}
        for name, eng in (("pe", nc.tensor), ("act", nc.scalar), ("dve", nc.vector),
                          ("pool", nc.gpsimd), ("sp", nc.sync)):
            sem = stack.enter_context(nc.semaphore("sem_" + name))
            self.E[name] = EngState(name, eng, sem)
        self.bufs = []
        self.dma_bufs = []
        self.free_dsems = []
        self.nwaits = 0
        self.ninst = 0

    def buf(self, name):
        b = Buf(self, name)
        self.bufs.append(b)
        return b

    def bufs_n(self, name, n):
        return [self.buf(f"{name}{i}") for i in range(n)]

    def _wait_token(self, es, tok):
        if tok is None:
            return
        if tok[0] == "e":
            _, en, c = tok
            if en == es.name and en == "pe":
                return
            if es.known.get(en, 0) >= c:
                return
            es.eng.wait_ge(self.E[en].sem, c)
            es.known[en] = c
            self.nwaits += 1
        else:
            _, b = tok
            c = b.dcount
            key = ("d", id(b.dsem))
            if es.known.get(key, 0) >= c:
                return
            es.eng.wait_ge(b.dsem, 16 * c)
            es.known[key] = c
            self.nwaits += 1

    def _deps(self, es, reads, writes):
        for b in reads:
            self._wait_token(es, b.last_w)
        for b in writes:
            self._wait_token(es, b.last_w)
            for tok in list(b.readers.values()):
                self._wait_token(es, tok)

    def _commit(self, tok, reads, writes):
        for b in reads:
            if tok[0] == "e":
                b.readers[tok[1]] = tok
            else:
                b.readers[("d", id(tok[1]))] = tok
        for b in writes:
            b.last_w = tok
            b.readers = {}

    def op(self, en, fns, reads=(), writes=()):
        es = self.E[en]
        if callable(fns):
            fns = [fns]
        self._deps(es, reads, writes)
        ins = None
        for f in fns:
            ins = f()
            self.ninst += 1
        es.count += 1
        ins.then_inc(es.sem, 1)
        tok = ("e", en, es.count)
        self._commit(tok, reads, writes)
        return tok

    def op_begin(self, en, reads=(), writes=()):
        es = self.E[en]
        self._deps(es, reads, writes)
        return (en, list(reads), list(writes))

    def op_piece(self, h, fn):
        fn()
        self.ninst += 1

    def op_end(self, h, fn):
        en, reads, writes = h
        es = self.E[en]
        ins = fn()
        self.ninst += 1
        es.count += 1
        ins.then_inc(es.sem, 1)
        tok = ("e", en, es.count)
        self._commit(tok, reads, writes)
        return tok

    def dma(self, q, out, in_, sb, reads=(), writes=(), **kw):
        es = self.E[q]
        if sb.dsem is None:
            if self.free_dsems:
                sb.dsem, sb.dcount = self.free_dsems.pop()
            else:
                sb.dsem = self.stack.enter_context(self.nc.semaphore("dsem%d" % len(self.dma_bufs) + "_" + sb.name))
                sb.dcount = 0
            self.dma_bufs.append(sb)
        self._deps(es, reads, writes)
        ins = es.eng.dma_start(out=out, in_=in_, **kw)
        sb.dcount += 1
        ins.then_inc(sb.dsem, 16)
        self.ninst += 1
        tok = ("d", sb)
        self._commit(tok, reads, writes)
        return tok

    def barrier(self, engines=("pe", "act", "dve", "pool", "sp"), release=True):
        for en in engines:
            es = self.E[en]
            for on, os_ in self.E.items():
                if os_.count == 0 or (on == en and en in ("pe", "sp")):
                    continue
                if es.known.get(on, 0) < os_.count:
                    es.eng.wait_ge(os_.sem, os_.count)
                    es.known[on] = os_.count
            for b in self.dma_bufs:
                key = ("d", id(b.dsem))
                if b.dcount and es.known.get(key, 0) < b.dcount:
                    es.eng.wait_ge(b.dsem, 16 * b.dcount)
                    es.known[key] = b.dcount
        if not release:
            return
        for b in self.bufs:
            b.last_w = None
            b.readers = {}
        for b in self.dma_bufs:
            self.free_dsems.append((b.dsem, b.dcount))
            b.dsem = None
            b.dcount = 0
        self.dma_bufs = []

    def final_wait(self):
        self.barrier(engines=("sp",), release=False)


D = 4096
S = 4096
NT_ALL = 8
NT_MY = 4
TT = 512
THETA = 500000.0
DIN = 7280
DFF = 16384
PI = math.pi


W_IN_SLABS = ([(2048 + (kind * 4 + gp * 2) * 128, 256) for kind in range(6) for gp in range(2)] + [(6704, 256), (6960, 256), (7216, 64)]
              + [(hp * 256, 256) for hp in range(8)] + [(5120, 48)] + [(5168 + sl * 256, 256) for sl in range(6)])
W_IN_SLAB_INDEX = {cn: i for i, cn in enumerate(W_IN_SLABS)}


def host_tile_weights(w_ada, w_in, k1, k2, v1, v2, w_uq, w_ukv, w_out, w_ff1, w_ff2):
    t = {}
    t["w_ada"] = np.ascontiguousarray(w_ada.reshape(32, 128, 48, 512).transpose(2, 1, 0, 3))
    wi = np.zeros((len(W_IN_SLABS), 128, 32, 256), np.float32)
    for i, (c0, n) in enumerate(W_IN_SLABS):
        wi[i, :, :, :n] = w_in[:, c0:c0 + n].reshape(32, 128, n).transpose(1, 0, 2)
    t["w_in"] = wi
    t["k1"] = np.ascontiguousarray(k1.reshape(32, 128, 256).transpose(1, 0, 2))
    t["v1"] = np.ascontiguousarray(v1.reshape(32, 128, 256).transpose(1, 0, 2))
    t["k2"] = np.ascontiguousarray(k2.reshape(2, 128, 128).transpose(1, 0, 2))
    t["v2"] = np.ascontiguousarray(v2.reshape(2, 128, 128).transpose(1, 0, 2))
    t["w_uq"] = np.ascontiguousarray(w_uq.reshape(12, 128, 16, 192).transpose(2, 1, 0, 3))
    t["w_ukv"] = np.ascontiguousarray(w_ukv.reshape(4, 128, 16, 256).transpose(2, 1, 0, 3))
    t["w_out"] = np.ascontiguousarray(w_out.reshape(32, 128, 8, 512).transpose(2, 1, 0, 3))
    t["w_ff1"] = np.ascontiguousarray(w_ff1.reshape(32, 128, 64, 256).transpose(2, 1, 0, 3))
    t["w_ff2"] = np.ascontiguousarray(w_ff2.reshape(64, 2, 128, 4096).transpose(0, 2, 1, 3))
    return t


class K:
    pass


def dram(nc, name, shape, dt, kind=None):
    if kind is None:
        return nc.dram_tensor(name, list(shape), dt).ap()
    return nc.dram_tensor(name, list(shape), dt, kind=kind).ap()


def sb(k, name, shape, dt, stack=None):
    st = stack if stack is not None else k.stack
    t = st.enter_context(k.nc.sbuf_tensor(name, list(shape), dt))
    return t


def ps(k, name, shape, dt, stack=None):
    st = stack if stack is not None else k.stack
    t = st.enter_context(k.nc.psum_tensor(name, list(shape), dt))
    return t


def host_constants(hf):
    c = {}
    c["ident_f"] = np.eye(128, dtype=np.float32)
    pr = np.zeros((128, 128), np.float32)
    for m in range(16):
        pr[m + 16, m] = -1.0
        pr[m, m + 16] = 1.0
    c["prot_p"] = pr
    pm = np.zeros((128, 128), np.float32)
    for m in range(32):
        pm[m + 32, m] = -1.0
        pm[m, m + 32] = 1.0
    c["prot_m"] = pm
    inv = np.zeros((128, 2), np.float32)
    for r in range(32):
        inv[r, 0] = THETA ** (-(2.0 * (r % 16)) / 32.0)
    for r in range(64):
        inv[r, 1] = THETA ** (-(2.0 * (r % 32)) / 64.0)
    c["invf"] = inv
    NEGM = -30000.0
    qq = (np.arange(4)[:, None] * 128 + np.arange(128)[None, :])
    kk = np.arange(512)

    def band(dT):
        diff = dT * 512 + qq.T[:, :, None] - kk[None, None, :]
        return np.where((diff >= 0) & (diff < 512), 0.0, NEGM).astype(np.float32)

    def causal(dT):
        diff = dT * 512 + qq.T[:, :, None] - kk[None, None, :]
        return np.where(diff >= 0, 0.0, NEGM).astype(np.float32)
    c["mA"] = causal(0) if hf == 0 else causal(1)
    c["mB"] = causal(-1) if hf == 0 else causal(0)
    if hf == 0:
        c["mW"] = np.stack([band(1), band(0), band(-1)], axis=0)
    else:
        c["mW"] = np.stack([band(2), band(1), band(0)], axis=0)
    n = np.arange(512)
    mC = np.zeros((4, 128, 4, 512), np.float32)
    vS = np.zeros((4, 128, 4, 64), np.float32)
    aS = np.zeros((4, 128, 4, 64), np.float32)
    j = np.arange(64)
    for m in range(4):
        t = (2 * m + hf) * 512 + qq.T
        ok = (16 * n[None, None, :] + 31 <= t[:, :, None]) & (n[None, None, :] < 255)
        mC[m] = np.where(ok, 0.0, NEGM)
        cur = (t // 64)[:, :, None]
        jj = j[None, None, :]
        valid = jj <= cur
        forced = (jj == 0) | ((jj <= cur) & (jj > cur - 2))
        vS[m] = valid.astype(np.float32)
        aS[m] = np.where(valid, np.where(forced, 1e4, 0.0), -1e30)
    c["mC"] = mC
    c["vS"] = vS
    c["aS"] = aS
    cs = n * 16
    bs = j * 64
    ov = ((cs[:, None] < bs[None, :] + 64) & (cs[:, None] + 32 > bs[None, :])).astype(np.float32)
    ov[255:] = 0.0
    c["ovl"] = np.ascontiguousarray(ov.reshape(4, 128, 64).transpose(1, 0, 2))
    return c


def emit_p0(k, cT, w_ada, b_ada, modrow, ncc=48):
    nc, cx = k.nc, k.cx
    with ExitStack() as st:
        cond_f = sb(k, "p0_condf", [128, 32], F32, st)
        cond_rep = sb(k, "p0_condrep", [128, 32, 128], BF16, st)
        NSL = 3
        slabs = [sb(k, f"p0_slab{i}", [128, 32, 512], BF16, st) for i in range(NSL)]
        slab_b = cx.bufs_n("p0_slab", NSL)
        brow = [sb(k, f"p0_brow{i}", [128, 512], F32, st) for i in range(NSL)]
        brow_b = cx.bufs_n("p0_brow", NSL)
        mrow = [sb(k, f"p0_mrow{i}", [128, 512], F32, st) for i in range(2)]
        mrow_b = cx.bufs_n("p0_mrow", 2)
        pacc = [ps(k, f"p0_pacc{i}", [128, 512], F32, st) for i in range(2)]
        pacc_b = cx.bufs_n("p0_pacc", 2)
        ptr = [ps(k, f"p0_ptr{i}", [128, 512], F32, st) for i in range(2)]
        ptr_b = cx.bufs_n("p0_ptr", 2)
        b_cond = cx.buf("p0_cond")
        b_crep = cx.buf("p0_crep")

        cx.dma("sp", cond_f[:], cT, b_cond, writes=[b_cond])
        cx.op("act", lambda: nc.scalar.activation(out=cond_f[:], in_=cond_f[:], func=AF.Silu),
              reads=[b_cond], writes=[b_cond])
        cx.op("dve", lambda: nc.vector.tensor_copy(
            out=cond_rep[:], in_=cond_f[:].unsqueeze(2).to_broadcast([128, 32, 128])),
            reads=[b_cond], writes=[b_crep])

        def load(cc):
            s = cc % NSL
            cx.dma("pool", slabs[s][:], w_ada[cc], slab_b[s], writes=[slab_b[s]])
            cx.dma("sp", brow[s][:], b_ada[0:1, cc * 512:(cc + 1) * 512].to_broadcast([128, 512]),
                   brow_b[s], writes=[brow_b[s]])

        PRE = NSL - 1
        for cc in range(min(PRE, ncc)):
            load(cc)
        for cc in range(ncc):
            if cc + PRE < ncc:
                load(cc + PRE)
            s = cc % NSL
            pa, pab = pacc[cc % 2], pacc_b[cc % 2]
            cx.op("pe", [(lambda j=j: nc.tensor.matmul(pa[:], cond_rep[:, j, :], slabs[s][:, j, :],
                                                        start=(j == 0), stop=(j == 31))) for j in range(32)],
                  reads=[b_crep, slab_b[s]], writes=[pab])
            v = cc // 8
            plus = 1.0 if v in (1, 2, 4, 5) else 0.0
            mr, mrb = mrow[cc % 2], mrow_b[cc % 2]
            cx.op("dve", lambda: nc.vector.scalar_tensor_tensor(
                out=mr[:], in0=pa[:], scalar=plus, in1=brow[s][:], op0=ALU.add, op1=ALU.add),
                reads=[pab, brow_b[s]], writes=[mrb])
            cx.dma("sp", modrow[v:v + 1, (cc % 8) * 512:(cc % 8 + 1) * 512], mr[0:1, :], mrb, reads=[mrb])
            pt, ptb = ptr[cc % 2], ptr_b[cc % 2]
            cx.op("pe", [(lambda i=i: nc.tensor.transpose(pt[:, i * 128:(i + 1) * 128],
                                                           mr[:, i * 128:(i + 1) * 128], k.ident_f[:]))
                         for i in range(4)],
                  reads=[mrb, k.b_const], writes=[ptb])
            cx.op("act", lambda: nc.scalar.copy(
                out=k.modT[:, cc * 4:(cc + 1) * 4],
                in_=pt[:].rearrange("p (a b) -> p a b", b=128)[:, :, 0]),
                reads=[ptb], writes=[k.b_modT])
        cx.barrier()


def rope_tables(k, st, pos_ap, ntok, tabs, tab_b, tmp, tmp_b, posi, posi_b):
    nc, cx = k.nc, k.cx
    cx.dma("sp", posi[:, 0:ntok], pos_ap.to_broadcast([64, ntok]), posi_b, writes=[posi_b])
    posf = tmp[0]
    cx.op("dve", lambda: nc.vector.tensor_copy(out=posf[:, 0:ntok], in_=posi[:, 0:ntok]),
          reads=[posi_b], writes=[tmp_b[0]])
    ang, t1, t2 = tmp[1], tmp[2], tmp[3]
    b_ang, b1, b2 = tmp_b[1], tmp_b[2], tmp_b[3]
    ti, bi = k.p1_ti, k.p1_ti_b
    N = slice(0, ntok)
    for col, (cn, sn) in ((0, ("cosP", "sinP")), (1, ("cosM", "sinM"))):
        cx.op("dve", lambda: nc.vector.tensor_scalar(
            out=ang[:, N], in0=posf[:, N], scalar1=k.invf[0:64, col:col + 1], scalar2=None,
            op0=ALU.mult), reads=[tmp_b[0], k.b_const], writes=[b_ang])
        for name, shift in ((sn, 0.0), (cn, 0.5 * PI)):
            cx.op("dve", lambda: nc.vector.tensor_scalar(out=t1[:, N], in0=ang[:, N], scalar1=shift, scalar2=None, op0=ALU.add),
                  reads=[b_ang], writes=[b1])
            cx.op("dve", lambda: nc.vector.tensor_scalar(out=t2[:, N], in0=t1[:, N], scalar1=1.0 / (2 * PI), scalar2=None, op0=ALU.mult),
                  reads=[b1], writes=[b2])
            cx.op("dve", lambda: nc.vector.tensor_copy(out=ti[:, N], in_=t2[:, N]), reads=[b2], writes=[bi])
            cx.op("dve", lambda: nc.vector.tensor_copy(out=t2[:, N], in_=ti[:, N]), reads=[bi], writes=[b2])
            cx.op("dve", lambda: nc.vector.scalar_tensor_tensor(out=t1[:, N], in0=t2[:, N], scalar=-2 * PI, in1=t1[:, N], op0=ALU.mult, op1=ALU.add),
                  reads=[b1, b2], writes=[b1])
            cx.op("dve", lambda: nc.vector.tensor_scalar(out=t2[:, N], in0=t1[:, N], scalar1=PI, scalar2=None, op0=ALU.is_gt), reads=[b1], writes=[b2])
            cx.op("dve", lambda: nc.vector.scalar_tensor_tensor(out=t1[:, N], in0=t2[:, N], scalar=-2 * PI, in1=t1[:, N], op0=ALU.mult, op1=ALU.add),
                  reads=[b1, b2], writes=[b1])
            cx.op("dve", lambda: nc.vector.tensor_scalar(out=t2[:, N], in0=t1[:, N], scalar1=-PI, scalar2=None, op0=ALU.is_lt), reads=[b1], writes=[b2])
            cx.op("dve", lambda: nc.vector.scalar_tensor_tensor(out=t1[:, N], in0=t2[:, N], scalar=2 * PI, in1=t1[:, N], op0=ALU.mult, op1=ALU.add),
                  reads=[b1, b2], writes=[b1])
            cx.op("act", lambda: nc.scalar.activation(out=tabs[name][:, N], in_=t1[:, N], func=AF.Sin),
                  reads=[b1], writes=[tab_b[name]])


def emit_p1(k, x_all, xq, pos_all, pos_q, w_in, scr, n_all=NT_ALL, n_my=NT_MY, do_kv=True, do_q=True):
    nc, cx = k.nc, k.cx
    with ExitStack() as st:
        NSL = 3
        CW = 256
        slabs = [sb(k, f"p1_slab{i}", [128, 32, CW], BF16, st) for i in range(NSL)]
        slab_b = cx.bufs_n("p1_slab", NSL)
        hT = [sb(k, f"p1_hT{i}", [128, 32, TT], BF16, st) for i in range(2)]
        hT_b = cx.bufs_n("p1_hT", 2)
        xst = [sb(k, f"p1_x{i}", [128, D], F32, st) for i in range(2)]
        xst_b = cx.bufs_n("p1_x", 2)
        tabs = {n: [sb(k, f"p1_{n}{i}", [64, TT], F32, st) for i in range(2)] for n in ("cosP", "sinP", "cosM", "sinM")}
        tab_b = {n: cx.bufs_n("p1_" + n, 2) for n in tabs}
        tmp = [sb(k, f"p1_tmp{i}", [64, TT], F32, st) for i in range(4)]
        tmp_b = cx.bufs_n("p1_tmp", 4)
        k.p1_ti = sb(k, "p1_ti", [64, TT], I32, st)
        k.p1_ti_b = cx.buf("p1_ti")
        posi = sb(k, "p1_posi", [64, TT], I32, st)
        posi_b = cx.buf("p1_posi")
        NSTG = 4
        stg = [sb(k, f"p1_stg{i}", [128, TT], BF16, st) for i in range(NSTG)]
        stg_b = cx.bufs_n("p1_stg", NSTG)
        stgf = [sb(k, f"p1_stgf{i}", [128, TT], F32, st) for i in range(2)]
        stgf_b = cx.bufs_n("p1_stgf", 2)
        qf = [sb(k, f"p1_qf{i}", [64, TT], F32, st) for i in range(2)]
        qf_b = cx.bufs_n("p1_qf", 2)
        rt = [sb(k, f"p1_rt{i}", [64, TT], F32, st) for i in range(2)]
        rt_b = cx.bufs_n("p1_rt", 2)
        ptr = [ps(k, f"p1_ptr{i}", [128, 512], F32, st) for i in range(2)]
        ptr_b = cx.bufs_n("p1_ptr", 2)
        pacc = [ps(k, f"p1_pacc{i}", [128, 512], F32, st) for i in range(3)]
        pacc_b = cx.bufs_n("p1_pacc", 3)
        ppar = [ps(k, f"p1_ppar{i}", [64, 512], F32, st) for i in range(2)]
        ppar_b = cx.bufs_n("p1_ppar", 2)
        cnt = {"x": 0, "stg": 0, "stgf": 0, "qf": 0, "pacc": 0, "ppar": 0, "sq": 0, "slab": 0, "ptr": 0}

        def nxt(name, n):
            i = cnt[name] % n
            cnt[name] += 1
            return i

        def build_hT(src, row0, slot):
            for tb in range(4):
                xi = nxt("x", 2)
                cx.dma("sp", xst[xi][:], src[row0 + tb * 128: row0 + (tb + 1) * 128, :], xst_b[xi], writes=[xst_b[xi]])
                for j4 in range(8):
                    pi = nxt("ptr", 2)
                    cx.op("pe", [(lambda i=i: nc.tensor.transpose(
                        ptr[pi][:, i * 128:(i + 1) * 128], xst[xi][:, (j4 * 4 + i) * 128:(j4 * 4 + i + 1) * 128], k.ident_f[:]))
                        for i in range(4)], reads=[xst_b[xi], k.b_const], writes=[ptr_b[pi]])
                    for i in range(4):
                        j = j4 * 4 + i
                        cx.op("act", lambda: nc.scalar.activation(
                            out=hT[slot][:, j, tb * 128:(tb + 1) * 128], in_=ptr[pi][:, i * 128:(i + 1) * 128],
                            func=AF.Identity, scale=k.modT[:, 32 + j:33 + j], bias=k.modT[:, j:j + 1]),
                            reads=[ptr_b[pi], k.b_modT], writes=[hT_b[slot]])

        def load_slab(c0, n):
            s = nxt("slab", NSL)
            cx.dma("pool", slabs[s][:, :, 0:n], w_in[W_IN_SLAB_INDEX[(c0, n)], :, :, 0:n], slab_b[s], writes=[slab_b[s]])
            return s

        def proj_fm(s, off, M, slot):
            pi = nxt("pacc", 3)
            cx.op("pe", [(lambda j=j: nc.tensor.matmul(pacc[pi][0:M, :], slabs[s][:, j, off:off + M], hT[slot][:, j, :],
                                                        start=(j == 0), stop=(j == 31))) for j in range(32)],
                  reads=[slab_b[s], hT_b[slot]], writes=[pacc_b[pi]])
            return pi

        def proj_tm(s, n, slot, tb):
            pi = nxt("pacc", 3)
            cx.op("pe", [(lambda j=j: nc.tensor.matmul(pacc[pi][:, 0:n], hT[slot][:, j, tb * 128:(tb + 1) * 128], slabs[s][:, j, 0:n],
                                                        start=(j == 0), stop=(j == 31))) for j in range(32)],
                  reads=[slab_b[s], hT_b[slot]], writes=[pacc_b[pi]])
            return pi

        def rope_evac(pi, M, R, cosn, sinn, prot, ti, dst):
            si = nxt("stg", NSTG)
            qi = nxt("qf", 2)
            cx.op("act", lambda: nc.scalar.copy(out=stg[si][0:M, :], in_=pacc[pi][0:M, :]),
                  reads=[pacc_b[pi]], writes=[stg_b[si]])
            cx.op("act", lambda: nc.scalar.copy(out=qf[qi][0:R, :], in_=pacc[pi][0:R, :]),
                  reads=[pacc_b[pi]], writes=[qf_b[qi]])
            pp = nxt("ppar", 2)
            cx.op("pe", lambda: nc.tensor.matmul(ppar[pp][0:R, :], prot[0:R, 0:R], qf[qi][0:R, :], start=True, stop=True),
                  reads=[qf_b[qi], k.b_const], writes=[ppar_b[pp]])
            cx.op("dve", lambda: nc.vector.tensor_tensor(out=rt[0][0:R, :], in0=qf[qi][0:R, :], in1=tabs[cosn][ti][0:R, :], op=ALU.mult),
                  reads=[qf_b[qi], tab_b[cosn][ti]], writes=[rt_b[0]])
            cx.op("dve", lambda: nc.vector.tensor_tensor(out=rt[1][0:R, :], in0=ppar[pp][0:R, :], in1=tabs[sinn][ti][0:R, :], op=ALU.mult),
                  reads=[ppar_b[pp], tab_b[sinn][ti]], writes=[rt_b[1]])
            cx.op("dve", lambda: nc.vector.tensor_tensor(out=stg[si][0:R, :], in0=rt[0][0:R, :], in1=rt[1][0:R, :], op=ALU.add),
                  reads=[rt_b[0], rt_b[1]], writes=[stg_b[si]])
            cx.dma("sp", dst, stg[si][0:M, :], stg_b[si], reads=[stg_b[si]])

        def plain_evac(pi, M, dst):
            si = nxt("stg", NSTG)
            cx.op("act", lambda: nc.scalar.copy(out=stg[si][0:M, :], in_=pacc[pi][0:M, :]),
                  reads=[pacc_b[pi]], writes=[stg_b[si]])
            cx.dma("sp", dst, stg[si][0:M, :], stg_b[si], reads=[stg_b[si]])

        def plain_evac_f32(pi, M, dst, func=None):
            fi = nxt("stgf", 2)
            if func is None:
                cx.op("act", lambda: nc.scalar.copy(out=stgf[fi][0:M, :], in_=pacc[pi][0:M, :]),
                      reads=[pacc_b[pi]], writes=[stgf_b[fi]])
            else:
                cx.op("act", lambda: nc.scalar.activation(out=stgf[fi][0:M, :], in_=pacc[pi][0:M, :], func=func),
                      reads=[pacc_b[pi]], writes=[stgf_b[fi]])
            cx.dma("sp", dst, stgf[fi][0:M, :], stgf_b[fi], reads=[stgf_b[fi]])

        if do_kv:
            for pair in range(0, n_all, 2):
                tiles = [t for t in (pair, pair + 1) if t < n_all]
                for li, t in enumerate(tiles):
                    build_hT(x_all, t * TT, li)
                    rope_tables(k, st, pos_all[0:1, t * TT:(t + 1) * TT], TT,
                                {n: tabs[n][li] for n in tabs}, {n: tab_b[n][li] for n in tabs}, tmp, tmp_b, posi, posi_b)
                for kind in range(6):
                    for gp in range(2):
                        c0 = 2048 + (kind * 4 + gp * 2) * 128
                        s = load_slab(c0, 256)
                        for li, t in enumerate(tiles):
                            if kind in (0, 2, 4):
                                dst = {0: scr["kcmpT"], 2: scr["kselT"], 4: scr["kwinT"]}[kind]
                                for gi in range(2):
                                    pi = proj_fm(s, gi * 128, 128, li)
                                    rope_evac(pi, 128, 32, "cosP", "sinP", k.prot_p, li, dst[gp * 2 + gi, :, t * TT:(t + 1) * TT])
                            elif kind == 1:
                                for gi in range(2):
                                    pi = proj_fm(s, gi * 128, 128, li)
                                    plain_evac(pi, 128, scr["vcmpT"][gp * 2 + gi, :, t * TT:(t + 1) * TT])
                            else:
                                dst = scr["vsel"] if kind == 3 else scr["vwin"]
                                for tb in range(4):
                                    pi = proj_tm(s, 256, li, tb)
                                    si = nxt("stg", NSTG)
                                    cx.op("act", lambda: nc.scalar.copy(out=stg[si][:, 0:256], in_=pacc[pi][:, 0:256]),
                                          reads=[pacc_b[pi]], writes=[stg_b[si]])
                                    r0 = t * TT + tb * 128
                                    cx.dma("sp", dst[r0:r0 + 128, gp * 256:(gp + 1) * 256], stg[si][:, 0:256], stg_b[si], reads=[stg_b[si]])
                for half in range(2):
                    s2 = load_slab(6704 + half * 256, 256)
                    for li, t in enumerate(tiles):
                        for ci in range(2):
                            pi = proj_fm(s2, ci * 128, 128, li)
                            plain_evac_f32(pi, 128, scr["ckvf"][half * 2 + ci, :, t * TT:(t + 1) * TT])
                s3 = load_slab(7216, 64)
                for li, t in enumerate(tiles):
                    pi = proj_fm(s3, 0, 64, li)
                    rope_evac(pi, 64, 64, "cosM", "sinM", k.prot_m, li, scr["kpeT"][:, t * TT:(t + 1) * TT])
        if do_q:
            for pair in range(0, n_my, 2):
                tiles = [t for t in (pair, pair + 1) if t < n_my]
                for li, t in enumerate(tiles):
                    build_hT(xq, t * TT, li)
                    rope_tables(k, st, pos_q[0:1, t * TT:(t + 1) * TT], TT,
                                {n: tabs[n][li] for n in tabs}, {n: tab_b[n][li] for n in tabs}, tmp, tmp_b, posi, posi_b)
                for hp in range(8):
                    s = load_slab(hp * 256, 256)
                    for li, t in enumerate(tiles):
                        for hi in range(2):
                            pi = proj_fm(s, hi * 128, 128, li)
                            rope_evac(pi, 128, 32, "cosP", "sinP", k.prot_p, li, scr["qT"][hp * 2 + hi, :, t * TT:(t + 1) * TT])
                s = load_slab(5120, 48)
                for li, t in enumerate(tiles):
                    for tb in range(4):
                        pi = proj_tm(s, 48, li, tb)
                        fi = nxt("stgf", 2)
                        cx.op("act", lambda: nc.scalar.activation(out=stgf[fi][:, 0:48], in_=pacc[pi][:, 0:48], func=AF.Sigmoid),
                              reads=[pacc_b[pi]], writes=[stgf_b[fi]])
                        r0 = t * TT + tb * 128
                        cx.dma("sp", scr["gtok"][r0:r0 + 128, :], stgf[fi][:, 0:48], stgf_b[fi], reads=[stgf_b[fi]])
                for sl in range(6):
                    s = load_slab(5168 + sl * 256, 256)
                    for li, t in enumerate(tiles):
                        for ci in range(2):
                            pi = proj_fm(s, ci * 128, 128, li)
                            plain_evac_f32(pi, 128, scr["cqf"][sl * 2 + ci, :, t * TT:(t + 1) * TT])
        cx.barrier()


class Attn:
    def __init__(self, k, st, pfx, extra_bank=False):
        cx = k.cx
        self.k = k
        self.pS = [ps(k, f"{pfx}_pS{i}", [128, 512], F32, st) for i in range(2)]
        self.pS_b = cx.bufs_n(pfx + "_pS", 2)
        self.pT = [ps(k, f"{pfx}_pT{i}", [128, 512], BF16, st) for i in range(2)]
        self.pT_b = cx.bufs_n(pfx + "_pT", 2)
        self.pObank = ps(k, f"{pfx}_pO", [128, 512], F32, st)
        self.pO = [self.pObank[:, i * 128:(i + 1) * 128] for i in range(2)]
        self.pO_b = cx.bufs_n(pfx + "_pO", 2)
        if extra_bank:
            self.pXbank = [ps(k, f"{pfx}_pX{i}", [128, 512], F32, st) for i in range(2)]
            self.pX = [self.pXbank[i][:, 0:64] for i in range(2)]
        else:
            self.pX = [None, None]
        self.pX_b = cx.bufs_n(pfx + "_pX", 2)
        self.Sm = [sb(k, f"{pfx}_Sm{i}", [128, 512], F32, st) for i in range(2)]
        self.Sm_b = cx.bufs_n(pfx + "_Sm", 2)
        self.P = [sb(k, f"{pfx}_P{i}", [128, 512], BF16, st) for i in range(2)]
        self.P_b = cx.bufs_n(pfx + "_P", 2)
        self.PT = [sb(k, f"{pfx}_PT{i}", [128, 512], BF16, st) for i in range(2)]
        self.PT_b = cx.bufs_n(pfx + "_PT", 2)
        self.ls = [sb(k, f"{pfx}_ls{i}", [128, 16], F32, st) for i in range(4)]
        self.ls_b = cx.bufs_n(pfx + "_ls", 4)
        self.rl = sb(k, pfx + "_rl", [128, 4], F32, st)
        self.rl_b = cx.buf(pfx + "_rl")
        self.jobs = []
        self.nrun = 0
        self.done_t = 0
        self.done_pv = 0

    def run(self, qk_fns, qk_reads, ntiles, width, masks, scale, v_fn, v_reads, finish, x_fn=None, x_reads=(), copy_eng="dve"):
        r = self.nrun
        self.nrun += 1
        for kt in range(ntiles):
            self.jobs.append(dict(run=r, kt=kt, nt=ntiles, width=width, qk=qk_fns(kt), qk_reads=qk_reads, masks=masks(kt), scale=scale,
                                  v_fn=v_fn, v_reads=v_reads, finish=finish, x_fn=x_fn, x_reads=list(x_reads), copy_eng=copy_eng))
            self._step()

    def _qk(self, i):
        k = self.k
        nc, cx = k.nc, k.cx
        jb = self.jobs[i]
        b = i % 2
        w = jb["width"]
        pS = self.pS[b]
        cx.op("pe", [(lambda f=f: f(pS[:, 0:w])) for f in jb["qk"]], reads=jb["qk_reads"], writes=[self.pS_b[b]])
        src, src_b = pS, self.pS_b[b]
        for (map_, mreads) in jb["masks"]:
            o_ap, i_ap = self.Sm[b][:, 0:w], src[:, 0:w]
            if len(map_.shape) == 3:
                o_ap = o_ap.rearrange("p (a b) -> p a b", b=map_.shape[2])
                i_ap = i_ap.rearrange("p (a b) -> p a b", b=map_.shape[2])
            cx.op("dve", lambda: nc.vector.tensor_tensor(out=o_ap, in0=i_ap, in1=map_, op=ALU.add),
                  reads=[src_b] + mreads, writes=[self.Sm_b[b]])
            src, src_b = self.Sm[b], self.Sm_b[b]
        li = jb["run"] % 4
        kt = jb["kt"]
        cx.op("act", lambda: nc.scalar.activation(out=self.P[b][:, 0:w], in_=src[:, 0:w], func=AF.Exp, scale=jb["scale"],
                                                  accum_out=self.ls[li][:, kt:kt + 1]),
              reads=[src_b], writes=[self.P_b[b], self.ls_b[li]])

    def _t(self, i):
        k = self.k
        nc, cx = k.nc, k.cx
        jb = self.jobs[i]
        b = i % 2
        w = jb["width"]
        nkb = w // 128
        cx.op("pe", [(lambda kb=kb: nc.tensor.transpose(self.pT[b][:, kb * 128:(kb + 1) * 128], self.P[b][:, kb * 128:(kb + 1) * 128], k.ident_b[:]))
                     for kb in range(nkb)], reads=[self.P_b[b], k.b_const], writes=[self.pT_b[b]])
        if jb["copy_eng"] == "act":
            cx.op("act", lambda: nc.scalar.copy(out=self.PT[b][:, 0:w], in_=self.pT[b][:, 0:w]), reads=[self.pT_b[b]], writes=[self.PT_b[b]])
        else:
            cx.op("dve", lambda: nc.vector.tensor_copy(out=self.PT[b][:, 0:w], in_=self.pT[b][:, 0:w]), reads=[self.pT_b[b]], writes=[self.PT_b[b]])

    def _pv(self, i):
        k = self.k
        nc, cx = k.nc, k.cx
        jb = self.jobs[i]
        b = i % 2
        w = jb["width"]
        nkb = w // 128
        oi = jb["run"] % 2
        kt, nt = jb["kt"], jb["nt"]
        cx.op("pe", [(lambda kb=kb: nc.tensor.matmul(self.pO[oi], self.PT[b][:, kb * 128:(kb + 1) * 128], jb["v_fn"](kt, kb),
                                                      start=(kt == 0 and kb == 0), stop=(kt == nt - 1 and kb == nkb - 1)))
                     for kb in range(nkb)], reads=[self.PT_b[b]] + jb["v_reads"], writes=[self.pO_b[oi]])
        if jb["x_fn"] is not None and os.environ.get("P4_X", "1") == "1":
            cx.op("pe", [(lambda kb=kb: nc.tensor.matmul(self.pX[oi], self.PT[b][:, kb * 128:(kb + 1) * 128], jb["x_fn"](kb),
                                                          start=(kb == 0), stop=(kb == nkb - 1)))
                         for kb in range(nkb)], reads=[self.PT_b[b]] + jb["x_reads"], writes=[self.pX_b[oi]])
        if kt == nt - 1:
            jb["finish"](oi, jb["run"] % 4)
        self.jobs[i] = None

    def _step(self):
        i = len(self.jobs) - 1
        self._qk(i)
        if i - 1 >= 0:
            self._t(i - 1)
            self.done_t = i
        if i - 2 >= 0:
            self._pv(i - 2)
            self.done_pv = i - 1

    def drain(self):
        n = len(self.jobs)
        for i in range(self.done_t, n):
            self._t(i)
        for i in range(self.done_pv, n):
            self._pv(i)
        self.jobs = []
        self.done_t = 0
        self.done_pv = 0

    def rsum(self, li, ntiles, col):
        k = self.k
        nc, cx = k.nc, k.cx
        cx.op("dve", lambda: nc.vector.tensor_reduce(out=self.rl[:, col:col + 1], in_=self.ls[li][:, 0:ntiles], axis=AX.X, op=ALU.add),
              reads=[self.ls_b[li]], writes=[self.rl_b])
        cx.op("dve", lambda: nc.vector.tensor_scalar(out=self.rl[:, col:col + 1], in0=self.rl[:, col:col + 1], scalar1=1e-30, scalar2=None, op0=ALU.max),
              reads=[self.rl_b], writes=[self.rl_b])
        cx.op("dve", lambda: nc.vector.reciprocal(out=self.rl[:, col:col + 1], in_=self.rl[:, col:col + 1]),
              reads=[self.rl_b], writes=[self.rl_b])


class AttnSync:
    def __init__(self, k, st, pfx):
        cx = k.cx
        self.k = k
        self.pS = [ps(k, f"{pfx}_pS{i}", [128, 512], F32, st) for i in range(2)]
        self.pS_b = cx.bufs_n(pfx + "_pS", 2)
        self.pT = [ps(k, f"{pfx}_pT{i}", [128, 512], BF16, st) for i in range(2)]
        self.pT_b = cx.bufs_n(pfx + "_pT", 2)
        self.pObank = ps(k, f"{pfx}_pO", [128, 512], F32, st)
        self.pO = [self.pObank[:, i * 128:(i + 1) * 128] for i in range(2)]
        self.pO_b = cx.bufs_n(pfx + "_pO", 2)
        self.Sm = [sb(k, f"{pfx}_Sm{i}", [128, 512], F32, st) for i in range(2)]
        self.Sm_b = cx.bufs_n(pfx + "_Sm", 2)
        self.P = [sb(k, f"{pfx}_P{i}", [128, 512], BF16, st) for i in range(2)]
        self.P_b = cx.bufs_n(pfx + "_P", 2)
        self.PT = [sb(k, f"{pfx}_PT{i}", [128, 512], BF16, st) for i in range(2)]
        self.PT_b = cx.bufs_n(pfx + "_PT", 2)
        self.ls = [sb(k, f"{pfx}_ls{i}", [128, 16], F32, st) for i in range(2)]
        self.ls_b = cx.bufs_n(pfx + "_ls", 2)
        self.rl = sb(k, pfx + "_rl", [128, 4], F32, st)
        self.rl_b = cx.buf(pfx + "_rl")
        self.c = {"S": 0, "T": 0, "O": 0, "Sm": 0, "P": 0, "PT": 0, "ls": 0}

    def nx(self, n, m=2):
        i = self.c[n] % m
        self.c[n] += 1
        return i

    def run(self, qk_fns, qk_reads, ntiles, width, masks, scale, v_fn, v_reads):
        k = self.k
        nc, cx = k.nc, k.cx
        oi = self.nx("O")
        li = self.nx("ls")
        nkb = width // 128
        for kt in range(ntiles):
            si = self.nx("S")
            pS = self.pS[si]
            cx.op("pe", [(lambda f=f: f(pS[:, 0:width])) for f in qk_fns(kt)], reads=qk_reads, writes=[self.pS_b[si]])
            src, src_b = pS, self.pS_b[si]
            ml = masks(kt)
            if ml:
                mi = self.nx("Sm")
                for (map_, mreads) in ml:
                    o_ap, i_ap = self.Sm[mi][:, 0:width], src[:, 0:width]
                    if len(map_.shape) == 3:
                        o_ap = o_ap.rearrange("p (a b) -> p a b", b=map_.shape[2])
                        i_ap = i_ap.rearrange("p (a b) -> p a b", b=map_.shape[2])
                    cx.op("dve", lambda: nc.vector.tensor_tensor(out=o_ap, in0=i_ap, in1=map_, op=ALU.add),
                          reads=[src_b] + mreads, writes=[self.Sm_b[mi]])
                    src, src_b = self.Sm[mi], self.Sm_b[mi]
            pi = self.nx("P")
            cx.op("act", lambda: nc.scalar.activation(out=self.P[pi][:, 0:width], in_=src[:, 0:width], func=AF.Exp, scale=scale,
                                                      accum_out=self.ls[li][:, kt:kt + 1]),
                  reads=[src_b], writes=[self.P_b[pi], self.ls_b[li]])
            ti = self.nx("T")
            cx.op("pe", [(lambda kb=kb: nc.tensor.transpose(self.pT[ti][:, kb * 128:(kb + 1) * 128], self.P[pi][:, kb * 128:(kb + 1) * 128], k.ident_b[:]))
                         for kb in range(nkb)], reads=[self.P_b[pi], k.b_const], writes=[self.pT_b[ti]])
            pti = self.nx("PT")
            cx.op("dve", lambda: nc.vector.tensor_copy(out=self.PT[pti][:, 0:width], in_=self.pT[ti][:, 0:width]),
                  reads=[self.pT_b[ti]], writes=[self.PT_b[pti]])
            cx.op("pe", [(lambda kb=kb: nc.tensor.matmul(self.pO[oi], self.PT[pti][:, kb * 128:(kb + 1) * 128], v_fn(kt, kb),
                                                          start=(kt == 0 and kb == 0), stop=(kt == ntiles - 1 and kb == nkb - 1)))
                         for kb in range(nkb)], reads=[self.PT_b[pti]] + v_reads, writes=[self.pO_b[oi]])
        return oi, li

    def rsum(self, li, ntiles, col):
        k = self.k
        nc, cx = k.nc, k.cx
        cx.op("dve", lambda: nc.vector.tensor_reduce(out=self.rl[:, col:col + 1], in_=self.ls[li][:, 0:ntiles], axis=AX.X, op=ALU.add),
              reads=[self.ls_b[li]], writes=[self.rl_b])
        cx.op("dve", lambda: nc.vector.tensor_scalar(out=self.rl[:, col:col + 1], in0=self.rl[:, col:col + 1], scalar1=1e-30, scalar2=None, op0=ALU.max),
              reads=[self.rl_b], writes=[self.rl_b])
        cx.op("dve", lambda: nc.vector.reciprocal(out=self.rl[:, col:col + 1], in_=self.rl[:, col:col + 1]),
              reads=[self.rl_b], writes=[self.rl_b])


class OutT:
    def __init__(self, k, st, pfx, nstg=2):
        cx = k.cx
        self.k = k
        self.ob = [sb(k, f"{pfx}_ob{i}", [128, 128], BF16, st) for i in range(2)]
        self.ob_b = cx.bufs_n(pfx + "_ob", 2)
        self.ptbank = ps(k, f"{pfx}_opt", [128, 512], BF16, st)
        self.pt = [self.ptbank[:, i * 128:(i + 1) * 128] for i in range(2)]
        self.pt_b = cx.bufs_n(pfx + "_opt", 2)
        self.stg = [sb(k, f"{pfx}_ostg{i}", [128, 512], BF16, st) for i in range(nstg)]
        self.stg_b = cx.bufs_n(pfx + "_ostg", nstg)
        self.n = 0

    def put(self, src_ap, src_reads, qb, si):
        k = self.k
        nc, cx = k.nc, k.cx
        i = self.n % 2
        self.n += 1
        cx.op("act", lambda: nc.scalar.copy(out=self.ob[i][:], in_=src_ap), reads=src_reads, writes=[self.ob_b[i]])
        cx.op("pe", lambda: nc.tensor.transpose(self.pt[i], self.ob[i][:], k.ident_b[:]), reads=[self.ob_b[i], k.b_const], writes=[self.pt_b[i]])
        cx.op("act", lambda: nc.scalar.copy(out=self.stg[si][:, qb * 128:(qb + 1) * 128], in_=self.pt[i]),
              reads=[self.pt_b[i]], writes=[self.stg_b[si]])

    def flush(self, si, dst):
        self.k.cx.dma("sp", dst, self.stg[si][:], self.stg_b[si], reads=[self.stg_b[si]])


def emit_p2(k, scr, w_uq, w_ukv, qnT_d, kvnT_d, pos_q, mA_d, mB_d, n_all=NT_ALL, n_my=NT_MY, heads=range(16)):
    nc, cx = k.nc, k.cx
    SC = 192.0 ** -0.5
    with ExitStack() as st:
        ckvn = sb(k, "p2_ckvn", [128, 4, n_all * TT], BF16, st)
        ckvn_b = cx.buf("p2_ckvn")
        kpe = sb(k, "p2_kpe", [64, n_all * TT], BF16, st)
        kpe_b = cx.buf("p2_kpe")
        qnT = sb(k, "p2_qnT", [128, 12], F32, st)
        kvnT = sb(k, "p2_kvnT", [128, 4], F32, st)
        mA = sb(k, "p2_mA", [128, 4, 512], F32, st)
        mB = sb(k, "p2_mB", [128, 4, 512], F32, st)
        b_c = cx.buf("p2_c")
        for dst, src in ((qnT, qnT_d), (kvnT, kvnT_d), (mA, mA_d), (mB, mB_d)):
            cx.dma("sp", dst[:], src, b_c, writes=[b_c])
        cx.dma("sp", kpe[:], scr["kpeT"][:, 0:n_all * TT], kpe_b, writes=[kpe_b])
        with ExitStack() as st2:
            ld = [sb(k, f"p2_ld{i}", [128, 12, TT], F32, st2) for i in range(2)]
            ld_b = cx.bufs_n("p2_ld", 2)
            sq = [sb(k, f"p2_sq{i}", [128, TT], BF16, st2) for i in range(2)]
            sq_b = cx.bufs_n("p2_sq", 2)
            rs = sb(k, "p2_rs", [128, TT], F32, st2)
            rs_b = cx.buf("p2_rs")
            cqn = [sb(k, f"p2_cqn{i}", [128, 12, TT], BF16, st2) for i in range(2)]
            cqn_b = cx.bufs_n("p2_cqn", 2)
            pss = ps(k, "p2_pss", [128, 512], F32, st2)
            pss_b = cx.buf("p2_pss")
            wq = [sb(k, f"p2_wq{i}", [128, 12, 192], BF16, st2) for i in range(2)]
            wq_b = cx.bufs_n("p2_wq", 2)
            pq = [ps(k, f"p2_pq{i}", [128, 512], F32, st2) for i in range(2)]
            pq_b = cx.bufs_n("p2_pq", 2)
            pp = [ps(k, f"p2_pp{i}", [64, 512], F32, st2) for i in range(2)]
            pp_b = cx.bufs_n("p2_pp", 2)
            pqe = [ps(k, f"p2_pqe{i}", [64, 512], F32, st2) for i in range(2)]
            pqe_b = cx.bufs_n("p2_pqe", 2)
            stg = [sb(k, f"p2_stg{i}", [128, TT], BF16, st2) for i in range(4)]
            stg_b = cx.bufs_n("p2_stg", 4)
            qf = [sb(k, f"p2_qf{i}", [64, TT], F32, st2) for i in range(2)]
            qf_b = cx.bufs_n("p2_qf", 2)
            rt = [sb(k, f"p2_rt{i}", [64, TT], F32, st2) for i in range(2)]
            rt_b = cx.bufs_n("p2_rt", 2)
            tabs = {n: sb(k, f"p2_{n}", [64, TT], F32, st2) for n in ("cosP", "sinP", "cosM", "sinM")}
            tab_b = {n: cx.buf("p2_" + n) for n in tabs}
            tmp = [sb(k, f"p2_tmp{i}", [64, TT], F32, st2) for i in range(4)]
            tmp_b = cx.bufs_n("p2_tmp", 4)
            posi = sb(k, "p2_posi", [64, TT], I32, st2)
            posi_b = cx.buf("p2_posi")
            k.p1_ti = sb(k, "p2_ti", [64, TT], I32, st2)
            k.p1_ti_b = cx.buf("p2_ti")
            cn = {"ld": 0, "sq": 0, "stg": 0}

            def rstd(src_d, nch, tok0, width, li):
                cx.dma("sp", ld[li][:, 0:nch, :], src_d[:, :, tok0:tok0 + TT].rearrange("c p t -> p c t"), ld_b[li], writes=[ld_b[li]])
                for ci in range(nch):
                    qi = cn["sq"] % 2
                    cn["sq"] += 1
                    cx.op("act", lambda: nc.scalar.activation(out=sq[qi][:], in_=ld[li][:, ci, :], func=AF.Square),
                          reads=[ld_b[li]], writes=[sq_b[qi]])
                    cx.op("pe", lambda: nc.tensor.matmul(pss[:], k.ones_bf[:], sq[qi][:], start=(ci == 0), stop=(ci == nch - 1)),
                          reads=[sq_b[qi], k.b_const], writes=[pss_b])
                cx.op("act", lambda: nc.scalar.activation(out=rs[:], in_=pss[:], func=AF.Sqrt, scale=1.0 / width, bias=k.eps6[:]),
                      reads=[pss_b, k.b_const], writes=[rs_b])
                cx.op("dve", lambda: nc.vector.reciprocal(out=rs[:], in_=rs[:]), reads=[rs_b], writes=[rs_b])

            for t in range(n_all):
                li = cn["ld"] % 2
                cn["ld"] += 1
                rstd(scr["ckvf"], 4, t * TT, 512.0, li)
                for ci in range(4):
                    cx.op("dve", lambda: nc.vector.scalar_tensor_tensor(out=ckvn[:, ci, t * TT:(t + 1) * TT], in0=ld[li][:, ci, :], scalar=kvnT[:, ci:ci + 1],
                                                                       in1=rs[:], op0=ALU.mult, op1=ALU.mult),
                          reads=[ld_b[li], rs_b, b_c], writes=[ckvn_b])
            for m in range(n_my):
                li = cn["ld"] % 2
                cn["ld"] += 1
                rstd(scr["cqf"], 12, m * TT, 1536.0, li)
                ci_ = m % 2
                for ci in range(12):
                    cx.op("dve", lambda: nc.vector.scalar_tensor_tensor(out=cqn[ci_][:, ci, :], in0=ld[li][:, ci, :], scalar=qnT[:, ci:ci + 1],
                                                                       in1=rs[:], op0=ALU.mult, op1=ALU.mult),
                          reads=[ld_b[li], rs_b, b_c], writes=[cqn_b[ci_]])
                rope_tables(k, st2, pos_q[0:1, m * TT:(m + 1) * TT], TT, tabs, tab_b, tmp, tmp_b, posi, posi_b)
                for h in heads:
                    wi = h % 2
                    cx.dma("pool", wq[wi][:], w_uq[h], wq_b[wi], writes=[wq_b[wi]])
                    qi = h % 2
                    cx.op("pe", [(lambda j=j: nc.tensor.matmul(pq[qi][:], wq[wi][:, j, 0:128], cqn[ci_][:, j, :], start=(j == 0), stop=(j == 11)))
                                 for j in range(12)], reads=[wq_b[wi], cqn_b[ci_]], writes=[pq_b[qi]])
                    si = cn["stg"] % 4
                    cn["stg"] += 1
                    cx.op("act", lambda: nc.scalar.copy(out=stg[si][:], in_=pq[qi][:]), reads=[pq_b[qi]], writes=[stg_b[si]])
                    cx.dma("sp", scr["qmnT"][h, :, m * TT:(m + 1) * TT], stg[si][:], stg_b[si], reads=[stg_b[si]])
                    cx.op("pe", [(lambda j=j: nc.tensor.matmul(pqe[qi][:], wq[wi][:, j, 128:192], cqn[ci_][:, j, :], start=(j == 0), stop=(j == 11)))
                                 for j in range(12)], reads=[wq_b[wi], cqn_b[ci_]], writes=[pqe_b[qi]])
                    cx.op("act", lambda: nc.scalar.copy(out=qf[qi][:], in_=pqe[qi][:]), reads=[pqe_b[qi]], writes=[qf_b[qi]])
                    cx.op("pe", lambda: nc.tensor.matmul(pp[qi][:], k.prot_m[0:64, 0:64], qf[qi][:], start=True, stop=True),
                          reads=[qf_b[qi], k.b_const], writes=[pp_b[qi]])
                    cx.op("dve", lambda: nc.vector.tensor_tensor(out=rt[0][:], in0=qf[qi][:], in1=tabs["cosM"][:], op=ALU.mult),
                          reads=[qf_b[qi], tab_b["cosM"]], writes=[rt_b[0]])
                    cx.op("dve", lambda: nc.vector.tensor_tensor(out=rt[1][:], in0=pp[qi][:], in1=tabs["sinM"][:], op=ALU.mult),
                          reads=[pp_b[qi], tab_b["sinM"]], writes=[rt_b[1]])
                    si = cn["stg"] % 4
                    cn["stg"] += 1
                    cx.op("dve", lambda: nc.vector.tensor_tensor(out=stg[si][0:64, :], in0=rt[0][:], in1=rt[1][:], op=ALU.add),
                          reads=[rt_b[0], rt_b[1]], writes=[stg_b[si]])
                    cx.dma("sp", scr["qmpT"][h, :, m * TT:(m + 1) * TT], stg[si][0:64, :], stg_b[si], reads=[stg_b[si]])
            cx.barrier()
        with ExitStack() as st3:
            at = Attn(k, st3, "p2a")
            ot = OutT(k, st3, "p2o")
            wkv = [sb(k, f"p2_wkv{i}", [128, 4, 256], BF16, st3) for i in range(2)]
            wkv_b = cx.bufs_n("p2_wkv", 2)
            KT = [sb(k, f"p2_KT{i}", [128, n_all * TT], BF16, st3) for i in range(2)]
            KT_b = cx.bufs_n("p2_KT", 2)
            V = [sb(k, f"p2_V{i}", [128, n_all * 4, 128], BF16, st3) for i in range(2)]
            V_b = cx.bufs_n("p2_V", 2)
            qn = [sb(k, f"p2_qn{i}", [128, n_my * TT], BF16, st3) for i in range(2)]
            qn_b = cx.bufs_n("p2_qn", 2)
            qp = [sb(k, f"p2_qp{i}", [64, n_my * TT], BF16, st3) for i in range(2)]
            qp_b = cx.bufs_n("p2_qp", 2)
            pk = [ps(k, f"p2_pk{i}", [128, 512], F32, st3) for i in range(2)]
            pk_b = cx.bufs_n("p2_pk", 2)
            on = sb(k, "p2_on", [128, 128], F32, st3)
            on_b = cx.buf("p2_on")
            for hi_, h in enumerate(heads):
                b = hi_ % 2
                cx.dma("pool", wkv[b][:], w_ukv[h], wkv_b[b], writes=[wkv_b[b]])
                cx.dma("sp", qn[b][:], scr["qmnT"][h, :, 0:n_my * TT], qn_b[b], writes=[qn_b[b]])
                cx.dma("sp", qp[b][:], scr["qmpT"][h, :, 0:n_my * TT], qp_b[b], writes=[qp_b[b]])
                for t in range(n_all):
                    cx.op("pe", [(lambda j=j: nc.tensor.matmul(pk[0][:], wkv[b][:, j, 0:128], ckvn[:, j, t * TT:(t + 1) * TT], start=(j == 0), stop=(j == 3)))
                                 for j in range(4)], reads=[wkv_b[b], ckvn_b], writes=[pk_b[0]])
                    cx.op("act", lambda: nc.scalar.copy(out=KT[b][:, t * TT:(t + 1) * TT], in_=pk[0][:]), reads=[pk_b[0]], writes=[KT_b[b]])
                    fns = []
                    for tb in range(4):
                        for j in range(4):
                            fns.append(lambda j=j, tb=tb: nc.tensor.matmul(pk[1][:, tb * 128:(tb + 1) * 128], ckvn[:, j, t * TT + tb * 128:t * TT + (tb + 1) * 128],
                                                                            wkv[b][:, j, 128:256], start=(j == 0), stop=(j == 3)))
                    cx.op("pe", fns, reads=[wkv_b[b], ckvn_b], writes=[pk_b[1]])
                    cx.op("dve", lambda: nc.vector.tensor_copy(out=V[b][:, t * 4:(t + 1) * 4, :], in_=pk[1][:].rearrange("p (a d) -> p a d", d=128)),
                          reads=[pk_b[1]], writes=[V_b[b]])
                for m in range(n_my):
                    si = (hi_ * n_my + m) % 2
                    for qb in range(4):
                        q0 = m * TT + qb * 128
                        nt = min(2 * m + 2, n_all)

                        def mk_qk(b=b, q0=q0):
                            def qk(kt):
                                return [lambda o: nc.tensor.matmul(o, qn[b][:, q0:q0 + 128], KT[b][:, kt * TT:(kt + 1) * TT], start=True, stop=False),
                                        lambda o: nc.tensor.matmul(o, qp[b][0:64, q0:q0 + 128], kpe[0:64, kt * TT:(kt + 1) * TT], start=False, stop=True)]
                            return qk

                        def mk_masks(m=m, qb=qb):
                            def masks(kt):
                                if kt == 2 * m:
                                    return [(mA[:, qb, :], [b_c])]
                                if kt == 2 * m + 1:
                                    return [(mB[:, qb, :], [b_c])]
                                return []
                            return masks

                        def mk_fin(nt=nt, qb=qb, si=si, h=h, m=m):
                            def fin(oi, li):
                                at.rsum(li, nt, 0)
                                cx.op("dve", lambda: nc.vector.tensor_scalar(out=on[:], in0=at.pO[oi], scalar1=at.rl[:, 0:1], scalar2=None, op0=ALU.mult),
                                      reads=[at.pO_b[oi], at.rl_b], writes=[on_b])
                                ot.put(on[:], [on_b], qb, si)
                                if qb == 3:
                                    ot.flush(si, scr["oT"][16 + h, :, m * TT:(m + 1) * TT])
                            return fin

                        at.run(mk_qk(), [qn_b[b], qp_b[b], KT_b[b], kpe_b], nt, 512, mk_masks(), SC,
                               (lambda b=b: (lambda kt, kb: V[b][:, kt * 4 + kb, :]))(), [V_b[b]], mk_fin())
            at.drain()
        cx.barrier()


def emit_p3(k, scr, k1_d, k2_d, v1_d, v2_d, posk_d, posv_d):
    nc, cx = k.nc, k.cx
    with ExitStack() as st:
        w1 = sb(k, "p3_w1", [128, 32, 256], BF16, st)
        w2 = sb(k, "p3_w2", [128, 2, 128], BF16, st)
        posT = sb(k, "p3_posT", [128, 32], F32, st)
        b_w = cx.buf("p3_w")
        src = [sb(k, f"p3_src{i}", [128, S], BF16, st) for i in range(2)]
        src_b = cx.bufs_n("p3_src", 2)
        kp = sb(k, "p3_kp", [128, 32, 256], BF16, st)
        kp_b = cx.buf("p3_kp")
        ppre = [ps(k, f"p3_ppre{i}", [128, 256], F32, st) for i in range(2)]
        ppre_b = cx.bufs_n("p3_ppre", 2)
        pout = ps(k, "p3_pout", [128, 256], F32, st)
        pout_b = cx.buf("p3_pout")
        xs = sb(k, "p3_xs", [128, 256], F32, st)
        xs_b = cx.buf("p3_xs")
        t1 = sb(k, "p3_t1", [128, 256], F32, st)
        t1_b = cx.buf("p3_t1")
        gl = sb(k, "p3_gl", [128, 2, 256], BF16, st)
        gl_b = cx.buf("p3_gl")
        cx.op("dve", lambda: nc.vector.memset(kp[:], 0.0), writes=[kp_b])
        cx.op("dve", lambda: nc.vector.memset(k.kcT[:], 0.0), writes=[k.kcv_b])
        cx.op("dve", lambda: nc.vector.memset(k.vc[:], 0.0), writes=[k.kcv_b])
        n_src = 0
        for which in range(2):
            w1d, w2d, posd, srcd = ((k1_d, k2_d, posk_d, scr["kcmpT"]), (v1_d, v2_d, posv_d, scr["vcmpT"]))[which]
            cx.dma("pool", w1[:], w1d, b_w, writes=[b_w])
            cx.dma("pool", w2[:], w2d, b_w, writes=[b_w])
            cx.dma("sp", posT[:], posd, b_w, writes=[b_w])
            for g in range(4):
                si = n_src % 2
                n_src += 1
                cx.dma("sp", src[si][:], srcd[g, :, :], src_b[si], writes=[src_b[si]])
                for l in range(32):
                    cx.op("dve", lambda: nc.vector.tensor_scalar(out=kp[:, l, 0:255], in0=src[si][:, l:l + 16 * 254 + 1:16], scalar1=posT[:, l:l + 1],
                                                                 scalar2=None, op0=ALU.add), reads=[src_b[si], b_w], writes=[kp_b])
                for hc in range(2):
                    pi = hc
                    cx.op("pe", [(lambda l=l: nc.tensor.matmul(ppre[pi][:], w1[:, l, hc * 128:(hc + 1) * 128], kp[:, l, :], start=(l == 0), stop=(l == 31)))
                                 for l in range(32)], reads=[b_w, kp_b], writes=[ppre_b[pi]])
                    cx.op("act", lambda: nc.scalar.copy(out=xs[:], in_=ppre[pi][:]), reads=[ppre_b[pi]], writes=[xs_b])
                    cx.op("dve", lambda: nc.vector.tensor_tensor(out=t1[:], in0=xs[:], in1=xs[:], op=ALU.mult), reads=[xs_b], writes=[t1_b])
                    cx.op("dve", lambda: nc.vector.tensor_scalar(out=t1[:], in0=t1[:], scalar1=0.044715, scalar2=1.0, op0=ALU.mult, op1=ALU.add),
                          reads=[t1_b], writes=[t1_b])
                    cx.op("dve", lambda: nc.vector.tensor_tensor(out=t1[:], in0=t1[:], in1=xs[:], op=ALU.mult), reads=[t1_b, xs_b], writes=[t1_b])
                    cx.op("act", lambda: nc.scalar.activation(out=t1[:], in_=t1[:], func=AF.Tanh, scale=0.7978845608028654), reads=[t1_b], writes=[t1_b])
                    cx.op("dve", lambda: nc.vector.tensor_scalar(out=t1[:], in0=t1[:], scalar1=0.5, scalar2=0.5, op0=ALU.mult, op1=ALU.add),
                          reads=[t1_b], writes=[t1_b])
                    cx.op("dve", lambda: nc.vector.tensor_tensor(out=gl[:, hc, :], in0=t1[:], in1=xs[:], op=ALU.mult), reads=[t1_b, xs_b], writes=[gl_b])
                if which == 0:
                    cx.op("pe", [(lambda hc=hc: nc.tensor.matmul(pout[:], w2[:, hc, :], gl[:, hc, :], start=(hc == 0), stop=(hc == 1))) for hc in range(2)],
                          reads=[b_w, gl_b], writes=[pout_b])
                    cx.op("act", lambda: nc.scalar.copy(out=k.kcT[:, g, 0:256], in_=pout[:]), reads=[pout_b], writes=[k.kcv_b])
                else:
                    for c in range(2):
                        cx.op("pe", [(lambda hc=hc: nc.tensor.matmul(pout[:, 0:128], gl[:, hc, c * 128:(c + 1) * 128], w2[:, hc, :], start=(hc == 0), stop=(hc == 1)))
                                     for hc in range(2)], reads=[b_w, gl_b], writes=[pout_b])
                        cx.op("act", lambda: nc.scalar.copy(out=k.vc[:, g, c, :], in_=pout[:, 0:128]), reads=[pout_b], writes=[k.kcv_b])
        cx.barrier()


def emit_p4(k, scr, cst, n_all=NT_ALL, n_my=NT_MY, groups=range(4), mode="cws"):
    nc, cx = k.nc, k.cx
    SC = 128.0 ** -0.5
    with ExitStack() as st:
        at = Attn(k, st, "p4a" + mode, extra_bank=True)
        ot = OutT(k, st, "p4o" + mode, 4)
        mA = sb(k, f"p4{mode}_mA", [128, 4, 512], F32, st)
        mB = sb(k, f"p4{mode}_mB", [128, 4, 512], F32, st)
        mW = sb(k, f"p4{mode}_mW", [128, 3, 4, 512], F32, st)
        mC = sb(k, f"p4{mode}_mC", [128, 4, 512], F32, st)
        vS = sb(k, f"p4{mode}_vS", [128, 4, 64], F32, st)
        aS = sb(k, f"p4{mode}_aS", [128, 4, 64], F32, st)
        ovl = sb(k, f"p4{mode}_ovl", [128, 4, 64], BF16, st)
        ovf = sb(k, f"p4{mode}_ovf", [128, 4, 64], F32, st)
        b_c = cx.buf(f"p4{mode}_c")
        b_cm = cx.buf(f"p4{mode}_cm")
        cx.dma("sp", mA[:], cst["mA"], b_c, writes=[b_c])
        cx.dma("sp", mB[:], cst["mB"], b_c, writes=[b_c])
        cx.dma("sp", mW[:], cst["mW"].rearrange("w p q k -> p w q k"), b_c, writes=[b_c])
        cx.dma("sp", ovf[:], cst["ovl"], b_c, writes=[b_c])
        cx.op("dve", lambda: nc.vector.tensor_copy(out=ovl[:], in_=ovf[:]), reads=[b_c], writes=[b_c])
        ksel = [sb(k, f"p4{mode}_ksel{i}", [128, n_all * TT], BF16, st) for i in range(1)]
        kwin = [sb(k, f"p4{mode}_kwin{i}", [128, n_all * TT], BF16, st) for i in range(1)]
        vsel = [sb(k, f"p4{mode}_vsel{i}", [128, n_all * 4, 128], BF16, st) for i in range(1)]
        vwin = [sb(k, f"p4{mode}_vwin{i}", [128, n_all * 4, 128], BF16, st) for i in range(1)]
        kv_b = cx.buf(f"p4{mode}_kv")
        qg = [sb(k, f"p4{mode}_q{i}", [128, 4, TT], BF16, st) for i in range(2)]
        qg_b = cx.bufs_n(f"p4{mode}_q", 2)
        gt = [sb(k, f"p4{mode}_gt{i}", [128, 4, 48], F32, st) for i in range(2)]
        gt_b = cx.bufs_n(f"p4{mode}_gt", 2)
        oacc_t = [sb(k, f"p4{mode}_oacc{i}", [128, 4, 128], F32, st) for i in range(2)]
        oacc_bt = cx.bufs_n(f"p4{mode}_oacc", 2)
        otmp = sb(k, f"p4{mode}_otmp", [128, 2, 128], F32, st)
        otmp_b = cx.bufs_n(f"p4{mode}_otmp", 2)
        imp = sb(k, f"p4{mode}_imp", [128, 64], F32, st)
        imp_b = cx.buf(f"p4{mode}_imp")
        wk = sb(k, f"p4{mode}_wk", [128, 64], F32, st)
        wk_b = cx.buf(f"p4{mode}_wk")
        m16 = sb(k, f"p4{mode}_m16", [128, 16], F32, st)
        m16_b = cx.buf(f"p4{mode}_m16")
        sneg_t = [sb(k, f"p4{mode}_sneg{i}", [128, 64], F32, st) for i in range(2)]
        sneg_bt = cx.bufs_n(f"p4{mode}_sneg", 2)
        nqb = 0
        gm = sb(k, f"p4{mode}_gm", [128, 4], F32, st)
        gm_b = cx.buf(f"p4{mode}_gm")
        vsel_v = scr["vsel"].rearrange("(b p) c -> p b c", p=128)
        vwin_v = scr["vwin"].rearrange("(b p) c -> p b c", p=128)
        it = 0
        for g in groups:
            nb = n_all * 4
            at.drain()
            cx.dma("sp", ksel[0][:], scr["kselT"][g, :, 0:n_all * TT], kv_b, writes=[kv_b])
            cx.dma("sp", kwin[0][:], scr["kwinT"][g, :, 0:n_all * TT], kv_b, writes=[kv_b])
            cx.dma("sp", vsel[0][:], vsel_v[:, 0:nb, g * 128:(g + 1) * 128], kv_b, writes=[kv_b])
            cx.dma("sp", vwin[0][:], vwin_v[:, 0:nb, g * 128:(g + 1) * 128], kv_b, writes=[kv_b])
            for m in range(n_my):
                at.drain()
                qi = it % 2
                it += 1
                cx.dma("sp", qg[qi][:], scr["qT"][g * 4:(g + 1) * 4, :, m * TT:(m + 1) * TT].rearrange("h p t -> p h t"), qg_b[qi], writes=[qg_b[qi]])
                cx.dma("sp", gt[qi][:], scr["gtok"][m * TT:(m + 1) * TT, :].rearrange("(b p) c -> p b c", p=128), gt_b[qi], writes=[gt_b[qi]])
                cx.dma("sp", mC[:], cst["mC"][m], b_cm, writes=[b_cm])
                cx.dma("sp", vS[:], cst["vS"][m], b_cm, writes=[b_cm])
                cx.dma("sp", aS[:], cst["aS"][m], b_cm, writes=[b_cm])
                for qb in range(4):
                    Q = slice(qb * 128, (qb + 1) * 128)
                    oacc, oacc_b = oacc_t[nqb % 2], oacc_bt[nqb % 2]
                    sneg, sneg_b = sneg_t[nqb % 2], sneg_bt[nqb % 2]
                    nqb += 1
                    r0_ = m * TT + qb * 128
                    if "c" not in mode:
                        cx.dma("sp", oacc[:], scr["ocmp"][g, r0_:r0_ + 128, :, :], oacc_b, writes=[oacc_b])
                        cx.dma("sp", sneg[:], scr["snegs"][g, r0_:r0_ + 128, :], sneg_b, writes=[sneg_b])
                    nt = min(2 * m + 2, n_all)
                    wt = [(w, 2 * m - 1 + w) for w in range(3) if 0 <= 2 * m - 1 + w < n_all]

                    def topk(qb=qb, sneg=sneg, sneg_b=sneg_b, oacc=oacc, oacc_b=oacc_b, g=g, r0_=r0_):
                        cx.op("dve", lambda: nc.vector.tensor_tensor(out=imp[:], in0=imp[:], in1=vS[:, qb, :], op=ALU.mult), reads=[imp_b, b_cm], writes=[imp_b])
                        cx.op("dve", lambda: nc.vector.tensor_tensor(out=imp[:], in0=imp[:], in1=aS[:, qb, :], op=ALU.add), reads=[imp_b, b_cm], writes=[imp_b])
                        cx.op("dve", lambda: nc.vector.max(out=m16[:, 0:8], in_=imp[:]), reads=[imp_b], writes=[m16_b])
                        cx.op("dve", lambda: nc.vector.match_replace(out=wk[:], in_to_replace=m16[:, 0:8], in_values=imp[:], imm_value=-3e38),
                              reads=[imp_b, m16_b], writes=[wk_b])
                        cx.op("dve", lambda: nc.vector.max(out=m16[:, 8:16], in_=wk[:]), reads=[wk_b], writes=[m16_b])
                        cx.op("dve", lambda: nc.vector.tensor_scalar(out=sneg[:], in0=imp[:], scalar1=m16[:, 15:16], scalar2=-30000.0, op0=ALU.is_lt, op1=ALU.mult),
                              reads=[imp_b, m16_b], writes=[sneg_b])
                        if "w" not in mode and os.environ.get("NOSTORE","0") != "1":
                            cx.dma("sp", scr["snegs"][g, r0_:r0_ + 128, :], sneg[:], sneg_b, reads=[sneg_b])
                            cx.dma("sp", scr["ocmp"][g, r0_:r0_ + 128, :, :], oacc[:], oacc_b, reads=[oacc_b])

                    BR = mode
                    for r in (range(4) if "c" in BR else []):
                        h = g * 4 + r

                        def mk_fin_c(r=r, h=h, qb=qb, qi=qi, topk=topk, oacc=oacc, oacc_b=oacc_b):
                            def fin(oi, li):
                                LV = int(os.environ.get("P4_FINLVL", "9"))
                                if LV < 1:
                                    return
                                at.rsum(li, 1, 0)
                                if LV < 2:
                                    return
                                if r == 0:
                                    cx.op("dve", lambda: nc.vector.tensor_scalar(out=imp[:], in0=at.pX[oi], scalar1=at.rl[:, 0:1], scalar2=None, op0=ALU.mult),
                                          reads=[at.pX_b[oi], at.rl_b], writes=[imp_b])
                                else:
                                    cx.op("dve", lambda: nc.vector.scalar_tensor_tensor(out=imp[:], in0=at.pX[oi], scalar=at.rl[:, 0:1], in1=imp[:], op0=ALU.mult, op1=ALU.add),
                                          reads=[at.pX_b[oi], at.rl_b, imp_b], writes=[imp_b])
                                if LV < 3:
                                    return
                                cx.op("dve", lambda: nc.vector.tensor_tensor(out=gm[:, 0:1], in0=at.rl[:, 0:1], in1=gt[qi][:, qb, 3 * h:3 * h + 1], op=ALU.mult),
                                      reads=[at.rl_b, gt_b[qi]], writes=[gm_b])
                                if LV < 4:
                                    return
                                cx.op("act", lambda: nc.scalar.activation(out=oacc[:, r, :], in_=at.pO[oi], func=AF.Copy, scale=gm[:, 0:1]),
                                      reads=[at.pO_b[oi], gm_b], writes=[oacc_b])
                                if r == 3 and os.environ.get("P4_NOTOPK", "0") != "1":
                                    topk()
                            return fin

                        at.run((lambda r=r, qi=qi, Q=Q, g=g: (lambda kt: [lambda o: nc.tensor.matmul(o, qg[qi][:, r, Q], k.kcT[:, g, 0:256], start=True, stop=True)]))(),
                               [qg_b[qi], k.kcv_b], 1, 256, (lambda qb=qb: (lambda kt: [(mC[:, qb, 0:256], [b_cm])]))(), SC,
                               (lambda g=g: (lambda kt, kb: k.vc[:, g, kb, :]))(), [k.kcv_b], mk_fin_c(),
                               x_fn=(lambda kb: ovl[:, kb, :]), x_reads=[b_c])
                    for r in (range(4) if "w" in BR else []):
                        h = g * 4 + r

                        def mk_fin_w(r=r, h=h, qb=qb, qi=qi, nw=len(wt), oacc=oacc, oacc_b=oacc_b):
                            def fin(oi, li):
                                if os.environ.get("P4_NOFINW", "0") == "1":
                                    return
                                at.rsum(li, nw, 2)
                                cx.op("dve", lambda: nc.vector.tensor_tensor(out=gm[:, 2:3], in0=at.rl[:, 2:3], in1=gt[qi][:, qb, 3 * h + 2:3 * h + 3], op=ALU.mult),
                                      reads=[at.rl_b, gt_b[qi]], writes=[gm_b])
                                ti_ = oi
                                cx.op("act", lambda: nc.scalar.activation(out=otmp[:, ti_, :], in_=at.pO[oi], func=AF.Copy, scale=gm[:, 2:3]),
                                      reads=[at.pO_b[oi], gm_b], writes=[otmp_b[ti_]])
                                cx.op("dve", lambda: nc.vector.tensor_tensor(out=oacc[:, r, :], in0=oacc[:, r, :], in1=otmp[:, ti_, :], op=ALU.add),
                                      reads=[otmp_b[ti_], oacc_b], writes=[oacc_b])
                            return fin

                        at.run((lambda r=r, qi=qi, Q=Q, wt=wt: (lambda i: [lambda o: nc.tensor.matmul(o, qg[qi][:, r, Q], kwin[0][:, wt[i][1] * TT:(wt[i][1] + 1) * TT], start=True, stop=True)]))(),
                               [qg_b[qi], kv_b], len(wt), 512, (lambda wt=wt, qb=qb: (lambda i: [(mW[:, wt[i][0], qb, :], [b_c])]))(), SC,
                               (lambda wt=wt: (lambda i, kb: vwin[0][:, wt[i][1] * 4 + kb, :]))(), [kv_b], mk_fin_w())
                    for r in (range(4) if "s" in BR else []):
                        h = g * 4 + r

                        def mk_masks_s(m=m, qb=qb, sneg=sneg, sneg_b=sneg_b):
                            def masks(kt):
                                ml = [(sneg[:, kt * 8:(kt + 1) * 8].unsqueeze(2).to_broadcast([128, 8, 64]), [sneg_b])]
                                if kt == 2 * m:
                                    ml.append((mA[:, qb, :], [b_c]))
                                if kt == 2 * m + 1:
                                    ml.append((mB[:, qb, :], [b_c]))
                                return ml
                            return masks

                        def mk_fin_s(r=r, h=h, qb=qb, qi=qi, nt=nt, g=g, m=m, oacc=oacc, oacc_b=oacc_b):
                            def fin(oi, li):
                                at.rsum(li, nt, 1)
                                cx.op("dve", lambda: nc.vector.tensor_tensor(out=gm[:, 1:2], in0=at.rl[:, 1:2], in1=gt[qi][:, qb, 3 * h + 1:3 * h + 2], op=ALU.mult),
                                      reads=[at.rl_b, gt_b[qi]], writes=[gm_b])
                                ti_ = oi
                                cx.op("act", lambda: nc.scalar.activation(out=otmp[:, ti_, :], in_=at.pO[oi], func=AF.Copy, scale=gm[:, 1:2]),
                                      reads=[at.pO_b[oi], gm_b], writes=[otmp_b[ti_]])
                                cx.op("dve", lambda: nc.vector.tensor_tensor(out=oacc[:, r, :], in0=oacc[:, r, :], in1=otmp[:, ti_, :], op=ALU.add),
                                      reads=[otmp_b[ti_], oacc_b], writes=[oacc_b])
                                if r == 3:
                                    for r2 in range(4):
                                        ot.put(oacc[:, r2, :], [oacc_b], qb, r2)
                                    if qb == 3:
                                        for r2 in range(4):
                                            ot.flush(r2, scr["oT"][g * 4 + r2, :, m * TT:(m + 1) * TT])
                            return fin

                        at.run((lambda r=r, qi=qi, Q=Q: (lambda kt: [lambda o: nc.tensor.matmul(o, qg[qi][:, r, Q], ksel[0][:, kt * TT:(kt + 1) * TT], start=True, stop=True)]))(),
                               [qg_b[qi], kv_b], nt, 512, mk_masks_s(), SC, (lambda kt, kb: vsel[0][:, kt * 4 + kb, :]), [kv_b], mk_fin_s(), copy_eng="act")
            at.drain()
        cx.barrier()


def emit_p4_sync(k, scr, cst, n_all=NT_ALL, n_my=NT_MY, groups=range(4), skip_cmp=False):
    nc, cx = k.nc, k.cx
    SC = 128.0 ** -0.5
    with ExitStack() as st:
        at = AttnSync(k, st, "p4ya")
        ot = OutT(k, st, "p4yo", 4)
        mA = sb(k, "p4y_mA", [128, 4, 512], F32, st)
        mB = sb(k, "p4y_mB", [128, 4, 512], F32, st)
        mW = sb(k, "p4y_mW", [128, 3, 4, 512], F32, st)
        mC = sb(k, "p4y_mC", [128, 4, 512], F32, st)
        vS = sb(k, "p4y_vS", [128, 4, 64], F32, st)
        aS = sb(k, "p4y_aS", [128, 4, 64], F32, st)
        ovl = sb(k, "p4y_ovl", [128, 4, 64], BF16, st)
        ovf = sb(k, "p4y_ovf", [128, 4, 64], F32, st)
        b_c = cx.buf("p4y_c")
        b_cm = cx.buf("p4y_cm")
        cx.dma("sp", mA[:], cst["mA"], b_c, writes=[b_c])
        cx.dma("sp", mB[:], cst["mB"], b_c, writes=[b_c])
        cx.dma("sp", mW[:], cst["mW"].rearrange("w p q k -> p w q k"), b_c, writes=[b_c])
        cx.dma("sp", ovf[:], cst["ovl"], b_c, writes=[b_c])
        cx.op("dve", lambda: nc.vector.tensor_copy(out=ovl[:], in_=ovf[:]), reads=[b_c], writes=[b_c])
        ksel = [sb(k, f"p4y_ksel{i}", [128, n_all * TT], BF16, st) for i in range(1)]
        kwin = [sb(k, f"p4y_kwin{i}", [128, n_all * TT], BF16, st) for i in range(1)]
        vsel = [sb(k, f"p4y_vsel{i}", [128, n_all * 4, 128], BF16, st) for i in range(1)]
        vwin = [sb(k, f"p4y_vwin{i}", [128, n_all * 4, 128], BF16, st) for i in range(1)]
        kv_b = cx.buf("p4y_kv")
        qg = [sb(k, f"p4y_q{i}", [128, 4, TT], BF16, st) for i in range(2)]
        qg_b = cx.bufs_n("p4y_q", 2)
        gt = [sb(k, f"p4y_gt{i}", [128, 4, 48], F32, st) for i in range(2)]
        gt_b = cx.bufs_n("p4y_gt", 2)
        oacc = sb(k, "p4y_oacc", [128, 4, 128], F32, st)
        oacc_b = cx.buf("p4y_oacc")
        pn = sb(k, "p4y_pn", [128, 256], BF16, st)
        pn_b = cx.buf("p4y_pn")
        pimp = ps(k, "p4y_pimp", [128, 64], F32, st)
        pimp_b = cx.buf("p4y_pimp")
        imp = sb(k, "p4y_imp", [128, 64], F32, st)
        imp_b = cx.buf("p4y_imp")
        wk = sb(k, "p4y_wk", [128, 64], F32, st)
        wk_b = cx.buf("p4y_wk")
        m16 = sb(k, "p4y_m16", [128, 16], F32, st)
        m16_b = cx.buf("p4y_m16")
        sneg = sb(k, "p4y_sneg", [128, 64], F32, st)
        sneg_b = cx.buf("p4y_sneg")
        gm = sb(k, "p4y_gm", [128, 4], F32, st)
        gm_b = cx.buf("p4y_gm")
        vsel_v = scr["vsel"].rearrange("(b p) c -> p b c", p=128)
        vwin_v = scr["vwin"].rearrange("(b p) c -> p b c", p=128)
        it = 0
        for g in groups:
            nb = n_all * 4
            cx.dma("sp", ksel[0][:], scr["kselT"][g, :, 0:n_all * TT], kv_b, writes=[kv_b])
            cx.dma("sp", kwin[0][:], scr["kwinT"][g, :, 0:n_all * TT], kv_b, writes=[kv_b])
            cx.dma("sp", vsel[0][:], vsel_v[:, 0:nb, g * 128:(g + 1) * 128], kv_b, writes=[kv_b])
            cx.dma("sp", vwin[0][:], vwin_v[:, 0:nb, g * 128:(g + 1) * 128], kv_b, writes=[kv_b])
            for m in range(n_my):
                qi = it % 2
                it += 1
                cx.dma("sp", qg[qi][:], scr["qT"][g * 4:(g + 1) * 4, :, m * TT:(m + 1) * TT].rearrange("h p t -> p h t"), qg_b[qi], writes=[qg_b[qi]])
                cx.dma("sp", gt[qi][:], scr["gtok"][m * TT:(m + 1) * TT, :].rearrange("(b p) c -> p b c", p=128), gt_b[qi], writes=[gt_b[qi]])
                cx.dma("sp", mC[:], cst["mC"][m], b_cm, writes=[b_cm])
                cx.dma("sp", vS[:], cst["vS"][m], b_cm, writes=[b_cm])
                cx.dma("sp", aS[:], cst["aS"][m], b_cm, writes=[b_cm])
                for qb in range(4):
                    Q = slice(qb * 128, (qb + 1) * 128)
                    if skip_cmp:
                        r0_ = m * TT + qb * 128
                        cx.dma("sp", oacc[:], scr["ocmp"][g, r0_:r0_ + 128, :, :], oacc_b, writes=[oacc_b])
                        cx.dma("sp", sneg[:], scr["snegs"][g, r0_:r0_ + 128, :], sneg_b, writes=[sneg_b])
                    else:
                        for r in range(4):
                            h = g * 4 + r
                            oi, li = at.run(lambda kt: [lambda o: nc.tensor.matmul(o, qg[qi][:, r, Q], k.kcT[:, g, 0:256], start=True, stop=True)],
                                            [qg_b[qi], k.kcv_b], 1, 256, lambda kt: [(mC[:, qb, 0:256], [b_cm])], SC,
                                            lambda kt, kb: k.vc[:, g, kb, :], [k.kcv_b])
                            at.rsum(li, 1, 0)
                            pidx = (at.c["P"] - 1) % 2
                            cx.op("dve", lambda: nc.vector.tensor_scalar(out=pn[:], in0=at.P[pidx][:, 0:256], scalar1=at.rl[:, 0:1], scalar2=None, op0=ALU.mult),
                                  reads=[at.P_b[pidx], at.rl_b], writes=[pn_b])
                            ti = at.nx("T")
                            cx.op("pe", [(lambda c=c: nc.tensor.transpose(at.pT[ti][:, c * 128:(c + 1) * 128], pn[:, c * 128:(c + 1) * 128], k.ident_b[:])) for c in range(2)],
                                  reads=[pn_b, k.b_const], writes=[at.pT_b[ti]])
                            pti = at.nx("PT")
                            cx.op("dve", lambda: nc.vector.tensor_copy(out=at.PT[pti][:, 0:256], in_=at.pT[ti][:, 0:256]), reads=[at.pT_b[ti]], writes=[at.PT_b[pti]])
                            cx.op("pe", [(lambda c=c: nc.tensor.matmul(pimp[:], at.PT[pti][:, c * 128:(c + 1) * 128], ovl[:, c, :], start=(r == 0 and c == 0), stop=(r == 3 and c == 1)))
                                         for c in range(2)], reads=[at.PT_b[pti], b_c], writes=[pimp_b])
                            cx.op("dve", lambda: nc.vector.tensor_tensor(out=gm[:, 0:1], in0=at.rl[:, 0:1], in1=gt[qi][:, qb, 3 * h:3 * h + 1], op=ALU.mult),
                                  reads=[at.rl_b, gt_b[qi]], writes=[gm_b])
                            cx.op("dve", lambda: nc.vector.tensor_scalar(out=oacc[:, r, :], in0=at.pO[oi], scalar1=gm[:, 0:1], scalar2=None, op0=ALU.mult),
                                  reads=[at.pO_b[oi], gm_b], writes=[oacc_b])
                        cx.op("dve", lambda: nc.vector.tensor_tensor(out=imp[:], in0=pimp[:], in1=vS[:, qb, :], op=ALU.mult), reads=[pimp_b, b_cm], writes=[imp_b])
                        cx.op("dve", lambda: nc.vector.tensor_tensor(out=imp[:], in0=imp[:], in1=aS[:, qb, :], op=ALU.add), reads=[imp_b, b_cm], writes=[imp_b])
                        cx.op("dve", lambda: nc.vector.max(out=m16[:, 0:8], in_=imp[:]), reads=[imp_b], writes=[m16_b])
                        cx.op("dve", lambda: nc.vector.match_replace(out=wk[:], in_to_replace=m16[:, 0:8], in_values=imp[:], imm_value=-3e38),
                              reads=[imp_b, m16_b], writes=[wk_b])
                        cx.op("dve", lambda: nc.vector.max(out=m16[:, 8:16], in_=wk[:]), reads=[wk_b], writes=[m16_b])
                        cx.op("dve", lambda: nc.vector.tensor_scalar(out=sneg[:], in0=imp[:], scalar1=m16[:, 15:16], scalar2=30000.0, op0=ALU.is_lt, op1=ALU.mult),
                              reads=[imp_b, m16_b], writes=[sneg_b])
                        cx.op("dve", lambda: nc.vector.tensor_scalar(out=sneg[:], in0=sneg[:], scalar1=-1.0, scalar2=None, op0=ALU.mult), reads=[sneg_b], writes=[sneg_b])
                    nt = min(2 * m + 2, n_all)
                    for r in range(4):
                        h = g * 4 + r

                        def masks(kt):
                            ml = [(sneg[:, kt * 8:(kt + 1) * 8].unsqueeze(2).to_broadcast([128, 8, 64]), [sneg_b])]
                            if kt == 2 * m:
                                ml.append((mA[:, qb, :], [b_c]))
                            if kt == 2 * m + 1:
                                ml.append((mB[:, qb, :], [b_c]))
                            return ml
                        oi, li = at.run(lambda kt: [lambda o: nc.tensor.matmul(o, qg[qi][:, r, Q], ksel[0][:, kt * TT:(kt + 1) * TT], start=True, stop=True)],
                                        [qg_b[qi], kv_b], nt, 512, masks, SC, lambda kt, kb: vsel[0][:, kt * 4 + kb, :], [kv_b])
                        at.rsum(li, nt, 1)
                        cx.op("dve", lambda: nc.vector.tensor_tensor(out=gm[:, 1:2], in0=at.rl[:, 1:2], in1=gt[qi][:, qb, 3 * h + 1:3 * h + 2], op=ALU.mult),
                              reads=[at.rl_b, gt_b[qi]], writes=[gm_b])
                        cx.op("dve", lambda: nc.vector.scalar_tensor_tensor(out=oacc[:, r, :], in0=at.pO[oi], scalar=gm[:, 1:2], in1=oacc[:, r, :], op0=ALU.mult, op1=ALU.add),
                              reads=[at.pO_b[oi], gm_b, oacc_b], writes=[oacc_b])
                    wt = [(w, 2 * m - 1 + w) for w in range(3) if 0 <= 2 * m - 1 + w < n_all]
                    for r in range(4):
                        h = g * 4 + r
                        oi, li = at.run(lambda i: [lambda o: nc.tensor.matmul(o, qg[qi][:, r, Q], kwin[0][:, wt[i][1] * TT:(wt[i][1] + 1) * TT], start=True, stop=True)],
                                        [qg_b[qi], kv_b], len(wt), 512, lambda i: [(mW[:, wt[i][0], qb, :], [b_c])], SC,
                                        lambda i, kb: vwin[0][:, wt[i][1] * 4 + kb, :], [kv_b])
                        at.rsum(li, len(wt), 2)
                        cx.op("dve", lambda: nc.vector.tensor_tensor(out=gm[:, 2:3], in0=at.rl[:, 2:3], in1=gt[qi][:, qb, 3 * h + 2:3 * h + 3], op=ALU.mult),
                              reads=[at.rl_b, gt_b[qi]], writes=[gm_b])
                        cx.op("dve", lambda: nc.vector.scalar_tensor_tensor(out=oacc[:, r, :], in0=at.pO[oi], scalar=gm[:, 2:3], in1=oacc[:, r, :], op0=ALU.mult, op1=ALU.add),
                              reads=[at.pO_b[oi], gm_b, oacc_b], writes=[oacc_b])
                    for r in range(4):
                        ot.put(oacc[:, r, :], [oacc_b], qb, r)
                for r in range(4):
                    ot.flush(r, scr["oT"][g * 4 + r, :, m * TT:(m + 1) * TT])
        cx.barrier()


class LNorm:
    def __init__(self, k, st, pfx, g_d, b_d):
        cx = k.cx
        self.k = k
        self.g_d, self.b_d = g_d, b_d
        self.stt = sb(k, pfx + "_st", [128, 8], F32, st)
        self.stt_b = cx.buf(pfx + "_st")
        self.rows = [sb(k, f"{pfx}_row{i}", [128, 2, 512], F32, st) for i in range(2)]
        self.rows_b = cx.bufs_n(pfx + "_row", 2)
        self.n = 0

    def run(self, y, y_b, junk, junk_b):
        k = self.k
        nc, cx = k.nc, k.cx
        s = self.stt
        sb_ = self.stt_b
        cx.op("act", lambda: nc.scalar.activation(out=junk, in_=y, func=AF.Identity, accum_out=s[:, 0:1]), reads=[y_b], writes=[junk_b, sb_])
        cx.op("act", lambda: nc.scalar.activation(out=junk, in_=y, func=AF.Square, accum_out=s[:, 1:2]), reads=[y_b], writes=[junk_b, sb_])
        cx.op("dve", lambda: nc.vector.tensor_scalar(out=s[:, 2:3], in0=s[:, 0:1], scalar1=1.0 / D, scalar2=None, op0=ALU.mult), reads=[sb_], writes=[sb_])
        cx.op("dve", lambda: nc.vector.tensor_tensor(out=s[:, 3:4], in0=s[:, 2:3], in1=s[:, 2:3], op=ALU.mult), reads=[sb_], writes=[sb_])
        cx.op("dve", lambda: nc.vector.scalar_tensor_tensor(out=s[:, 4:5], in0=s[:, 1:2], scalar=1.0 / D, in1=s[:, 3:4], op0=ALU.mult, op1=ALU.subtract),
              reads=[sb_], writes=[sb_])
        cx.op("act", lambda: nc.scalar.activation(out=s[:, 5:6], in_=s[:, 4:5], func=AF.Sqrt, bias=k.eps5[:], scale=1.0), reads=[sb_, k.b_const], writes=[sb_])
        cx.op("dve", lambda: nc.vector.reciprocal(out=s[:, 5:6], in_=s[:, 5:6]), reads=[sb_], writes=[sb_])
        cx.op("dve", lambda: nc.vector.tensor_scalar(out=y, in0=y, scalar1=s[:, 2:3], scalar2=s[:, 5:6], op0=ALU.subtract, op1=ALU.mult),
              reads=[y_b, sb_], writes=[y_b])
        for fc in range(8):
            ri = self.n % 2
            self.n += 1
            C = slice(fc * 512, (fc + 1) * 512)
            cx.dma("sp", self.rows[ri][:, 0, :], self.g_d[0:1, C].to_broadcast([128, 512]), self.rows_b[ri], writes=[self.rows_b[ri]])
            cx.dma("sp", self.rows[ri][:, 1, :], self.b_d[0:1, C].to_broadcast([128, 512]), self.rows_b[ri], writes=[self.rows_b[ri]])
            cx.op("dve", lambda: nc.vector.tensor_tensor(out=y[:, C], in0=y[:, C], in1=self.rows[ri][:, 0, :], op=ALU.mult), reads=[y_b, self.rows_b[ri]], writes=[y_b])
            cx.op("dve", lambda: nc.vector.tensor_tensor(out=y[:, C], in0=y[:, C], in1=self.rows[ri][:, 1, :], op=ALU.add), reads=[y_b, self.rows_b[ri]], writes=[y_b])


def emit_p5(k, scr, xq, w_out, modrow, ln_g, ln_b, alpha, n_my=NT_MY):
    nc, cx = k.nc, k.cx
    with ExitStack() as st:
        oT = sb(k, "p5_oT", [128, 32, TT], BF16, st)
        oT_b = cx.buf("p5_oT")
        yb = sb(k, "p5_y", [128, 4, D], F32, st)
        y_b = cx.bufs_n("p5_y", 4)
        slabs = [sb(k, f"p5_slab{i}", [128, 32, 512], BF16, st) for i in range(2)]
        slab_b = cx.bufs_n("p5_slab", 2)
        grow = [sb(k, f"p5_grow{i}", [128, 512], F32, st) for i in range(2)]
        grow_b = cx.bufs_n("p5_grow", 2)
        tmp = [sb(k, f"p5_tmp{i}", [128, 512], F32, st) for i in range(2)]
        tmp_b = cx.bufs_n("p5_tmp", 2)
        junk = sb(k, "p5_junk", [128, D], BF16, st)
        junk_b = cx.buf("p5_junk")
        hst = [sb(k, f"p5_hst{i}", [128, 32, 128], BF16, st) for i in range(2)]
        hst_b = cx.bufs_n("p5_hst", 2)
        pa = [ps(k, f"p5_pa{i}", [128, 512], F32, st) for i in range(3)]
        pa_b = cx.bufs_n("p5_pa", 3)
        ptr = [ps(k, f"p5_ptr{i}", [128, 512], F32, st) for i in range(2)]
        ptr_b = cx.bufs_n("p5_ptr", 2)
        ln = LNorm(k, st, "p5_ln", ln_g, ln_b)
        n_sl = 0
        n_pa = 0
        n_tr = 0
        n_h = 0
        for m in range(n_my):
            cx.dma("sp", oT[:], scr["oT"][:, :, m * TT:(m + 1) * TT].rearrange("c p t -> p c t"), oT_b, writes=[oT_b])
            for tb in range(4):
                r0 = m * TT + tb * 128
                cx.dma("sp", yb[:, tb, :], xq[r0:r0 + 128, :], y_b[tb], writes=[y_b[tb]])
            for fc in range(8):
                si = n_sl % 2
                n_sl += 1
                C = slice(fc * 512, (fc + 1) * 512)
                cx.dma("pool", slabs[si][:], w_out[fc], slab_b[si], writes=[slab_b[si]])
                cx.dma("sp", grow[si][:], modrow[2:3, C].to_broadcast([128, 512]), grow_b[si], writes=[grow_b[si]])
                for tb in range(4):
                    pi = n_pa % 3
                    n_pa += 1
                    cx.op("pe", [(lambda j=j: nc.tensor.matmul(pa[pi][:], oT[:, j, tb * 128:(tb + 1) * 128], slabs[si][:, j, :], start=(j == 0), stop=(j == 31)))
                                 for j in range(32)], reads=[oT_b, slab_b[si]], writes=[pa_b[pi]])
                    ti = pi % 2
                    cx.op("dve", lambda: nc.vector.tensor_tensor(out=tmp[ti][:], in0=pa[pi][:], in1=grow[si][:], op=ALU.mult),
                          reads=[pa_b[pi], grow_b[si]], writes=[tmp_b[ti]])
                    cx.op("dve", lambda: nc.vector.scalar_tensor_tensor(out=yb[:, tb, C], in0=yb[:, tb, C], scalar=alpha, in1=tmp[ti][:], op0=ALU.mult, op1=ALU.add),
                          reads=[y_b[tb], tmp_b[ti]], writes=[y_b[tb]])
            for tb in range(4):
                r0 = m * TT + tb * 128
                ln.run(yb[:, tb, :], y_b[tb], junk[:], junk_b)
                cx.dma("sp", scr["x1"][r0:r0 + 128, :], yb[:, tb, :], y_b[tb], reads=[y_b[tb]])
                hi = n_h % 2
                n_h += 1
                for j4 in range(8):
                    pi = n_tr % 2
                    n_tr += 1
                    cx.op("pe", [(lambda i=i: nc.tensor.transpose(ptr[pi][:, i * 128:(i + 1) * 128], yb[:, tb, (j4 * 4 + i) * 128:(j4 * 4 + i + 1) * 128], k.ident_f[:]))
                                 for i in range(4)], reads=[y_b[tb], k.b_const], writes=[ptr_b[pi]])
                    for i in range(4):
                        j = j4 * 4 + i
                        cx.op("act", lambda: nc.scalar.activation(out=hst[hi][:, j, :], in_=ptr[pi][:, i * 128:(i + 1) * 128], func=AF.Identity,
                                                                  scale=k.modT[:, 128 + j:129 + j], bias=k.modT[:, 96 + j:97 + j]),
                              reads=[ptr_b[pi], k.b_modT], writes=[hst_b[hi]])
                cx.dma("sp", scr["h2T"][:, :, r0:r0 + 128].rearrange("c p t -> p c t"), hst[hi][:], hst_b[hi], reads=[hst_b[hi]])
        cx.barrier()


def emit_p6(k, scr, w1_d, w2_d, modrow, ln_g, ln_b, alpha, out_d, n_my=NT_MY, n_sc=DFF // 256):
    nc, cx = k.nc, k.cx
    with ExitStack() as st:
        h2 = sb(k, "p6_h2", [128, 32, TT], BF16, st)
        h2_b = cx.buf("p6_h2")
        acc = sb(k, "p6_acc", [128, 4, D], F32, st)
        acc_b = cx.bufs_n("p6_acc", 4)
        accf_b = [[cx.buf(f"p6_acc{tb}_{fc}") for fc in range(8)] for tb in range(4)]
        etmp = [sb(k, f"p6_etmp{i}", [128, 512], F32, st) for i in range(3)]
        etmp_b = cx.bufs_n("p6_etmp", 3)
        n_e = 0
        s1 = [sb(k, f"p6_s1{i}", [128, 32, 256], BF16, st) for i in range(2)]
        s1_b = cx.bufs_n("p6_s1", 2)
        s2 = [sb(k, f"p6_s2{i}", [128, 2, D], BF16, st) for i in range(2)]
        s2_b = cx.bufs_n("p6_s2", 2)
        uT = [sb(k, f"p6_uT{i}", [128, 2, TT], BF16, st) for i in range(2)]
        uT_b = cx.bufs_n("p6_uT", 2)
        rl = [sb(k, f"p6_rl{i}", [128, TT], F32, st) for i in range(2)]
        rl_b = cx.bufs_n("p6_rl", 2)
        xr = sb(k, "p6_xr", [128, D], F32, st)
        xr_b = cx.buf("p6_xr")
        grow = [sb(k, f"p6_grow{i}", [128, 512], F32, st) for i in range(2)]
        grow_b = cx.bufs_n("p6_grow", 2)
        pu = [ps(k, f"p6_pu{i}", [128, 512], F32, st) for i in range(2)]
        pu_b = cx.bufs_n("p6_pu", 2)
        po = [ps(k, f"p6_po{i}", [128, 512], F32, st) for i in range(6)]
        po_b = cx.bufs_n("p6_po", 6)
        ln = LNorm(k, st, "p6_ln", ln_g, ln_b)
        n_u = 0
        n_o = 0
        n_g = 0
        tot = n_my * n_sc

        def load1(idx):
            if idx < tot:
                cx.dma("pool", s1[idx % 2][:], w1_d[idx % n_sc], s1_b[idx % 2], writes=[s1_b[idx % 2]])

        def load2(idx):
            if idx < tot:
                cx.dma("pool", s2[idx % 2][:], w2_d[idx % n_sc], s2_b[idx % 2], writes=[s2_b[idx % 2]])

        def ff1_steps(idx):
            nonlocal n_u
            si = idx % 2
            ui = idx % 2
            for c in range(2):
                pi = n_u % 2
                n_u += 1
                h = cx.op_begin("pe", reads=[s1_b[si], h2_b], writes=[pu_b[pi]])
                for j in range(32):
                    fn = (lambda j=j: nc.tensor.matmul(pu[pi][:], s1[si][:, j, c * 128:(c + 1) * 128], h2[:, j, :], start=(j == 0), stop=(j == 31)))
                    if j < 31:
                        cx.op_piece(h, fn)
                    else:
                        cx.op_end(h, fn)
                        cx.op("act", lambda: nc.scalar.activation(out=rl[pi][:], in_=pu[pi][:], func=AF.Relu), reads=[pu_b[pi]], writes=[rl_b[pi]])
                        cx.op("pool", lambda: nc.gpsimd.tensor_tensor(out=uT[ui][:, c, :], in0=rl[pi][:], in1=rl[pi][:], op=ALU.mult),
                              reads=[rl_b[pi]], writes=[uT_b[ui]])
                    yield

        def ff1(idx):
            for _ in ff1_steps(idx):
                pass

        load1(0)
        load2(0)
        load1(1)
        load2(1)
        def start_tile(m):
            cx.dma("sp", h2[:], scr["h2T"][:, :, m * TT:(m + 1) * TT].rearrange("c p t -> p c t"), h2_b, writes=[h2_b])
            ff1(m * n_sc)

        for m in range(n_my):
            start_tile(m)
            for sc in range(n_sc):
                idx = m * n_sc + sc
                si = idx % 2
                ui = idx % 2
                load1(idx + 2)
                gen = ff1_steps(idx + 1) if sc + 1 < n_sc else None
                for tb in range(4):
                    for fc in range(8):
                        if gen is not None:
                            next(gen, None)
                            next(gen, None)
                        oi = n_o % 6
                        n_o += 1
                        C = slice(fc * 512, (fc + 1) * 512)
                        cx.op("pe", [(lambda c=c: nc.tensor.matmul(po[oi][:], uT[ui][:, c, tb * 128:(tb + 1) * 128], s2[si][:, c, C], start=(c == 0), stop=(c == 1)))
                                     for c in range(2)], reads=[uT_b[ui], s2_b[si]], writes=[po_b[oi]])
                        ab = accf_b[tb][fc]
                        if sc == 0:
                            cx.op("dve", lambda: nc.vector.tensor_copy(out=acc[:, tb, C], in_=po[oi][:]), reads=[po_b[oi]], writes=[ab, acc_b[tb]])
                        elif n_o % 2 == 0:
                            cx.op("dve", lambda: nc.vector.tensor_tensor(out=acc[:, tb, C], in0=acc[:, tb, C], in1=po[oi][:], op=ALU.add),
                                  reads=[po_b[oi], ab], writes=[ab])
                        else:
                            ei = n_e % 3
                            n_e += 1
                            cx.op("act", lambda: nc.scalar.copy(out=etmp[ei][:], in_=po[oi][:]), reads=[po_b[oi]], writes=[etmp_b[ei]])
                            cx.op("pool", lambda: nc.gpsimd.tensor_tensor(out=acc[:, tb, C], in0=acc[:, tb, C], in1=etmp[ei][:], op=ALU.add),
                                  reads=[etmp_b[ei], ab], writes=[ab])
                if gen is not None:
                    for _ in gen:
                        pass
                load2(idx + 2)
            for tb in range(4):
                r0 = m * TT + tb * 128
                cx.dma("sp", xr[:], scr["x1"][r0:r0 + 128, :], xr_b, writes=[xr_b])
                cx.op("dve", lambda: nc.vector.tensor_copy(out=acc[:, tb, 0:1], in_=acc[:, tb, 0:1]), reads=accf_b[tb], writes=[acc_b[tb]])
                for fc in range(8):
                    gi = n_g % 2
                    n_g += 1
                    C = slice(fc * 512, (fc + 1) * 512)
                    cx.dma("sp", grow[gi][:], modrow[5:6, C].to_broadcast([128, 512]), grow_b[gi], writes=[grow_b[gi]])
                    cx.op("dve", lambda: nc.vector.tensor_tensor(out=acc[:, tb, C], in0=acc[:, tb, C], in1=grow[gi][:], op=ALU.mult),
                          reads=[acc_b[tb], grow_b[gi]], writes=[acc_b[tb]])
                cx.op("dve", lambda: nc.vector.scalar_tensor_tensor(out=acc[:, tb, :], in0=xr[:], scalar=alpha, in1=acc[:, tb, :], op0=ALU.mult, op1=ALU.add),
                      reads=[xr_b, acc_b[tb]], writes=[acc_b[tb]])
                ln.run(acc[:, tb, :], acc_b[tb], xr[:], xr_b)
                cx.dma("sp", out_d[r0:r0 + 128, :], acc[:, tb, :], acc_b[tb], reads=[acc_b[tb]])
        cx.barrier()


ALPHA_C = 2.0 ** 0.25
CONST_NAMES = ("ident_f", "prot_p", "prot_m", "invf", "mA", "mB", "mW", "mC", "vS", "aS", "ovl")


def setup(nc, stack):
    k = K()
    k.nc = nc
    k.stack = stack
    k.cx = Ctx(nc, stack)
    cx = k.cx
    I = "ExternalInput"
    k.cd = {}
    shapes = {"ident_f": [128, 128], "prot_p": [128, 128], "prot_m": [128, 128], "invf": [128, 2], "mA": [128, 4, 512], "mB": [128, 4, 512],
              "mW": [3, 128, 4, 512], "mC": [4, 128, 4, 512], "vS": [4, 128, 4, 64], "aS": [4, 128, 4, 64], "ovl": [128, 4, 64]}
    for n in CONST_NAMES:
        k.cd[n] = dram(nc, "c_" + n, shapes[n], F32, I)
    k.ident_f = sb(k, "ident_f", [128, 128], F32)
    k.prot_p = sb(k, "prot_p", [128, 128], F32)
    k.prot_m = sb(k, "prot_m", [128, 128], F32)
    k.invf = sb(k, "invf", [128, 2], F32)
    k.modT = sb(k, "modT", [128, 192], F32)
    k.negpi = sb(k, "negpi", [128, 1], F32)
    k.ones_bf = sb(k, "ones_bf", [128, 128], BF16)
    k.ident_b = sb(k, "ident_b", [128, 128], BF16)
    k.eps6 = sb(k, "eps6", [128, 1], F32)
    k.eps5 = sb(k, "eps5", [128, 1], F32)
    k.kcv_b = cx.buf("kcv")
    k.b_const = cx.buf("const")
    k.b_modT = cx.buf("modT")
    for dst, src in ((k.ident_f, "ident_f"), (k.prot_p, "prot_p"), (k.prot_m, "prot_m"), (k.invf, "invf")):
        cx.dma("sp", dst[:], k.cd[src], k.b_const, writes=[k.b_const])
    cx.op("dve", lambda: nc.vector.memset(k.negpi[:], -PI), writes=[k.b_const])
    cx.op("dve", lambda: nc.vector.memset(k.ones_bf[:], 1.0), writes=[k.b_const])
    cx.op("dve", lambda: nc.vector.memset(k.eps6[:], 1e-6), writes=[k.b_const])
    cx.op("dve", lambda: nc.vector.memset(k.eps5[:], 1e-5), writes=[k.b_const])
    cx.op("dve", lambda: nc.vector.tensor_copy(out=k.ident_b[:], in_=k.ident_f[:]), reads=[k.b_const], writes=[k.b_const])
    return k


def build_program():
    nc = bass.Bass("TRN2", target_bir_lowering=False)
    with ExitStack() as st:
        k = setup(nc, st)
        I = "ExternalInput"
        x_all = dram(nc, "x_all", [S, D], F32, I)
        xq = dram(nc, "xq", [NT_MY * TT, D], F32, I)
        cT = dram(nc, "cT", [128, 32], F32, I)
        pos_all = dram(nc, "pos_all", [1, S], I32, I)
        pos_q = dram(nc, "pos_q", [1, NT_MY * TT], I32, I)
        w_ada = dram(nc, "w_ada", [48, 128, 32, 512], F32, I)
        b_ada = dram(nc, "b_ada", [1, 6 * D], F32, I)
        w_in = dram(nc, "w_in", [len(W_IN_SLABS), 128, 32, 256], F32, I)
        k1 = dram(nc, "k1", [128, 32, 256], F32, I)
        k2 = dram(nc, "k2", [128, 2, 128], F32, I)
        v1 = dram(nc, "v1", [128, 32, 256], F32, I)
        v2 = dram(nc, "v2", [128, 2, 128], F32, I)
        posk = dram(nc, "posk", [128, 32], F32, I)
        posv = dram(nc, "posv", [128, 32], F32, I)
        qnT = dram(nc, "qnT", [128, 12], F32, I)
        kvnT = dram(nc, "kvnT", [128, 4], F32, I)
        w_uq = dram(nc, "w_uq", [16, 128, 12, 192], F32, I)
        w_ukv = dram(nc, "w_ukv", [16, 128, 4, 256], F32, I)
        w_out = dram(nc, "w_out", [8, 128, 32, 512], F32, I)
        l1g = dram(nc, "l1g", [1, D], F32, I)
        l1b = dram(nc, "l1b", [1, D], F32, I)
        l2g = dram(nc, "l2g", [1, D], F32, I)
        l2b = dram(nc, "l2b", [1, D], F32, I)
        w1 = dram(nc, "w_ff1", [64, 128, 32, 256], F32, I)
        w2 = dram(nc, "w_ff2", [64, 128, 2, 4096], F32, I)
        out = dram(nc, "out", [NT_MY * TT, D], F32, "ExternalOutput")
        modrow = dram(nc, "s_modrow", [6, D], F32)
        NQ = NT_MY * TT
        scr = dict(
            kcmpT=dram(nc, "s_kcmpT", [4, 128, S], BF16), vcmpT=dram(nc, "s_vcmpT", [4, 128, S], BF16),
            kselT=dram(nc, "s_kselT", [4, 128, S], BF16), kwinT=dram(nc, "s_kwinT", [4, 128, S], BF16),
            vsel=dram(nc, "s_vsel", [S, 512], BF16), vwin=dram(nc, "s_vwin", [S, 512], BF16),
            ckvf=dram(nc, "s_ckvf", [4, 128, S], F32), kpeT=dram(nc, "s_kpeT", [64, S], BF16),
            qT=dram(nc, "s_qT", [16, 128, NQ], BF16), gtok=dram(nc, "s_gtok", [NQ, 48], F32), cqf=dram(nc, "s_cqf", [12, 128, NQ], F32),
            qmnT=dram(nc, "s_qmnT", [16, 128, NQ], BF16), qmpT=dram(nc, "s_qmpT", [16, 64, NQ], BF16),
            ocmp=dram(nc, "s_ocmp", [4, NQ, 4, 128], F32), snegs=dram(nc, "s_snegs", [4, NQ, 64], F32),
            oT=dram(nc, "s_oT", [32, 128, NQ], BF16), x1=dram(nc, "s_x1", [NQ, D], F32), h2T=dram(nc, "s_h2T", [32, 128, NQ], BF16))
        emit_p0(k, cT, w_ada, b_ada, modrow)
        emit_p1(k, x_all, xq, pos_all, pos_q, w_in, scr)
        emit_p2(k, scr, w_uq, w_ukv, qnT, kvnT, pos_q, k.cd["mA"], k.cd["mB"])
        with ExitStack() as st34:
            k.kcT = sb(k, "kcT", [128, 4, 512], BF16, st34)
            k.vc = sb(k, "vc", [128, 4, 4, 128], BF16, st34)
            emit_p3(k, scr, k1, k2, v1, v2, posk, posv)
            emit_p4_sync(k, scr, k.cd)
        emit_p5(k, scr, xq, w_out, modrow, l1g, l1b, ALPHA_C)
        emit_p6(k, scr, w1, w2, modrow, l2g, l2b, ALPHA_C, out)
        k.cx.final_wait()
    return nc


def kernel(x, c, positions, w_ada, b_ada, w_in, nsa_pos_k, nsa_pos_v, nsa_cmp_k1, nsa_cmp_k2,
           nsa_cmp_v1, nsa_cmp_v2, mla_q_norm, mla_kv_norm, mla_w_uq, mla_w_ukv, w_out,
           ln1_g, ln1_b, w_ff1, w_ff2, ln2_g, ln2_b):
    f32 = np.float32
    A = lambda a: np.asarray(a)
    x = A(x); c = A(c); positions = A(positions)
    nc = build_program()
    tw = host_tile_weights(A(w_ada)[0], A(w_in)[0], A(nsa_cmp_k1)[0], A(nsa_cmp_k2)[0], A(nsa_cmp_v1)[0], A(nsa_cmp_v2)[0],
                           A(mla_w_uq)[0], A(mla_w_ukv)[0], A(w_out)[0], A(w_ff1)[0], A(w_ff2)[0])
    shared = {
        "b_ada": A(b_ada).reshape(1, -1),
        "posk": np.ascontiguousarray(A(nsa_pos_k)[0].T), "posv": np.ascontiguousarray(A(nsa_pos_v)[0].T),
        "qnT": np.ascontiguousarray(A(mla_q_norm)[0].reshape(12, 128).T), "kvnT": np.ascontiguousarray(A(mla_kv_norm)[0].reshape(4, 128).T),
        "l1g": A(ln1_g).reshape(1, -1), "l1b": A(ln1_b).reshape(1, -1), "l2g": A(ln2_g).reshape(1, -1), "l2b": A(ln2_b).reshape(1, -1),
    }
    shared.update(tw)
    consts = [host_constants(0), host_constants(1)]
    in_maps = []
    rows_of = []
    for core in range(8):
        b, hf = core // 2, core % 2
        rows = np.concatenate([np.arange((2 * m + hf) * TT, (2 * m + hf + 1) * TT) for m in range(NT_MY)])
        rows_of.append((b, rows))
        im = dict(shared)
        im["x_all"] = x[b]
        im["xq"] = np.ascontiguousarray(x[b][rows])
        im["cT"] = np.ascontiguousarray(c[b].reshape(32, 128).T)
        im["pos_all"] = np.ascontiguousarray(positions[b].reshape(1, -1)).astype(np.int32)
        im["pos_q"] = np.ascontiguousarray(positions[b][rows].reshape(1, -1)).astype(np.int32)
        for n in CONST_NAMES:
            im["c_" + n] = consts[hf][n]
        in_maps.append(im)
    res = run_bass_kernel_spmd(nc, in_maps, core_ids=list(range(8)))
    outp = np.empty((4, S, D), dtype=f32)
    for core in range(8):
        b, rows = rows_of[core]
        outp[b, rows] = np.asarray(res.results[core]["out"], dtype=f32)
    return outp
```

```python
import math
import os
from contextlib import ExitStack
import numpy as np
import concourse.bass as bass
import concourse.mybir as mybir
from concourse.bass_utils import run_bass_kernel_spmd


F32 = mybir.dt.float32
BF16 = mybir.dt.bfloat16
I32 = mybir.dt.int32
AF = mybir.ActivationFunctionType
ALU = mybir.AluOpType
AX = mybir.AxisListType


class Buf:
    __slots__ = ("name", "last_w", "readers", "dsem", "dcount", "ctx")

    def __init__(self, ctx, name):
        self.ctx = ctx
        self.name = name
        self.last_w = None
        self.readers = {}
        self.dsem = None
        self.dcount = 0


class EngState:
    def __init__(self, name, eng, sem):
        self.name = name
        self.eng = eng
        self.sem = sem
        self.count = 0
        self.known = {}


class Ctx:
    def __init__(self, nc, stack):
        self.nc = nc
        self.stack = stack
        self.E = {}
        for name, eng in (("pe", nc.tensor), ("act", nc.scalar), ("dve", nc.vector),
                          ("pool", nc.gpsimd), ("sp", nc.sync)):
            sem = stack.enter_context(nc.semaphore("sem_" + name))
            self.E[name] = EngState(name, eng, sem)
        self.bufs = []
        self.dma_bufs = []
        self.free_dsems = []
        self.nwaits = 0
        self.ninst = 0

    def buf(self, name):
        b = Buf(self, name)
        self.bufs.append(b)
        return b

    def bufs_n(self, name, n):
        return [self.buf(f"{name}{i}") for i in range(n)]

    def _wait_token(self, es, tok):
        if tok is None:
            return
        if tok[0] == "e":
            _, en, c = tok
            if en == es.name and en == "pe":
                return
            if es.known.get(en, 0) >= c:
                return
            es.eng.wait_ge(self.E[en].sem, c)
            es.known[en] = c
            self.nwaits += 1
        else:
            _, b = tok
            c = b.dcount
            key = ("d", id(b.dsem))
            if es.known.get(key, 0) >= c:
                return
            es.eng.wait_ge(b.dsem, 16 * c)
            es.known[key] = c
            self.nwaits += 1

    def _deps(self, es, reads, writes):
        for b in reads:
            self._wait_token(es, b.last_w)
        for b in writes:
            self._wait_token(es, b.last_w)
            for tok in list(b.readers.values()):
                self._wait_token(es, tok)

    def _commit(self, tok, reads, writes):
        for b in reads:
            if tok[0] == "e":
                b.readers[tok[1]] = tok
            else:
                b.readers[("d", id(tok[1]))] = tok
        for b in writes:
            b.last_w = tok
            b.readers = {}

    def op(self, en, fns, reads=(), writes=()):
        es = self.E[en]
        if callable(fns):
            fns = [fns]
        self._deps(es, reads, writes)
        ins = None
        for f in fns:
            ins = f()
            self.ninst += 1
        es.count += 1
        ins.then_inc(es.sem, 1)
        tok = ("e", en, es.count)
        self._commit(tok, reads, writes)
        return tok

    def op_begin(self, en, reads=(), writes=()):
        es = self.E[en]
        self._deps(es, reads, writes)
        return (en, list(reads), list(writes))

    def op_piece(self, h, fn):
        fn()
        self.ninst += 1

    def op_end(self, h, fn):
        en, reads, writes = h
        es = self.E[en]
        ins = fn()
        self.ninst += 1
        es.count += 1
        ins.then_inc(es.sem, 1)
        tok = ("e", en, es.count)
        self._commit(tok, reads, writes)
        return tok

    def dma(self, q, out, in_, sb, reads=(), writes=(), **kw):
        es = self.E[q]
        if sb.dsem is None:
            if self.free_dsems:
                sb.dsem, sb.dcount = self.free_dsems.pop()
            else:
                sb.dsem = self.stack.enter_context(self.nc.semaphore("dsem%d" % len(self.dma_bufs) + "_" + sb.name))
                sb.dcount = 0
            self.dma_bufs.append(sb)
        self._deps(es, reads, writes)
        ins = es.eng.dma_start(out=out, in_=in_, **kw)
        sb.dcount += 1
        ins.then_inc(sb.dsem, 16)
        self.ninst += 1
        tok = ("d", sb)
        self._commit(tok, reads, writes)
        return tok

    def barrier(self, engines=("pe", "act", "dve", "pool", "sp"), release=True):
        for en in engines:
            es = self.E[en]
            for on, os_ in self.E.items():
                if os_.count == 0 or (on == en and en in ("pe", "sp")):
                    continue
                if es.known.get(on, 0) < os_.count:
                    es.eng.wait_ge(os_.sem, os_.count)
                    es.known[on] = os_.count
            for b in self.dma_bufs:
                key = ("d", id(b.dsem))
                if b.dcount and es.known.get(key, 0) < b.dcount:
                    es.eng.wait_ge(b.dsem, 16 * b.dcount)
                    es.known[key] = b.dcount
        if not release:
            return
        for b in self.bufs:
            b.last_w = None
            b.readers = {}
        for b in self.dma_bufs:
            self.free_dsems.append((b.dsem, b.dcount))
            b.dsem = None
            b.dcount = 0
        self.dma_bufs = []

    def final_wait(self):
        self.barrier(engines=("sp",), release=False)


D = 4096
S = 4096
NT_ALL = 8
NT_MY = 4
TT = 512
THETA = 500000.0
DIN = 7280
DFF = 16384
PI = math.pi


W_IN_SLABS = ([(2048 + (kind * 4 + gp * 2) * 128, 256) for kind in range(6) for gp in range(2)] + [(6704, 256), (6960, 256), (7216, 64)]
              + [(hp * 256, 256) for hp in range(8)] + [(5120, 48)] + [(5168 + sl * 256, 256) for sl in range(6)])
W_IN_SLAB_INDEX = {cn: i for i, cn in enumerate(W_IN_SLABS)}


def host_tile_weights(w_ada, w_in, k1, k2, v1, v2, w_uq, w_ukv, w_out, w_ff1, w_ff2):
    t = {}
    t["w_ada"] = np.ascontiguousarray(w_ada.reshape(32, 128, 48, 512).transpose(2, 1, 0, 3))
    wi = np.zeros((len(W_IN_SLABS), 128, 32, 256), np.float32)
    for i, (c0, n) in enumerate(W_IN_SLABS):
        wi[i, :, :, :n] = w_in[:, c0:c0 + n].reshape(32, 128, n).transpose(1, 0, 2)
    t["w_in"] = wi
    t["k1"] = np.ascontiguousarray(k1.reshape(32, 128, 256).transpose(1, 0, 2))
    t["v1"] = np.ascontiguousarray(v1.reshape(32, 128, 256).transpose(1, 0, 2))
    t["k2"] = np.ascontiguousarray(k2.reshape(2, 128, 128).transpose(1, 0, 2))
    t["v2"] = np.ascontiguousarray(v2.reshape(2, 128, 128).transpose(1, 0, 2))
    t["w_uq"] = np.ascontiguousarray(w_uq.reshape(12, 128, 16, 192).transpose(2, 1, 0, 3))
    t["w_ukv"] = np.ascontiguousarray(w_ukv.reshape(4, 128, 16, 256).transpose(2, 1, 0, 3))
    t["w_out"] = np.ascontiguousarray(w_out.reshape(32, 128, 8, 512).transpose(2, 1, 0, 3))
    t["w_ff1"] = np.ascontiguousarray(w_ff1.reshape(32, 128, 64, 256).transpose(2, 1, 0, 3))
    t["w_ff2"] = np.ascontiguousarray(w_ff2.reshape(64, 2, 128, 4096).transpose(0, 2, 1, 3))
    return t


class K:
    pass


def dram(nc, name, shape, dt, kind=None):
    if kind is None:
        return nc.dram_tensor(name, list(shape), dt).ap()
    return nc.dram_tensor(name, list(shape), dt, kind=kind).ap()


def sb(k, name, shape, dt, stack=None):
    st = stack if stack is not None else k.stack
    t = st.enter_context(k.nc.sbuf_tensor(name, list(shape), dt))
    return t


def ps(k, name, shape, dt, stack=None):
    st = stack if stack is not None else k.stack
    t = st.enter_context(k.nc.psum_tensor(name, list(shape), dt))
    return t


def host_constants(hf):
    c = {}
    c["ident_f"] = np.eye(128, dtype=np.float32)
    pr = np.zeros((128, 128), np.float32)
    for m in range(16):
        pr[m + 16, m] = -1.0
        pr[m, m + 16] = 1.0
    c["prot_p"] = pr
    pm = np.zeros((128, 128), np.float32)
    for m in range(32):
        pm[m + 32, m] = -1.0
        pm[m, m + 32] = 1.0
    c["prot_m"] = pm
    inv = np.zeros((128, 2), np.float32)
    for r in range(32):
        inv[r, 0] = THETA ** (-(2.0 * (r % 16)) / 32.0)
    for r in range(64):
        inv[r, 1] = THETA ** (-(2.0 * (r % 32)) / 64.0)
    c["invf"] = inv
    NEGM = -30000.0
    qq = (np.arange(4)[:, None] * 128 + np.arange(128)[None, :])
    kk = np.arange(512)

    def band(dT):
        diff = dT * 512 + qq.T[:, :, None] - kk[None, None, :]
        return np.where((diff >= 0) & (diff < 512), 0.0, NEGM).astype(np.float32)

    def causal(dT):
        diff = dT * 512 + qq.T[:, :, None] - kk[None, None, :]
        return np.where(diff >= 0, 0.0, NEGM).astype(np.float32)
    c["mA"] = causal(0) if hf == 0 else causal(1)
    c["mB"] = causal(-1) if hf == 0 else causal(0)
    if hf == 0:
        c["mW"] = np.stack([band(1), band(0), band(-1)], axis=0)
    else:
        c["mW"] = np.stack([band(2), band(1), band(0)], axis=0)
    n = np.arange(512)
    mC = np.zeros((4, 128, 4, 512), np.float32)
    vS = np.zeros((4, 128, 4, 64), np.float32)
    aS = np.zeros((4, 128, 4, 64), np.float32)
    j = np.arange(64)
    for m in range(4):
        t = (2 * m + hf) * 512 + qq.T
        ok = (16 * n[None, None, :] + 31 <= t[:, :, None]) & (n[None, None, :] < 255)
        mC[m] = np.where(ok, 0.0, NEGM)
        cur = (t // 64)[:, :, None]
        jj = j[None, None, :]
        valid = jj <= cur
        forced = (jj == 0) | ((jj <= cur) & (jj > cur - 2))
        vS[m] = valid.astype(np.float32)
        aS[m] = np.where(valid, np.where(forced, 1e4, 0.0), -1e30)
    c["mC"] = mC
    c["vS"] = vS
    c["aS"] = aS
    cs = n * 16
    bs = j * 64
    ov = ((cs[:, None] < bs[None, :] + 64) & (cs[:, None] + 32 > bs[None, :])).astype(np.float32)
    ov[255:] = 0.0
    c["ovl"] = np.ascontiguousarray(ov.reshape(4, 128, 64).transpose(1, 0, 2))
    return c


def emit_p0(k, cT, w_ada, b_ada, modrow, ncc=48):
    nc, cx = k.nc, k.cx
    with ExitStack() as st:
        cond_f = sb(k, "p0_condf", [128, 32], F32, st)
        cond_rep = sb(k, "p0_condrep", [128, 32, 128], BF16, st)
        NSL = 3
        slabs = [sb(k, f"p0_slab{i}", [128, 32, 512], BF16, st) for i in range(NSL)]
        slab_b = cx.bufs_n("p0_slab", NSL)
        brow = [sb(k, f"p0_brow{i}", [128, 512], F32, st) for i in range(NSL)]
        brow_b = cx.bufs_n("p0_brow", NSL)
        mrow = [sb(k, f"p0_mrow{i}", [128, 512], F32, st) for i in range(2)]
        mrow_b = cx.bufs_n("p0_mrow", 2)
        pacc = [ps(k, f"p0_pacc{i}", [128, 512], F32, st) for i in range(2)]
        pacc_b = cx.bufs_n("p0_pacc", 2)
        ptr = [ps(k, f"p0_ptr{i}", [128, 512], F32, st) for i in range(2)]
        ptr_b = cx.bufs_n("p0_ptr", 2)
        b_cond = cx.buf("p0_cond")
        b_crep = cx.buf("p0_crep")

        cx.dma("sp", cond_f[:], cT, b_cond, writes=[b_cond])
        cx.op("act", lambda: nc.scalar.activation(out=cond_f[:], in_=cond_f[:], func=AF.Silu),
              reads=[b_cond], writes=[b_cond])
        cx.op("dve", lambda: nc.vector.tensor_copy(
            out=cond_rep[:], in_=cond_f[:].unsqueeze(2).to_broadcast([128, 32, 128])),
            reads=[b_cond], writes=[b_crep])

        def load(cc):
            s = cc % NSL
            cx.dma("pool", slabs[s][:], w_ada[cc], slab_b[s], writes=[slab_b[s]])
            cx.dma("sp", brow[s][:], b_ada[0:1, cc * 512:(cc + 1) * 512].to_broadcast([128, 512]),
                   brow_b[s], writes=[brow_b[s]])

        PRE = NSL - 1
        for cc in range(min(PRE, ncc)):
            load(cc)
        for cc in range(ncc):
            if cc + PRE < ncc:
                load(cc + PRE)
            s = cc % NSL
            pa, pab = pacc[cc % 2], pacc_b[cc % 2]
            cx.op("pe", [(lambda j=j: nc.tensor.matmul(pa[:], cond_rep[:, j, :], slabs[s][:, j, :],
                                                        start=(j == 0), stop=(j == 31))) for j in range(32)],
                  reads=[b_crep, slab_b[s]], writes=[pab])
            v = cc // 8
            plus = 1.0 if v in (1, 2, 4, 5) else 0.0
            mr, mrb = mrow[cc % 2], mrow_b[cc % 2]
            cx.op("dve", lambda: nc.vector.scalar_tensor_tensor(
                out=mr[:], in0=pa[:], scalar=plus, in1=brow[s][:], op0=ALU.add, op1=ALU.add),
                reads=[pab, brow_b[s]], writes=[mrb])
            cx.dma("sp", modrow[v:v + 1, (cc % 8) * 512:(cc % 8 + 1) * 512], mr[0:1, :], mrb, reads=[mrb])
            pt, ptb = ptr[cc % 2], ptr_b[cc % 2]
            cx.op("pe", [(lambda i=i: nc.tensor.transpose(pt[:, i * 128:(i + 1) * 128],
                                                           mr[:, i * 128:(i + 1) * 128], k.ident_f[:]))
                         for i in range(4)],
                  reads=[mrb, k.b_const], writes=[ptb])
            cx.op("act", lambda: nc.scalar.copy(
                out=k.modT[:, cc * 4:(cc + 1) * 4],
                in_=pt[:].rearrange("p (a b) -> p a b", b=128)[:, :, 0]),
                reads=[ptb], writes=[k.b_modT])
        cx.barrier()


def rope_tables(k, st, pos_ap, ntok, tabs, tab_b, tmp, tmp_b, posi, posi_b):
    nc, cx = k.nc, k.cx
    cx.dma("sp", posi[:, 0:ntok], pos_ap.to_broadcast([64, ntok]), posi_b, writes=[posi_b])
    posf = tmp[0]
    cx.op("dve", lambda: nc.vector.tensor_copy(out=posf[:, 0:ntok], in_=posi[:, 0:ntok]),
          reads=[posi_b], writes=[tmp_b[0]])
    ang, t1, t2 = tmp[1], tmp[2], tmp[3]
    b_ang, b1, b2 = tmp_b[1], tmp_b[2], tmp_b[3]
    ti, bi = k.p1_ti, k.p1_ti_b
    N = slice(0, ntok)
    for col, (cn, sn) in ((0, ("cosP", "sinP")), (1, ("cosM", "sinM"))):
        cx.op("dve", lambda: nc.vector.tensor_scalar(
            out=ang[:, N], in0=posf[:, N], scalar1=k.invf[0:64, col:col + 1], scalar2=None,
            op0=ALU.mult), reads=[tmp_b[0], k.b_const], writes=[b_ang])
        for name, shift in ((sn, 0.0), (cn, 0.5 * PI)):
            cx.op("dve", lambda: nc.vector.tensor_scalar(out=t1[:, N], in0=ang[:, N], scalar1=shift, scalar2=None, op0=ALU.add),
                  reads=[b_ang], writes=[b1])
            cx.op("dve", lambda: nc.vector.tensor_scalar(out=t2[:, N], in0=t1[:, N], scalar1=1.0 / (2 * PI), scalar2=None, op0=ALU.mult),
                  reads=[b1], writes=[b2])
            cx.op("dve", lambda: nc.vector.tensor_copy(out=ti[:, N], in_=t2[:, N]), reads=[b2], writes=[bi])
            cx.op("dve", lambda: nc.vector.tensor_copy(out=t2[:, N], in_=ti[:, N]), reads=[bi], writes=[b2])
            cx.op("dve", lambda: nc.vector.scalar_tensor_tensor(out=t1[:, N], in0=t2[:, N], scalar=-2 * PI, in1=t1[:, N], op0=ALU.mult, op1=ALU.add),
                  reads=[b1, b2], writes=[b1])
            cx.op("dve", lambda: nc.vector.tensor_scalar(out=t2[:, N], in0=t1[:, N], scalar1=PI, scalar2=None, op0=ALU.is_gt), reads=[b1], writes=[b2])
            cx.op("dve", lambda: nc.vector.scalar_tensor_tensor(out=t1[:, N], in0=t2[:, N], scalar=-2 * PI, in1=t1[:, N], op0=ALU.mult, op1=ALU.add),
                  reads=[b1, b2], writes=[b1])
            cx.op("dve", lambda: nc.vector.tensor_scalar(out=t2[:, N], in0=t1[:, N], scalar1=-PI, scalar2=None, op0=ALU.is_lt), reads=[b1], writes=[b2])
            cx.op("dve", lambda: nc.vector.scalar_tensor_tensor(out=t1[:, N], in0=t2[:, N], scalar=2 * PI, in1=t1[:, N], op0=ALU.mult, op1=ALU.add),
                  reads=[b1, b2], writes=[b1])
            cx.op("act", lambda: nc.scalar.activation(out=tabs[name][:, N], in_=t1[:, N], func=AF.Sin),
                  reads=[b1], writes=[tab_b[name]])


def emit_p1(k, x_all, xq, pos_all, pos_q, w_in, scr, n_all=NT_ALL, n_my=NT_MY, do_kv=True, do_q=True):
    nc, cx = k.nc, k.cx
    with ExitStack() as st:
        NSL = 3
        CW = 256
        slabs = [sb(k, f"p1_slab{i}", [128, 32, CW], BF16, st) for i in range(NSL)]
        slab_b = cx.bufs_n("p1_slab", NSL)
        hT = [sb(k, f"p1_hT{i}", [128, 32, TT], BF16, st) for i in range(2)]
        hT_b = cx.bufs_n("p1_hT", 2)
        xst = [sb(k, f"p1_x{i}", [128, D], F32, st) for i in range(2)]
        xst_b = cx.bufs_n("p1_x", 2)
        tabs = {n: [sb(k, f"p1_{n}{i}", [64, TT], F32, st) for i in range(2)] for n in ("cosP", "sinP", "cosM", "sinM")}
        tab_b = {n: cx.bufs_n("p1_" + n, 2) for n in tabs}
        tmp = [sb(k, f"p1_tmp{i}", [64, TT], F32, st) for i in range(4)]
        tmp_b = cx.bufs_n("p1_tmp", 4)
        k.p1_ti = sb(k, "p1_ti", [64, TT], I32, st)
        k.p1_ti_b = cx.buf("p1_ti")
        posi = sb(k, "p1_posi", [64, TT], I32, st)
        posi_b = cx.buf("p1_posi")
        NSTG = 4
        stg = [sb(k, f"p1_stg{i}", [128, TT], BF16, st) for i in range(NSTG)]
        stg_b = cx.bufs_n("p1_stg", NSTG)
        stgf = [sb(k, f"p1_stgf{i}", [128, TT], F32, st) for i in range(2)]
        stgf_b = cx.bufs_n("p1_stgf", 2)
        qf = [sb(k, f"p1_qf{i}", [64, TT], F32, st) for i in range(2)]
        qf_b = cx.bufs_n("p1_qf", 2)
        rt = [sb(k, f"p1_rt{i}", [64, TT], F32, st) for i in range(2)]
        rt_b = cx.bufs_n("p1_rt", 2)
        ptr = [ps(k, f"p1_ptr{i}", [128, 512], F32, st) for i in range(2)]
        ptr_b = cx.bufs_n("p1_ptr", 2)
        pacc = [ps(k, f"p1_pacc{i}", [128, 512], F32, st) for i in range(4)]
        pacc_b = cx.bufs_n("p1_pacc", 4)
        ppar = [ps(k, f"p1_ppar{i}", [64, 512], F32, st) for i in range(2)]
        ppar_b = cx.bufs_n("p1_ppar", 2)
        cnt = {"x": 0, "stg": 0, "stgf": 0, "qf": 0, "pacc": 0, "ppar": 0, "sq": 0, "slab": 0, "ptr": 0}

        def nxt(name, n):
            i = cnt[name] % n
            cnt[name] += 1
            return i

        def build_hT(src, row0, slot):
            for tb in range(4):
                xi = nxt("x", 2)
                cx.dma("sp", xst[xi][:], src[row0 + tb * 128: row0 + (tb + 1) * 128, :], xst_b[xi], writes=[xst_b[xi]])
                for j4 in range(8):
                    pi = nxt("ptr", 2)
                    cx.op("pe", [(lambda i=i: nc.tensor.transpose(
                        ptr[pi][:, i * 128:(i + 1) * 128], xst[xi][:, (j4 * 4 + i) * 128:(j4 * 4 + i + 1) * 128], k.ident_f[:]))
                        for i in range(4)], reads=[xst_b[xi], k.b_const], writes=[ptr_b[pi]])
                    for i in range(4):
                        j = j4 * 4 + i
                        cx.op("act", lambda: nc.scalar.activation(
                            out=hT[slot][:, j, tb * 128:(tb + 1) * 128], in_=ptr[pi][:, i * 128:(i + 1) * 128],
                            func=AF.Identity, scale=k.modT[:, 32 + j:33 + j], bias=k.modT[:, j:j + 1]),
                            reads=[ptr_b[pi], k.b_modT], writes=[hT_b[slot]])

        def load_slab(c0, n):
            s = nxt("slab", NSL)
            cx.dma("pool", slabs[s][:, :, 0:n], w_in[W_IN_SLAB_INDEX[(c0, n)], :, :, 0:n], slab_b[s], writes=[slab_b[s]])
            return s

        def proj_fm(s, off, M, slot):
            pi = nxt("pacc", 4)
            cx.op("pe", [(lambda j=j: nc.tensor.matmul(pacc[pi][0:M, :], slabs[s][:, j, off:off + M], hT[slot][:, j, :],
                                                        start=(j == 0), stop=(j == 31))) for j in range(32)],
                  reads=[slab_b[s], hT_b[slot]], writes=[pacc_b[pi]])
            return pi

        def proj_tm(s, n, slot, tb):
            pi = nxt("pacc", 4)
            cx.op("pe", [(lambda j=j: nc.tensor.matmul(pacc[pi][:, 0:n], hT[slot][:, j, tb * 128:(tb + 1) * 128], slabs[s][:, j, 0:n],
                                                        start=(j == 0), stop=(j == 31))) for j in range(32)],
                  reads=[slab_b[s], hT_b[slot]], writes=[pacc_b[pi]])
            return pi

        def rope_evac(pi, M, R, cosn, sinn, prot, ti, dst):
            si = nxt("stg", NSTG)
            qi = nxt("qf", 2)
            cx.op("act", lambda: nc.scalar.copy(out=stg[si][0:M, :], in_=pacc[pi][0:M, :]),
                  reads=[pacc_b[pi]], writes=[stg_b[si]])
            cx.op("act", lambda: nc.scalar.copy(out=qf[qi][0:R, :], in_=pacc[pi][0:R, :]),
                  reads=[pacc_b[pi]], writes=[qf_b[qi]])
            pp = nxt("ppar", 2)
            cx.op("pe", lambda: nc.tensor.matmul(ppar[pp][0:R, :], prot[0:R, 0:R], qf[qi][0:R, :], start=True, stop=True),
                  reads=[qf_b[qi], k.b_const], writes=[ppar_b[pp]])
            cx.op("dve", lambda: nc.vector.tensor_tensor(out=rt[0][0:R, :], in0=qf[qi][0:R, :], in1=tabs[cosn][ti][0:R, :], op=ALU.mult),
                  reads=[qf_b[qi], tab_b[cosn][ti]], writes=[rt_b[0]])
            cx.op("dve", lambda: nc.vector.tensor_tensor(out=rt[1][0:R, :], in0=ppar[pp][0:R, :], in1=tabs[sinn][ti][0:R, :], op=ALU.mult),
                  reads=[ppar_b[pp], tab_b[sinn][ti]], writes=[rt_b[1]])
            cx.op("dve", lambda: nc.vector.tensor_tensor(out=stg[si][0:R, :], in0=rt[0][0:R, :], in1=rt[1][0:R, :], op=ALU.add),
                  reads=[rt_b[0], rt_b[1]], writes=[stg_b[si]])
            cx.dma("sp", dst, stg[si][0:M, :], stg_b[si], reads=[stg_b[si]])

        def plain_evac(pi, M, dst):
            si = nxt("stg", NSTG)
            cx.op("act", lambda: nc.scalar.copy(out=stg[si][0:M, :], in_=pacc[pi][0:M, :]),
                  reads=[pacc_b[pi]], writes=[stg_b[si]])
            cx.dma("sp", dst, stg[si][0:M, :], stg_b[si], reads=[stg_b[si]])

        def plain_evac_f32(pi, M, dst, func=None):
            fi = nxt("stgf", 2)
            if func is None:
                cx.op("act", lambda: nc.scalar.copy(out=stgf[fi][0:M, :], in_=pacc[pi][0:M, :]),
                      reads=[pacc_b[pi]], writes=[stgf_b[fi]])
            else:
                cx.op("act", lambda: nc.scalar.activation(out=stgf[fi][0:M, :], in_=pacc[pi][0:M, :], func=func),
                      reads=[pacc_b[pi]], writes=[stgf_b[fi]])
            cx.dma("sp", dst, stgf[fi][0:M, :], stgf_b[fi], reads=[stgf_b[fi]])

        if do_kv:
            for pair in range(0, n_all, 2):
                tiles = [t for t in (pair, pair + 1) if t < n_all]
                for li, t in enumerate(tiles):
                    build_hT(x_all, t * TT, li)
                    rope_tables(k, st, pos_all[0:1, t * TT:(t + 1) * TT], TT,
                                {n: tabs[n][li] for n in tabs}, {n: tab_b[n][li] for n in tabs}, tmp, tmp_b, posi, posi_b)
                for kind in range(6):
                    for gp in range(2):
                        c0 = 2048 + (kind * 4 + gp * 2) * 128
                        s = load_slab(c0, 256)
                        for li, t in enumerate(tiles):
                            if kind in (0, 2, 4):
                                dst = {0: scr["kcmpT"], 2: scr["kselT"], 4: scr["kwinT"]}[kind]
                                for gi in range(2):
                                    pi = proj_fm(s, gi * 128, 128, li)
                                    rope_evac(pi, 128, 32, "cosP", "sinP", k.prot_p, li, dst[gp * 2 + gi, :, t * TT:(t + 1) * TT])
                            elif kind == 1:
                                for gi in range(2):
                                    pi = proj_fm(s, gi * 128, 128, li)
                                    plain_evac(pi, 128, scr["vcmpT"][gp * 2 + gi, :, t * TT:(t + 1) * TT])
                            else:
                                dst = scr["vsel"] if kind == 3 else scr["vwin"]
                                for tb in range(4):
                                    pi = proj_tm(s, 256, li, tb)
                                    si = nxt("stg", NSTG)
                                    cx.op("act", lambda: nc.scalar.copy(out=stg[si][:, 0:256], in_=pacc[pi][:, 0:256]),
                                          reads=[pacc_b[pi]], writes=[stg_b[si]])
                                    r0 = t * TT + tb * 128
                                    cx.dma("sp", dst[r0:r0 + 128, gp * 256:(gp + 1) * 256], stg[si][:, 0:256], stg_b[si], reads=[stg_b[si]])
                for half in range(2):
                    s2 = load_slab(6704 + half * 256, 256)
                    for li, t in enumerate(tiles):
                        for ci in range(2):
                            pi = proj_fm(s2, ci * 128, 128, li)
                            plain_evac_f32(pi, 128, scr["ckvf"][half * 2 + ci, :, t * TT:(t + 1) * TT])
                s3 = load_slab(7216, 64)
                for li, t in enumerate(tiles):
                    pi = proj_fm(s3, 0, 64, li)
                    rope_evac(pi, 64, 64, "cosM", "sinM", k.prot_m, li, scr["kpeT"][:, t * TT:(t + 1) * TT])
        if do_q:
            for pair in range(0, n_my, 2):
                tiles = [t for t in (pair, pair + 1) if t < n_my]
                for li, t in enumerate(tiles):
                    build_hT(xq, t * TT, li)
                    rope_tables(k, st, pos_q[0:1, t * TT:(t + 1) * TT], TT,
                                {n: tabs[n][li] for n in tabs}, {n: tab_b[n][li] for n in tabs}, tmp, tmp_b, posi, posi_b)
                for hp in range(8):
                    s = load_slab(hp * 256, 256)
                    for li, t in enumerate(tiles):
                        for hi in range(2):
                            pi = proj_fm(s, hi * 128, 128, li)
                            rope_evac(pi, 128, 32, "cosP", "sinP", k.prot_p, li, scr["qT"][hp * 2 + hi, :, t * TT:(t + 1) * TT])
                s = load_slab(5120, 48)
                for li, t in enumerate(tiles):
                    for tb in range(4):
                        pi = proj_tm(s, 48, li, tb)
                        fi = nxt("stgf", 2)
                        cx.op("act", lambda: nc.scalar.activation(out=stgf[fi][:, 0:48], in_=pacc[pi][:, 0:48], func=AF.Sigmoid),
                              reads=[pacc_b[pi]], writes=[stgf_b[fi]])
                        r0 = t * TT + tb * 128
                        cx.dma("sp", scr["gtok"][r0:r0 + 128, :], stgf[fi][:, 0:48], stgf_b[fi], reads=[stgf_b[fi]])
                for sl in range(6):
                    s = load_slab(5168 + sl * 256, 256)
                    for li, t in enumerate(tiles):
                        for ci in range(2):
                            pi = proj_fm(s, ci * 128, 128, li)
                            plain_evac_f32(pi, 128, scr["cqf"][sl * 2 + ci, :, t * TT:(t + 1) * TT])
        cx.barrier()


class Attn:
    def __init__(self, k, st, pfx, extra_bank=False):
        cx = k.cx
        self.k = k
        self.pS = [ps(k, f"{pfx}_pS{i}", [128, 512], F32, st) for i in range(2)]
        self.pS_b = cx.bufs_n(pfx + "_pS", 2)
        self.pT = [ps(k, f"{pfx}_pT{i}", [128, 512], BF16, st) for i in range(2)]
        self.pT_b = cx.bufs_n(pfx + "_pT", 2)
        self.pObank = ps(k, f"{pfx}_pO", [128, 512], F32, st)
        self.pO = [self.pObank[:, i * 128:(i + 1) * 128] for i in range(2)]
        self.pO_b = cx.bufs_n(pfx + "_pO", 2)
        if extra_bank:
            self.pXbank = [ps(k, f"{pfx}_pX{i}", [128, 512], F32, st) for i in range(2)]
            self.pX = [self.pXbank[i][:, 0:64] for i in range(2)]
        else:
            self.pX = [None, None]
        self.pX_b = cx.bufs_n(pfx + "_pX", 2)
        self.Sm = [sb(k, f"{pfx}_Sm{i}", [128, 512], F32, st) for i in range(2)]
        self.Sm_b = cx.bufs_n(pfx + "_Sm", 2)
        self.P = [sb(k, f"{pfx}_P{i}", [128, 512], BF16, st) for i in range(2)]
        self.P_b = cx.bufs_n(pfx + "_P", 2)
        self.PT = [sb(k, f"{pfx}_PT{i}", [128, 512], BF16, st) for i in range(2)]
        self.PT_b = cx.bufs_n(pfx + "_PT", 2)
        self.ls = [sb(k, f"{pfx}_ls{i}", [128, 16], F32, st) for i in range(4)]
        self.ls_b = cx.bufs_n(pfx + "_ls", 4)
        self.rl = sb(k, pfx + "_rl", [128, 4], F32, st)
        self.rl_b = cx.buf(pfx + "_rl")
        self.jobs = []
        self.nrun = 0
        self.done_t = 0
        self.done_pv = 0

    def run(self, qk_fns, qk_reads, ntiles, width, masks, scale, v_fn, v_reads, finish, x_fn=None, x_reads=(), copy_eng="dve"):
        r = self.nrun
        self.nrun += 1
        for kt in range(ntiles):
            self.jobs.append(dict(run=r, kt=kt, nt=ntiles, width=width, qk=qk_fns(kt), qk_reads=qk_reads, masks=masks(kt), scale=scale,
                                  v_fn=v_fn, v_reads=v_reads, finish=finish, x_fn=x_fn, x_reads=list(x_reads), copy_eng=copy_eng))
            self._step()

    def _qk(self, i):
        k = self.k
        nc, cx = k.nc, k.cx
        jb = self.jobs[i]
        b = i % 2
        w = jb["width"]
        pS = self.pS[b]
        cx.op("pe", [(lambda f=f: f(pS[:, 0:w])) for f in jb["qk"]], reads=jb["qk_reads"], writes=[self.pS_b[b]])
        src, src_b = pS, self.pS_b[b]
        for (map_, mreads) in jb["masks"]:
            o_ap, i_ap = self.Sm[b][:, 0:w], src[:, 0:w]
            if len(map_.shape) == 3:
                o_ap = o_ap.rearrange("p (a b) -> p a b", b=map_.shape[2])
                i_ap = i_ap.rearrange("p (a b) -> p a b", b=map_.shape[2])
            cx.op("dve", lambda: nc.vector.tensor_tensor(out=o_ap, in0=i_ap, in1=map_, op=ALU.add),
                  reads=[src_b] + mreads, writes=[self.Sm_b[b]])
            src, src_b = self.Sm[b], self.Sm_b[b]
        li = jb["run"] % 4
        kt = jb["kt"]
        cx.op("act", lambda: nc.scalar.activation(out=self.P[b][:, 0:w], in_=src[:, 0:w], func=AF.Exp, scale=jb["scale"],
                                                  accum_out=self.ls[li][:, kt:kt + 1]),
              reads=[src_b], writes=[self.P_b[b], self.ls_b[li]])

    def _t(self, i):
        k = self.k
        nc, cx = k.nc, k.cx
        jb = self.jobs[i]
        b = i % 2
        w = jb["width"]
        nkb = w // 128
        cx.op("pe", [(lambda kb=kb: nc.tensor.transpose(self.pT[b][:, kb * 128:(kb + 1) * 128], self.P[b][:, kb * 128:(kb + 1) * 128], k.ident_b[:]))
                     for kb in range(nkb)], reads=[self.P_b[b], k.b_const], writes=[self.pT_b[b]])
        if jb["copy_eng"] == "act":
            cx.op("act", lambda: nc.scalar.copy(out=self.PT[b][:, 0:w], in_=self.pT[b][:, 0:w]), reads=[self.pT_b[b]], writes=[self.PT_b[b]])
        else:
            cx.op("dve", lambda: nc.vector.tensor_copy(out=self.PT[b][:, 0:w], in_=self.pT[b][:, 0:w]), reads=[self.pT_b[b]], writes=[self.PT_b[b]])

    def _pv(self, i):
        k = self.k
        nc, cx = k.nc, k.cx
        jb = self.jobs[i]
        b = i % 2
        w = jb["width"]
        nkb = w // 128
        oi = jb["run"] % 2
        kt, nt = jb["kt"], jb["nt"]
        cx.op("pe", [(lambda kb=kb: nc.tensor.matmul(self.pO[oi], self.PT[b][:, kb * 128:(kb + 1) * 128], jb["v_fn"](kt, kb),
                                                      start=(kt == 0 and kb == 0), stop=(kt == nt - 1 and kb == nkb - 1)))
                     for kb in range(nkb)], reads=[self.PT_b[b]] + jb["v_reads"], writes=[self.pO_b[oi]])
        if jb["x_fn"] is not None and os.environ.get("P4_X", "1") == "1":
            cx.op("pe", [(lambda kb=kb: nc.tensor.matmul(self.pX[oi], self.PT[b][:, kb * 128:(kb + 1) * 128], jb["x_fn"](kb),
                                                          start=(kb == 0), stop=(kb == nkb - 1)))
                         for kb in range(nkb)], reads=[self.PT_b[b]] + jb["x_reads"], writes=[self.pX_b[oi]])
        if kt == nt - 1:
            jb["finish"](oi, jb["run"] % 4)
        self.jobs[i] = None

    def _step(self):
        i = len(self.jobs) - 1
        self._qk(i)
        if i - 1 >= 0:
            self._t(i - 1)
            self.done_t = i
        if i - 2 >= 0:
            self._pv(i - 2)
            self.done_pv = i - 1

    def drain(self):
        n = len(self.jobs)
        for i in range(self.done_t, n):
            self._t(i)
        for i in range(self.done_pv, n):
            self._pv(i)
        self.jobs = []
        self.done_t = 0
        self.done_pv = 0

    def rsum(self, li, ntiles, col):
        k = self.k
        nc, cx = k.nc, k.cx
        cx.op("dve", lambda: nc.vector.tensor_reduce(out=self.rl[:, col:col + 1], in_=self.ls[li][:, 0:ntiles], axis=AX.X, op=ALU.add),
              reads=[self.ls_b[li]], writes=[self.rl_b])
        cx.op("dve", lambda: nc.vector.tensor_scalar(out=self.rl[:, col:col + 1], in0=self.rl[:, col:col + 1], scalar1=1e-30, scalar2=None, op0=ALU.max),
              reads=[self.rl_b], writes=[self.rl_b])
        cx.op("dve", lambda: nc.vector.reciprocal(out=self.rl[:, col:col + 1], in_=self.rl[:, col:col + 1]),
              reads=[self.rl_b], writes=[self.rl_b])


class AttnSync:
    def __init__(self, k, st, pfx):
        cx = k.cx
        self.k = k
        self.pS = [ps(k, f"{pfx}_pS{i}", [128, 512], F32, st) for i in range(2)]
        self.pS_b = cx.bufs_n(pfx + "_pS", 2)
        self.pT = [ps(k, f"{pfx}_pT{i}", [128, 512], BF16, st) for i in range(2)]
        self.pT_b = cx.bufs_n(pfx + "_pT", 2)
        self.pObank = ps(k, f"{pfx}_pO", [128, 512], F32, st)
        self.pO = [self.pObank[:, i * 128:(i + 1) * 128] for i in range(2)]
        self.pO_b = cx.bufs_n(pfx + "_pO", 2)
        self.Sm = [sb(k, f"{pfx}_Sm{i}", [128, 512], F32, st) for i in range(2)]
        self.Sm_b = cx.bufs_n(pfx + "_Sm", 2)
        self.P = [sb(k, f"{pfx}_P{i}", [128, 512], BF16, st) for i in range(2)]
        self.P_b = cx.bufs_n(pfx + "_P", 2)
        self.PT = [sb(k, f"{pfx}_PT{i}", [128, 512], BF16, st) for i in range(2)]
        self.PT_b = cx.bufs_n(pfx + "_PT", 2)
        self.ls = [sb(k, f"{pfx}_ls{i}", [128, 16], F32, st) for i in range(2)]
        self.ls_b = cx.bufs_n(pfx + "_ls", 2)
        self.rl = sb(k, pfx + "_rl", [128, 4], F32, st)
        self.rl_b = cx.buf(pfx + "_rl")
        self.c = {"S": 0, "T": 0, "O": 0, "Sm": 0, "P": 0, "PT": 0, "ls": 0}

    def nx(self, n, m=2):
        i = self.c[n] % m
        self.c[n] += 1
        return i

    def run(self, qk_fns, qk_reads, ntiles, width, masks, scale, v_fn, v_reads):
        k = self.k
        nc, cx = k.nc, k.cx
        oi = self.nx("O")
        li = self.nx("ls")
        nkb = width // 128
        for kt in range(ntiles):
            si = self.nx("S")
            pS = self.pS[si]
            cx.op("pe", [(lambda f=f: f(pS[:, 0:width])) for f in qk_fns(kt)], reads=qk_reads, writes=[self.pS_b[si]])
            src, src_b = pS, self.pS_b[si]
            ml = masks(kt)
            if ml:
                mi = self.nx("Sm")
                for (map_, mreads) in ml:
                    o_ap, i_ap = self.Sm[mi][:, 0:width], src[:, 0:width]
                    if len(map_.shape) == 3:
                        o_ap = o_ap.rearrange("p (a b) -> p a b", b=map_.shape[2])
                        i_ap = i_ap.rearrange("p (a b) -> p a b", b=map_.shape[2])
                    cx.op("dve", lambda: nc.vector.tensor_tensor(out=o_ap, in0=i_ap, in1=map_, op=ALU.add),
                          reads=[src_b] + mreads, writes=[self.Sm_b[mi]])
                    src, src_b = self.Sm[mi], self.Sm_b[mi]
            pi = self.nx("P")
            cx.op("act", lambda: nc.scalar.activation(out=self.P[pi][:, 0:width], in_=src[:, 0:width], func=AF.Exp, scale=scale,
                                                      accum_out=self.ls[li][:, kt:kt + 1]),
                  reads=[src_b], writes=[self.P_b[pi], self.ls_b[li]])
            ti = self.nx("T")
            cx.op("pe", [(lambda kb=kb: nc.tensor.transpose(self.pT[ti][:, kb * 128:(kb + 1) * 128], self.P[pi][:, kb * 128:(kb + 1) * 128], k.ident_b[:]))
                         for kb in range(nkb)], reads=[self.P_b[pi], k.b_const], writes=[self.pT_b[ti]])
            pti = self.nx("PT")
            cx.op("dve", lambda: nc.vector.tensor_copy(out=self.PT[pti][:, 0:width], in_=self.pT[ti][:, 0:width]),
                  reads=[self.pT_b[ti]], writes=[self.PT_b[pti]])
            cx.op("pe", [(lambda kb=kb: nc.tensor.matmul(self.pO[oi], self.PT[pti][:, kb * 128:(kb + 1) * 128], v_fn(kt, kb),
                                                          start=(kt == 0 and kb == 0), stop=(kt == ntiles - 1 and kb == nkb - 1)))
                         for kb in range(nkb)], reads=[self.PT_b[pti]] + v_reads, writes=[self.pO_b[oi]])
        return oi, li

    def rsum(self, li, ntiles, col):
        k = self.k
        nc, cx = k.nc, k.cx
        cx.op("dve", lambda: nc.vector.tensor_reduce(out=self.rl[:, col:col + 1], in_=self.ls[li][:, 0:ntiles], axis=AX.X, op=ALU.add),
              reads=[self.ls_b[li]], writes=[self.rl_b])
        cx.op("dve", lambda: nc.vector.tensor_scalar(out=self.rl[:, col:col + 1], in0=self.rl[:, col:col + 1], scalar1=1e-30, scalar2=None, op0=ALU.max),
              reads=[self.rl_b], writes=[self.rl_b])
        cx.op("dve", lambda: nc.vector.reciprocal(out=self.rl[:, col:col + 1], in_=self.rl[:, col:col + 1]),
              reads=[self.rl_b], writes=[self.rl_b])


class OutT:
    def __init__(self, k, st, pfx, nstg=2):
        cx = k.cx
        self.k = k
        self.ob = [sb(k, f"{pfx}_ob{i}", [128, 128], BF16, st) for i in range(2)]
        self.ob_b = cx.bufs_n(pfx + "_ob", 2)
        self.ptbank = ps(k, f"{pfx}_opt", [128, 512], BF16, st)
        self.pt = [self.ptbank[:, i * 128:(i + 1) * 128] for i in range(2)]
        self.pt_b = cx.bufs_n(pfx + "_opt", 2)
        self.stg = [sb(k, f"{pfx}_ostg{i}", [128, 512], BF16, st) for i in range(nstg)]
        self.stg_b = cx.bufs_n(pfx + "_ostg", nstg)
        self.n = 0

    def put(self, src_ap, src_reads, qb, si):
        k = self.k
        nc, cx = k.nc, k.cx
        i = self.n % 2
        self.n += 1
        cx.op("act", lambda: nc.scalar.copy(out=self.ob[i][:], in_=src_ap), reads=src_reads, writes=[self.ob_b[i]])
        cx.op("pe", lambda: nc.tensor.transpose(self.pt[i], self.ob[i][:], k.ident_b[:]), reads=[self.ob_b[i], k.b_const], writes=[self.pt_b[i]])
        cx.op("act", lambda: nc.scalar.copy(out=self.stg[si][:, qb * 128:(qb + 1) * 128], in_=self.pt[i]),
              reads=[self.pt_b[i]], writes=[self.stg_b[si]])

    def flush(self, si, dst):
        self.k.cx.dma("sp", dst, self.stg[si][:], self.stg_b[si], reads=[self.stg_b[si]])


def emit_p2(k, scr, w_uq, w_ukv, qnT_d, kvnT_d, pos_q, mA_d, mB_d, n_all=NT_ALL, n_my=NT_MY, heads=range(16)):
    nc, cx = k.nc, k.cx
    SC = 192.0 ** -0.5
    with ExitStack() as st:
        ckvn = sb(k, "p2_ckvn", [128, 4, n_all * TT], BF16, st)
        ckvn_b = cx.buf("p2_ckvn")
        kpe = sb(k, "p2_kpe", [64, n_all * TT], BF16, st)
        kpe_b = cx.buf("p2_kpe")
        qnT = sb(k, "p2_qnT", [128, 12], F32, st)
        kvnT = sb(k, "p2_kvnT", [128, 4], F32, st)
        mA = sb(k, "p2_mA", [128, 4, 512], F32, st)
        mB = sb(k, "p2_mB", [128, 4, 512], F32, st)
        b_c = cx.buf("p2_c")
        for dst, src in ((qnT, qnT_d), (kvnT, kvnT_d), (mA, mA_d), (mB, mB_d)):
            cx.dma("sp", dst[:], src, b_c, writes=[b_c])
        cx.dma("sp", kpe[:], scr["kpeT"][:, 0:n_all * TT], kpe_b, writes=[kpe_b])
        with ExitStack() as st2:
            ld = [sb(k, f"p2_ld{i}", [128, 12, TT], F32, st2) for i in range(2)]
            ld_b = cx.bufs_n("p2_ld", 2)
            sq = [sb(k, f"p2_sq{i}", [128, TT], BF16, st2) for i in range(2)]
            sq_b = cx.bufs_n("p2_sq", 2)
            rs = sb(k, "p2_rs", [128, TT], F32, st2)
            rs_b = cx.buf("p2_rs")
            cqn = [sb(k, f"p2_cqn{i}", [128, 12, TT], BF16, st2) for i in range(2)]
            cqn_b = cx.bufs_n("p2_cqn", 2)
            pss = ps(k, "p2_pss", [128, 512], F32, st2)
            pss_b = cx.buf("p2_pss")
            wq = [sb(k, f"p2_wq{i}", [128, 12, 192], BF16, st2) for i in range(2)]
            wq_b = cx.bufs_n("p2_wq", 2)
            pq = [ps(k, f"p2_pq{i}", [128, 512], F32, st2) for i in range(2)]
            pq_b = cx.bufs_n("p2_pq", 2)
            pp = [ps(k, f"p2_pp{i}", [64, 512], F32, st2) for i in range(2)]
            pp_b = cx.bufs_n("p2_pp", 2)
            pqe = [ps(k, f"p2_pqe{i}", [64, 512], F32, st2) for i in range(2)]
            pqe_b = cx.bufs_n("p2_pqe", 2)
            stg = [sb(k, f"p2_stg{i}", [128, TT], BF16, st2) for i in range(4)]
            stg_b = cx.bufs_n("p2_stg", 4)
            qf = [sb(k, f"p2_qf{i}", [64, TT], F32, st2) for i in range(2)]
            qf_b = cx.bufs_n("p2_qf", 2)
            rt = [sb(k, f"p2_rt{i}", [64, TT], F32, st2) for i in range(2)]
            rt_b = cx.bufs_n("p2_rt", 2)
            tabs = {n: sb(k, f"p2_{n}", [64, TT], F32, st2) for n in ("cosP", "sinP", "cosM", "sinM")}
            tab_b = {n: cx.buf("p2_" + n) for n in tabs}
            tmp = [sb(k, f"p2_tmp{i}", [64, TT], F32, st2) for i in range(4)]
            tmp_b = cx.bufs_n("p2_tmp", 4)
            posi = sb(k, "p2_posi", [64, TT], I32, st2)
            posi_b = cx.buf("p2_posi")
            k.p1_ti = sb(k, "p2_ti", [64, TT], I32, st2)
            k.p1_ti_b = cx.buf("p2_ti")
            cn = {"ld": 0, "sq": 0, "stg": 0}

            def rstd(src_d, nch, tok0, width, li):
                cx.dma("sp", ld[li][:, 0:nch, :], src_d[:, :, tok0:tok0 + TT].rearrange("c p t -> p c t"), ld_b[li], writes=[ld_b[li]])
                for ci in range(nch):
                    qi = cn["sq"] % 2
                    cn["sq"] += 1
                    cx.op("act", lambda: nc.scalar.activation(out=sq[qi][:], in_=ld[li][:, ci, :], func=AF.Square),
                          reads=[ld_b[li]], writes=[sq_b[qi]])
                    cx.op("pe", lambda: nc.tensor.matmul(pss[:], k.ones_bf[:], sq[qi][:], start=(ci == 0), stop=(ci == nch - 1)),
                          reads=[sq_b[qi], k.b_const], writes=[pss_b])
                cx.op("act", lambda: nc.scalar.activation(out=rs[:], in_=pss[:], func=AF.Sqrt, scale=1.0 / width, bias=k.eps6[:]),
                      reads=[pss_b, k.b_const], writes=[rs_b])
                cx.op("dve", lambda: nc.vector.reciprocal(out=rs[:], in_=rs[:]), reads=[rs_b], writes=[rs_b])

            for t in range(n_all):
                li = cn["ld"] % 2
                cn["ld"] += 1
                rstd(scr["ckvf"], 4, t * TT, 512.0, li)
                for ci in range(4):
                    cx.op("dve", lambda: nc.vector.scalar_tensor_tensor(out=ckvn[:, ci, t * TT:(t + 1) * TT], in0=ld[li][:, ci, :], scalar=kvnT[:, ci:ci + 1],
                                                                       in1=rs[:], op0=ALU.mult, op1=ALU.mult),
                          reads=[ld_b[li], rs_b, b_c], writes=[ckvn_b])
            for m in range(n_my):
                li = cn["ld"] % 2
                cn["ld"] += 1
                rstd(scr["cqf"], 12, m * TT, 1536.0, li)
                ci_ = m % 2
                for ci in range(12):
                    cx.op("dve", lambda: nc.vector.scalar_tensor_tensor(out=cqn[ci_][:, ci, :], in0=ld[li][:, ci, :], scalar=qnT[:, ci:ci + 1],
                                                                       in1=rs[:], op0=ALU.mult, op1=ALU.mult),
                          reads=[ld_b[li], rs_b, b_c], writes=[cqn_b[ci_]])
                rope_tables(k, st2, pos_q[0:1, m * TT:(m + 1) * TT], TT, tabs, tab_b, tmp, tmp_b, posi, posi_b)
                for h in heads:
                    wi = h % 2
                    cx.dma("pool", wq[wi][:], w_uq[h], wq_b[wi], writes=[wq_b[wi]])
                    qi = h % 2
                    cx.op("pe", [(lambda j=j: nc.tensor.matmul(pq[qi][:], wq[wi][:, j, 0:128], cqn[ci_][:, j, :], start=(j == 0), stop=(j == 11)))
                                 for j in range(12)], reads=[wq_b[wi], cqn_b[ci_]], writes=[pq_b[qi]])
                    si = cn["stg"] % 4
                    cn["stg"] += 1
                    cx.op("act", lambda: nc.scalar.copy(out=stg[si][:], in_=pq[qi][:]), reads=[pq_b[qi]], writes=[stg_b[si]])
                    cx.dma("sp", scr["qmnT"][h, :, m * TT:(m + 1) * TT], stg[si][:], stg_b[si], reads=[stg_b[si]])
                    cx.op("pe", [(lambda j=j: nc.tensor.matmul(pqe[qi][:], wq[wi][:, j, 128:192], cqn[ci_][:, j, :], start=(j == 0), stop=(j == 11)))
                                 for j in range(12)], reads=[wq_b[wi], cqn_b[ci_]], writes=[pqe_b[qi]])
                    cx.op("act", lambda: nc.scalar.copy(out=qf[qi][:], in_=pqe[qi][:]), reads=[pqe_b[qi]], writes=[qf_b[qi]])
                    cx.op("pe", lambda: nc.tensor.matmul(pp[qi][:], k.prot_m[0:64, 0:64], qf[qi][:], start=True, stop=True),
                          reads=[qf_b[qi], k.b_const], writes=[pp_b[qi]])
                    cx.op("dve", lambda: nc.vector.tensor_tensor(out=rt[0][:], in0=qf[qi][:], in1=tabs["cosM"][:], op=ALU.mult),
                          reads=[qf_b[qi], tab_b["cosM"]], writes=[rt_b[0]])
                    cx.op("dve", lambda: nc.vector.tensor_tensor(out=rt[1][:], in0=pp[qi][:], in1=tabs["sinM"][:], op=ALU.mult),
                          reads=[pp_b[qi], tab_b["sinM"]], writes=[rt_b[1]])
                    si = cn["stg"] % 4
                    cn["stg"] += 1
                    cx.op("dve", lambda: nc.vector.tensor_tensor(out=stg[si][0:64, :], in0=rt[0][:], in1=rt[1][:], op=ALU.add),
                          reads=[rt_b[0], rt_b[1]], writes=[stg_b[si]])
                    cx.dma("sp", scr["qmpT"][h, :, m * TT:(m + 1) * TT], stg[si][0:64, :], stg_b[si], reads=[stg_b[si]])
            cx.barrier()
        with ExitStack() as st3:
            at = Attn(k, st3, "p2a")
            ot = OutT(k, st3, "p2o")
            wkv = [sb(k, f"p2_wkv{i}", [128, 4, 256], BF16, st3) for i in range(2)]
            wkv_b = cx.bufs_n("p2_wkv", 2)
            KT = [sb(k, f"p2_KT{i}", [128, n_all * TT], BF16, st3) for i in range(2)]
            KT_b = cx.bufs_n("p2_KT", 2)
            V = [sb(k, f"p2_V{i}", [128, n_all * 4, 128], BF16, st3) for i in range(2)]
            V_b = cx.bufs_n("p2_V", 2)
            qn = [sb(k, f"p2_qn{i}", [128, n_my * TT], BF16, st3) for i in range(2)]
            qn_b = cx.bufs_n("p2_qn", 2)
            qp = [sb(k, f"p2_qp{i}", [64, n_my * TT], BF16, st3) for i in range(2)]
            qp_b = cx.bufs_n("p2_qp", 2)
            pk = [ps(k, f"p2_pk{i}", [128, 512], F32, st3) for i in range(2)]
            pk_b = cx.bufs_n("p2_pk", 2)
            on = sb(k, "p2_on", [128, 128], F32, st3)
            on_b = cx.buf("p2_on")
            for hi_, h in enumerate(heads):
                b = hi_ % 2
                cx.dma("pool", wkv[b][:], w_ukv[h], wkv_b[b], writes=[wkv_b[b]])
                cx.dma("sp", qn[b][:], scr["qmnT"][h, :, 0:n_my * TT], qn_b[b], writes=[qn_b[b]])
                cx.dma("sp", qp[b][:], scr["qmpT"][h, :, 0:n_my * TT], qp_b[b], writes=[qp_b[b]])
                for t in range(n_all):
                    cx.op("pe", [(lambda j=j: nc.tensor.matmul(pk[0][:], wkv[b][:, j, 0:128], ckvn[:, j, t * TT:(t + 1) * TT], start=(j == 0), stop=(j == 3)))
                                 for j in range(4)], reads=[wkv_b[b], ckvn_b], writes=[pk_b[0]])
                    cx.op("act", lambda: nc.scalar.copy(out=KT[b][:, t * TT:(t + 1) * TT], in_=pk[0][:]), reads=[pk_b[0]], writes=[KT_b[b]])
                    fns = []
                    for tb in range(4):
                        for j in range(4):
                            fns.append(lambda j=j, tb=tb: nc.tensor.matmul(pk[1][:, tb * 128:(tb + 1) * 128], ckvn[:, j, t * TT + tb * 128:t * TT + (tb + 1) * 128],
                                                                            wkv[b][:, j, 128:256], start=(j == 0), stop=(j == 3)))
                    cx.op("pe", fns, reads=[wkv_b[b], ckvn_b], writes=[pk_b[1]])
                    cx.op("dve", lambda: nc.vector.tensor_copy(out=V[b][:, t * 4:(t + 1) * 4, :], in_=pk[1][:].rearrange("p (a d) -> p a d", d=128)),
                          reads=[pk_b[1]], writes=[V_b[b]])
                for m in range(n_my):
                    si = (hi_ * n_my + m) % 2
                    for qb in range(4):
                        q0 = m * TT + qb * 128
                        nt = min(2 * m + 2, n_all)

                        def mk_qk(b=b, q0=q0):
                            def qk(kt):
                                return [lambda o: nc.tensor.matmul(o, qn[b][:, q0:q0 + 128], KT[b][:, kt * TT:(kt + 1) * TT], start=True, stop=False),
                                        lambda o: nc.tensor.matmul(o, qp[b][0:64, q0:q0 + 128], kpe[0:64, kt * TT:(kt + 1) * TT], start=False, stop=True)]
                            return qk

                        def mk_masks(m=m, qb=qb):
                            def masks(kt):
                                if kt == 2 * m:
                                    return [(mA[:, qb, :], [b_c])]
                                if kt == 2 * m + 1:
                                    return [(mB[:, qb, :], [b_c])]
                                return []
                            return masks

                        def mk_fin(nt=nt, qb=qb, si=si, h=h, m=m):
                            def fin(oi, li):
                                at.rsum(li, nt, 0)
                                cx.op("dve", lambda: nc.vector.tensor_scalar(out=on[:], in0=at.pO[oi], scalar1=at.rl[:, 0:1], scalar2=None, op0=ALU.mult),
                                      reads=[at.pO_b[oi], at.rl_b], writes=[on_b])
                                ot.put(on[:], [on_b], qb, si)
                                if qb == 3:
                                    ot.flush(si, scr["oT"][16 + h, :, m * TT:(m + 1) * TT])
                            return fin

                        at.run(mk_qk(), [qn_b[b], qp_b[b], KT_b[b], kpe_b], nt, 512, mk_masks(), SC,
                               (lambda b=b: (lambda kt, kb: V[b][:, kt * 4 + kb, :]))(), [V_b[b]], mk_fin())
            at.drain()
        cx.barrier()


def emit_p3(k, scr, k1_d, k2_d, v1_d, v2_d, posk_d, posv_d):
    nc, cx = k.nc, k.cx
    with ExitStack() as st:
        w1 = sb(k, "p3_w1", [128, 32, 256], BF16, st)
        w2 = sb(k, "p3_w2", [128, 2, 128], BF16, st)
        posT = sb(k, "p3_posT", [128, 32], F32, st)
        b_w = cx.buf("p3_w")
        src = [sb(k, f"p3_src{i}", [128, S], BF16, st) for i in range(2)]
        src_b = cx.bufs_n("p3_src", 2)
        kp = sb(k, "p3_kp", [128, 32, 256], BF16, st)
        kp_b = cx.buf("p3_kp")
        ppre = [ps(k, f"p3_ppre{i}", [128, 256], F32, st) for i in range(2)]
        ppre_b = cx.bufs_n("p3_ppre", 2)
        pout = ps(k, "p3_pout", [128, 256], F32, st)
        pout_b = cx.buf("p3_pout")
        xs = sb(k, "p3_xs", [128, 256], F32, st)
        xs_b = cx.buf("p3_xs")
        t1 = sb(k, "p3_t1", [128, 256], F32, st)
        t1_b = cx.buf("p3_t1")
        gl = sb(k, "p3_gl", [128, 2, 256], BF16, st)
        gl_b = cx.buf("p3_gl")
        cx.op("dve", lambda: nc.vector.memset(kp[:], 0.0), writes=[kp_b])
        cx.op("dve", lambda: nc.vector.memset(k.kcT[:], 0.0), writes=[k.kcv_b])
        cx.op("dve", lambda: nc.vector.memset(k.vc[:], 0.0), writes=[k.kcv_b])
        n_src = 0
        for which in range(2):
            w1d, w2d, posd, srcd = ((k1_d, k2_d, posk_d, scr["kcmpT"]), (v1_d, v2_d, posv_d, scr["vcmpT"]))[which]
            cx.dma("pool", w1[:], w1d, b_w, writes=[b_w])
            cx.dma("pool", w2[:], w2d, b_w, writes=[b_w])
            cx.dma("sp", posT[:], posd, b_w, writes=[b_w])
            for g in range(4):
                si = n_src % 2
                n_src += 1
                cx.dma("sp", src[si][:], srcd[g, :, :], src_b[si], writes=[src_b[si]])
                for l in range(32):
                    cx.op("dve", lambda: nc.vector.tensor_scalar(out=kp[:, l, 0:255], in0=src[si][:, l:l + 16 * 254 + 1:16], scalar1=posT[:, l:l + 1],
                                                                 scalar2=None, op0=ALU.add), reads=[src_b[si], b_w], writes=[kp_b])
                for hc in range(2):
                    pi = hc
                    cx.op("pe", [(lambda l=l: nc.tensor.matmul(ppre[pi][:], w1[:, l, hc * 128:(hc + 1) * 128], kp[:, l, :], start=(l == 0), stop=(l == 31)))
                                 for l in range(32)], reads=[b_w, kp_b], writes=[ppre_b[pi]])
                    cx.op("act", lambda: nc.scalar.copy(out=xs[:], in_=ppre[pi][:]), reads=[ppre_b[pi]], writes=[xs_b])
                    cx.op("dve", lambda: nc.vector.tensor_tensor(out=t1[:], in0=xs[:], in1=xs[:], op=ALU.mult), reads=[xs_b], writes=[t1_b])
                    cx.op("dve", lambda: nc.vector.tensor_scalar(out=t1[:], in0=t1[:], scalar1=0.044715, scalar2=1.0, op0=ALU.mult, op1=ALU.add),
                          reads=[t1_b], writes=[t1_b])
                    cx.op("dve", lambda: nc.vector.tensor_tensor(out=t1[:], in0=t1[:], in1=xs[:], op=ALU.mult), reads=[t1_b, xs_b], writes=[t1_b])
                    cx.op("act", lambda: nc.scalar.activation(out=t1[:], in_=t1[:], func=AF.Tanh, scale=0.7978845608028654), reads=[t1_b], writes=[t1_b])
                    cx.op("dve", lambda: nc.vector.tensor_scalar(out=t1[:], in0=t1[:], scalar1=0.5, scalar2=0.5, op0=ALU.mult, op1=ALU.add),
                          reads=[t1_b], writes=[t1_b])
                    cx.op("dve", lambda: nc.vector.tensor_tensor(out=gl[:, hc, :], in0=t1[:], in1=xs[:], op=ALU.mult), reads=[t1_b, xs_b], writes=[gl_b])
                if which == 0:
                    cx.op("pe", [(lambda hc=hc: nc.tensor.matmul(pout[:], w2[:, hc, :], gl[:, hc, :], start=(hc == 0), stop=(hc == 1))) for hc in range(2)],
                          reads=[b_w, gl_b], writes=[pout_b])
                    cx.op("act", lambda: nc.scalar.copy(out=k.kcT[:, g, 0:256], in_=pout[:]), reads=[pout_b], writes=[k.kcv_b])
                else:
                    for c in range(2):
                        cx.op("pe", [(lambda hc=hc: nc.tensor.matmul(pout[:, 0:128], gl[:, hc, c * 128:(c + 1) * 128], w2[:, hc, :], start=(hc == 0), stop=(hc == 1)))
                                     for hc in range(2)], reads=[b_w, gl_b], writes=[pout_b])
                        cx.op("act", lambda: nc.scalar.copy(out=k.vc[:, g, c, :], in_=pout[:, 0:128]), reads=[pout_b], writes=[k.kcv_b])
        cx.barrier()


def emit_p4(k, scr, cst, n_all=NT_ALL, n_my=NT_MY, groups=range(4), mode="cws"):
    nc, cx = k.nc, k.cx
    SC = 128.0 ** -0.5
    with ExitStack() as st:
        at = Attn(k, st, "p4a" + mode, extra_bank=True)
        ot = OutT(k, st, "p4o" + mode, 4)
        mA = sb(k, f"p4{mode}_mA", [128, 4, 512], F32, st)
        mB = sb(k, f"p4{mode}_mB", [128, 4, 512], F32, st)
        mW = sb(k, f"p4{mode}_mW", [128, 3, 4, 512], F32, st)
        mC = sb(k, f"p4{mode}_mC", [128, 4, 512], F32, st)
        vS = sb(k, f"p4{mode}_vS", [128, 4, 64], F32, st)
        aS = sb(k, f"p4{mode}_aS", [128, 4, 64], F32, st)
        ovl = sb(k, f"p4{mode}_ovl", [128, 4, 64], BF16, st)
        ovf = sb(k, f"p4{mode}_ovf", [128, 4, 64], F32, st)
        b_c = cx.buf(f"p4{mode}_c")
        b_cm = cx.buf(f"p4{mode}_cm")
        cx.dma("sp", mA[:], cst["mA"], b_c, writes=[b_c])
        cx.dma("sp", mB[:], cst["mB"], b_c, writes=[b_c])
        cx.dma("sp", mW[:], cst["mW"].rearrange("w p q k -> p w q k"), b_c, writes=[b_c])
        cx.dma("sp", ovf[:], cst["ovl"], b_c, writes=[b_c])
        cx.op("dve", lambda: nc.vector.tensor_copy(out=ovl[:], in_=ovf[:]), reads=[b_c], writes=[b_c])
        ksel = [sb(k, f"p4{mode}_ksel{i}", [128, n_all * TT], BF16, st) for i in range(1)]
        kwin = [sb(k, f"p4{mode}_kwin{i}", [128, n_all * TT], BF16, st) for i in range(1)]
        vsel = [sb(k, f"p4{mode}_vsel{i}", [128, n_all * 4, 128], BF16, st) for i in range(1)]
        vwin = [sb(k, f"p4{mode}_vwin{i}", [128, n_all * 4, 128], BF16, st) for i in range(1)]
        kv_b = cx.buf(f"p4{mode}_kv")
        qg = [sb(k, f"p4{mode}_q{i}", [128, 4, TT], BF16, st) for i in range(2)]
        qg_b = cx.bufs_n(f"p4{mode}_q", 2)
        gt = [sb(k, f"p4{mode}_gt{i}", [128, 4, 48], F32, st) for i in range(2)]
        gt_b = cx.bufs_n(f"p4{mode}_gt", 2)
        oacc_t = [sb(k, f"p4{mode}_oacc{i}", [128, 4, 128], F32, st) for i in range(2)]
        oacc_bt = cx.bufs_n(f"p4{mode}_oacc", 2)
        otmp = sb(k, f"p4{mode}_otmp", [128, 2, 128], F32, st)
        otmp_b = cx.bufs_n(f"p4{mode}_otmp", 2)
        imp = sb(k, f"p4{mode}_imp", [128, 64], F32, st)
        imp_b = cx.buf(f"p4{mode}_imp")
        wk = sb(k, f"p4{mode}_wk", [128, 64], F32, st)
        wk_b = cx.buf(f"p4{mode}_wk")
        m16 = sb(k, f"p4{mode}_m16", [128, 16], F32, st)
        m16_b = cx.buf(f"p4{mode}_m16")
        sneg_t = [sb(k, f"p4{mode}_sneg{i}", [128, 64], F32, st) for i in range(2)]
        sneg_bt = cx.bufs_n(f"p4{mode}_sneg", 2)
        nqb = 0
        gm = sb(k, f"p4{mode}_gm", [128, 4], F32, st)
        gm_b = cx.buf(f"p4{mode}_gm")
        vsel_v = scr["vsel"].rearrange("(b p) c -> p b c", p=128)
        vwin_v = scr["vwin"].rearrange("(b p) c -> p b c", p=128)
        it = 0
        for g in groups:
            nb = n_all * 4
            at.drain()
            cx.dma("sp", ksel[0][:], scr["kselT"][g, :, 0:n_all * TT], kv_b, writes=[kv_b])
            cx.dma("sp", kwin[0][:], scr["kwinT"][g, :, 0:n_all * TT], kv_b, writes=[kv_b])
            cx.dma("sp", vsel[0][:], vsel_v[:, 0:nb, g * 128:(g + 1) * 128], kv_b, writes=[kv_b])
            cx.dma("sp", vwin[0][:], vwin_v[:, 0:nb, g * 128:(g + 1) * 128], kv_b, writes=[kv_b])
            for m in range(n_my):
                at.drain()
                qi = it % 2
                it += 1
                cx.dma("sp", qg[qi][:], scr["qT"][g * 4:(g + 1) * 4, :, m * TT:(m + 1) * TT].rearrange("h p t -> p h t"), qg_b[qi], writes=[qg_b[qi]])
                cx.dma("sp", gt[qi][:], scr["gtok"][m * TT:(m + 1) * TT, :].rearrange("(b p) c -> p b c", p=128), gt_b[qi], writes=[gt_b[qi]])
                cx.dma("sp", mC[:], cst["mC"][m], b_cm, writes=[b_cm])
                cx.dma("sp", vS[:], cst["vS"][m], b_cm, writes=[b_cm])
                cx.dma("sp", aS[:], cst["aS"][m], b_cm, writes=[b_cm])
                for qb in range(4):
                    Q = slice(qb * 128, (qb + 1) * 128)
                    oacc, oacc_b = oacc_t[nqb % 2], oacc_bt[nqb % 2]
                    sneg, sneg_b = sneg_t[nqb % 2], sneg_bt[nqb % 2]
                    nqb += 1
                    r0_ = m * TT + qb * 128
                    if "c" not in mode:
                        cx.dma("sp", oacc[:], scr["ocmp"][g, r0_:r0_ + 128, :, :], oacc_b, writes=[oacc_b])
                        cx.dma("sp", sneg[:], scr["snegs"][g, r0_:r0_ + 128, :], sneg_b, writes=[sneg_b])
                    nt = min(2 * m + 2, n_all)
                    wt = [(w, 2 * m - 1 + w) for w in range(3) if 0 <= 2 * m - 1 + w < n_all]

                    def topk(qb=qb, sneg=sneg, sneg_b=sneg_b, oacc=oacc, oacc_b=oacc_b, g=g, r0_=r0_):
                        cx.op("dve", lambda: nc.vector.tensor_tensor(out=imp[:], in0=imp[:], in1=vS[:, qb, :], op=ALU.mult), reads=[imp_b, b_cm], writes=[imp_b])
                        cx.op("dve", lambda: nc.vector.tensor_tensor(out=imp[:], in0=imp[:], in1=aS[:, qb, :], op=ALU.add), reads=[imp_b, b_cm], writes=[imp_b])
                        cx.op("dve", lambda: nc.vector.max(out=m16[:, 0:8], in_=imp[:]), reads=[imp_b], writes=[m16_b])
                        cx.op("dve", lambda: nc.vector.match_replace(out=wk[:], in_to_replace=m16[:, 0:8], in_values=imp[:], imm_value=-3e38),
                              reads=[imp_b, m16_b], writes=[wk_b])
                        cx.op("dve", lambda: nc.vector.max(out=m16[:, 8:16], in_=wk[:]), reads=[wk_b], writes=[m16_b])
                        cx.op("dve", lambda: nc.vector.tensor_scalar(out=sneg[:], in0=imp[:], scalar1=m16[:, 15:16], scalar2=-30000.0, op0=ALU.is_lt, op1=ALU.mult),
                              reads=[imp_b, m16_b], writes=[sneg_b])
                        if "w" not in mode and os.environ.get("NOSTORE","0") != "1":
                            cx.dma("sp", scr["snegs"][g, r0_:r0_ + 128, :], sneg[:], sneg_b, reads=[sneg_b])
                            cx.dma("sp", scr["ocmp"][g, r0_:r0_ + 128, :, :], oacc[:], oacc_b, reads=[oacc_b])

                    BR = mode
                    for r in (range(4) if "c" in BR else []):
                        h = g * 4 + r

                        def mk_fin_c(r=r, h=h, qb=qb, qi=qi, topk=topk, oacc=oacc, oacc_b=oacc_b):
                            def fin(oi, li):
                                LV = int(os.environ.get("P4_FINLVL", "9"))
                                if LV < 1:
                                    return
                                at.rsum(li, 1, 0)
                                if LV < 2:
                                    return
                                if r == 0:
                                    cx.op("dve", lambda: nc.vector.tensor_scalar(out=imp[:], in0=at.pX[oi], scalar1=at.rl[:, 0:1], scalar2=None, op0=ALU.mult),
                                          reads=[at.pX_b[oi], at.rl_b], writes=[imp_b])
                                else:
                                    cx.op("dve", lambda: nc.vector.scalar_tensor_tensor(out=imp[:], in0=at.pX[oi], scalar=at.rl[:, 0:1], in1=imp[:], op0=ALU.mult, op1=ALU.add),
                                          reads=[at.pX_b[oi], at.rl_b, imp_b], writes=[imp_b])
                                if LV < 3:
                                    return
                                cx.op("dve", lambda: nc.vector.tensor_tensor(out=gm[:, 0:1], in0=at.rl[:, 0:1], in1=gt[qi][:, qb, 3 * h:3 * h + 1], op=ALU.mult),
                                      reads=[at.rl_b, gt_b[qi]], writes=[gm_b])
                                if LV < 4:
                                    return
                                cx.op("act", lambda: nc.scalar.activation(out=oacc[:, r, :], in_=at.pO[oi], func=AF.Copy, scale=gm[:, 0:1]),
                                      reads=[at.pO_b[oi], gm_b], writes=[oacc_b])
                                if r == 3 and os.environ.get("P4_NOTOPK", "0") != "1":
                                    topk()
                            return fin

                        at.run((lambda r=r, qi=qi, Q=Q, g=g: (lambda kt: [lambda o: nc.tensor.matmul(o, qg[qi][:, r, Q], k.kcT[:, g, 0:256], start=True, stop=True)]))(),
                               [qg_b[qi], k.kcv_b], 1, 256, (lambda qb=qb: (lambda kt: [(mC[:, qb, 0:256], [b_cm])]))(), SC,
                               (lambda g=g: (lambda kt, kb: k.vc[:, g, kb, :]))(), [k.kcv_b], mk_fin_c(),
                               x_fn=(lambda kb: ovl[:, kb, :]), x_reads=[b_c])
                    for r in (range(4) if "w" in BR else []):
                        h = g * 4 + r

                        def mk_fin_w(r=r, h=h, qb=qb, qi=qi, nw=len(wt), oacc=oacc, oacc_b=oacc_b):
                            def fin(oi, li):
                                if os.environ.get("P4_NOFINW", "0") == "1":
                                    return
                                at.rsum(li, nw, 2)
                                cx.op("dve", lambda: nc.vector.tensor_tensor(out=gm[:, 2:3], in0=at.rl[:, 2:3], in1=gt[qi][:, qb, 3 * h + 2:3 * h + 3], op=ALU.mult),
                                      reads=[at.rl_b, gt_b[qi]], writes=[gm_b])
                                ti_ = oi
                                cx.op("act", lambda: nc.scalar.activation(out=otmp[:, ti_, :], in_=at.pO[oi], func=AF.Copy, scale=gm[:, 2:3]),
                                      reads=[at.pO_b[oi], gm_b], writes=[otmp_b[ti_]])
                                cx.op("dve", lambda: nc.vector.tensor_tensor(out=oacc[:, r, :], in0=oacc[:, r, :], in1=otmp[:, ti_, :], op=ALU.add),
                                      reads=[otmp_b[ti_], oacc_b], writes=[oacc_b])
                            return fin

                        at.run((lambda r=r, qi=qi, Q=Q, wt=wt: (lambda i: [lambda o: nc.tensor.matmul(o, qg[qi][:, r, Q], kwin[0][:, wt[i][1] * TT:(wt[i][1] + 1) * TT], start=True, stop=True)]))(),
                               [qg_b[qi], kv_b], len(wt), 512, (lambda wt=wt, qb=qb: (lambda i: [(mW[:, wt[i][0], qb, :], [b_c])]))(), SC,
                               (lambda wt=wt: (lambda i, kb: vwin[0][:, wt[i][1] * 4 + kb, :]))(), [kv_b], mk_fin_w())
                    for r in (range(4) if "s" in BR else []):
                        h = g * 4 + r

                        def mk_masks_s(m=m, qb=qb, sneg=sneg, sneg_b=sneg_b):
                            def masks(kt):
                                ml = [(sneg[:, kt * 8:(kt + 1) * 8].unsqueeze(2).to_broadcast([128, 8, 64]), [sneg_b])]
                                if kt == 2 * m:
                                    ml.append((mA[:, qb, :], [b_c]))
                                if kt == 2 * m + 1:
                                    ml.append((mB[:, qb, :], [b_c]))
                                return ml
                            return masks

                        def mk_fin_s(r=r, h=h, qb=qb, qi=qi, nt=nt, g=g, m=m, oacc=oacc, oacc_b=oacc_b):
                            def fin(oi, li):
                                at.rsum(li, nt, 1)
                                cx.op("dve", lambda: nc.vector.tensor_tensor(out=gm[:, 1:2], in0=at.rl[:, 1:2], in1=gt[qi][:, qb, 3 * h + 1:3 * h + 2], op=ALU.mult),
                                      reads=[at.rl_b, gt_b[qi]], writes=[gm_b])
                                ti_ = oi
                                cx.op("act", lambda: nc.scalar.activation(out=otmp[:, ti_, :], in_=at.pO[oi], func=AF.Copy, scale=gm[:, 1:2]),
                                      reads=[at.pO_b[oi], gm_b], writes=[otmp_b[ti_]])
                                cx.op("dve", lambda: nc.vector.tensor_tensor(out=oacc[:, r, :], in0=oacc[:, r, :], in1=otmp[:, ti_, :], op=ALU.add),
                                      reads=[otmp_b[ti_], oacc_b], writes=[oacc_b])
                                if r == 3:
                                    for r2 in range(4):
                                        ot.put(oacc[:, r2, :], [oacc_b], qb, r2)
                                    if qb == 3:
                                        for r2 in range(4):
                                            ot.flush(r2, scr["oT"][g * 4 + r2, :, m * TT:(m + 1) * TT])
                            return fin

                        at.run((lambda r=r, qi=qi, Q=Q: (lambda kt: [lambda o: nc.tensor.matmul(o, qg[qi][:, r, Q], ksel[0][:, kt * TT:(kt + 1) * TT], start=True, stop=True)]))(),
                               [qg_b[qi], kv_b], nt, 512, mk_masks_s(), SC, (lambda kt, kb: vsel[0][:, kt * 4 + kb, :]), [kv_b], mk_fin_s(), copy_eng="act")
            at.drain()
        cx.barrier()


def emit_p4_sync(k, scr, cst, n_all=NT_ALL, n_my=NT_MY, groups=range(4), skip_cmp=False):
    nc, cx = k.nc, k.cx
    SC = 128.0 ** -0.5
    with ExitStack() as st:
        at = AttnSync(k, st, "p4ya")
        ot = OutT(k, st, "p4yo", 4)
        mA = sb(k, "p4y_mA", [128, 4, 512], F32, st)
        mB = sb(k, "p4y_mB", [128, 4, 512], F32, st)
        mW = sb(k, "p4y_mW", [128, 3, 4, 512], F32, st)
        mC = sb(k, "p4y_mC", [128, 4, 512], F32, st)
        vS = sb(k, "p4y_vS", [128, 4, 64], F32, st)
        aS = sb(k, "p4y_aS", [128, 4, 64], F32, st)
        ovl = sb(k, "p4y_ovl", [128, 4, 64], BF16, st)
        ovf = sb(k, "p4y_ovf", [128, 4, 64], F32, st)
        b_c = cx.buf("p4y_c")
        b_cm = cx.buf("p4y_cm")
        cx.dma("sp", mA[:], cst["mA"], b_c, writes=[b_c])
        cx.dma("sp", mB[:], cst["mB"], b_c, writes=[b_c])
        cx.dma("sp", mW[:], cst["mW"].rearrange("w p q k -> p w q k"), b_c, writes=[b_c])
        cx.dma("sp", ovf[:], cst["ovl"], b_c, writes=[b_c])
        cx.op("dve", lambda: nc.vector.tensor_copy(out=ovl[:], in_=ovf[:]), reads=[b_c], writes=[b_c])
        ksel = [sb(k, f"p4y_ksel{i}", [128, n_all * TT], BF16, st) for i in range(1)]
        kwin = [sb(k, f"p4y_kwin{i}", [128, n_all * TT], BF16, st) for i in range(1)]
        vsel = [sb(k, f"p4y_vsel{i}", [128, n_all * 4, 128], BF16, st) for i in range(1)]
        vwin = [sb(k, f"p4y_vwin{i}", [128, n_all * 4, 128], BF16, st) for i in range(1)]
        kv_b = cx.buf("p4y_kv")
        qg = [sb(k, f"p4y_q{i}", [128, 4, TT], BF16, st) for i in range(2)]
        qg_b = cx.bufs_n("p4y_q", 2)
        gt = [sb(k, f"p4y_gt{i}", [128, 4, 48], F32, st) for i in range(2)]
        gt_b = cx.bufs_n("p4y_gt", 2)
        oacc = sb(k, "p4y_oacc", [128, 4, 128], F32, st)
        oacc_b = cx.buf("p4y_oacc")
        pn = sb(k, "p4y_pn", [128, 256], BF16, st)
        pn_b = cx.buf("p4y_pn")
        pimp = ps(k, "p4y_pimp", [128, 64], F32, st)
        pimp_b = cx.buf("p4y_pimp")
        imp = sb(k, "p4y_imp", [128, 64], F32, st)
        imp_b = cx.buf("p4y_imp")
        wk = sb(k, "p4y_wk", [128, 64], F32, st)
        wk_b = cx.buf("p4y_wk")
        m16 = sb(k, "p4y_m16", [128, 16], F32, st)
        m16_b = cx.buf("p4y_m16")
        sneg = sb(k, "p4y_sneg", [128, 64], F32, st)
        sneg_b = cx.buf("p4y_sneg")
        gm = sb(k, "p4y_gm", [128, 4], F32, st)
        gm_b = cx.buf("p4y_gm")
        vsel_v = scr["vsel"].rearrange("(b p) c -> p b c", p=128)
        vwin_v = scr["vwin"].rearrange("(b p) c -> p b c", p=128)
        it = 0
        for g in groups:
            nb = n_all * 4
            cx.dma("sp", ksel[0][:], scr["kselT"][g, :, 0:n_all * TT], kv_b, writes=[kv_b])
            cx.dma("sp", kwin[0][:], scr["kwinT"][g, :, 0:n_all * TT], kv_b, writes=[kv_b])
            cx.dma("sp", vsel[0][:], vsel_v[:, 0:nb, g * 128:(g + 1) * 128], kv_b, writes=[kv_b])
            cx.dma("sp", vwin[0][:], vwin_v[:, 0:nb, g * 128:(g + 1) * 128], kv_b, writes=[kv_b])
            for m in range(n_my):
                qi = it % 2
                it += 1
                cx.dma("sp", qg[qi][:], scr["qT"][g * 4:(g + 1) * 4, :, m * TT:(m + 1) * TT].rearrange("h p t -> p h t"), qg_b[qi], writes=[qg_b[qi]])
                cx.dma("sp", gt[qi][:], scr["gtok"][m * TT:(m + 1) * TT, :].rearrange("(b p) c -> p b c", p=128), gt_b[qi], writes=[gt_b[qi]])
                cx.dma("sp", mC[:], cst["mC"][m], b_cm, writes=[b_cm])
                cx.dma("sp", vS[:], cst["vS"][m], b_cm, writes=[b_cm])
                cx.dma("sp", aS[:], cst["aS"][m], b_cm, writes=[b_cm])
                for qb in range(4):
                    Q = slice(qb * 128, (qb + 1) * 128)
                    if skip_cmp:
                        r0_ = m * TT + qb * 128
                        cx.dma("sp", oacc[:], scr["ocmp"][g, r0_:r0_ + 128, :, :], oacc_b, writes=[oacc_b])
                        cx.dma("sp", sneg[:], scr["snegs"][g, r0_:r0_ + 128, :], sneg_b, writes=[sneg_b])
                    else:
                        for r in range(4):
                            h = g * 4 + r
                            oi, li = at.run(lambda kt: [lambda o: nc.tensor.matmul(o, qg[qi][:, r, Q], k.kcT[:, g, 0:256], start=True, stop=True)],
                                            [qg_b[qi], k.kcv_b], 1, 256, lambda kt: [(mC[:, qb, 0:256], [b_cm])], SC,
                                            lambda kt, kb: k.vc[:, g, kb, :], [k.kcv_b])
                            at.rsum(li, 1, 0)
                            pidx = (at.c["P"] - 1) % 2
                            cx.op("dve", lambda: nc.vector.tensor_scalar(out=pn[:], in0=at.P[pidx][:, 0:256], scalar1=at.rl[:, 0:1], scalar2=None, op0=ALU.mult),
                                  reads=[at.P_b[pidx], at.rl_b], writes=[pn_b])
                            ti = at.nx("T")
                            cx.op("pe", [(lambda c=c: nc.tensor.transpose(at.pT[ti][:, c * 128:(c + 1) * 128], pn[:, c * 128:(c + 1) * 128], k.ident_b[:])) for c in range(2)],
                                  reads=[pn_b, k.b_const], writes=[at.pT_b[ti]])
                            pti = at.nx("PT")
                            cx.op("dve", lambda: nc.vector.tensor_copy(out=at.PT[pti][:, 0:256], in_=at.pT[ti][:, 0:256]), reads=[at.pT_b[ti]], writes=[at.PT_b[pti]])
                            cx.op("pe", [(lambda c=c: nc.tensor.matmul(pimp[:], at.PT[pti][:, c * 128:(c + 1) * 128], ovl[:, c, :], start=(r == 0 and c == 0), stop=(r == 3 and c == 1)))
                                         for c in range(2)], reads=[at.PT_b[pti], b_c], writes=[pimp_b])
                            cx.op("dve", lambda: nc.vector.tensor_tensor(out=gm[:, 0:1], in0=at.rl[:, 0:1], in1=gt[qi][:, qb, 3 * h:3 * h + 1], op=ALU.mult),
                                  reads=[at.rl_b, gt_b[qi]], writes=[gm_b])
                            cx.op("dve", lambda: nc.vector.tensor_scalar(out=oacc[:, r, :], in0=at.pO[oi], scalar1=gm[:, 0:1], scalar2=None, op0=ALU.mult),
                                  reads=[at.pO_b[oi], gm_b], writes=[oacc_b])
                        cx.op("dve", lambda: nc.vector.tensor_tensor(out=imp[:], in0=pimp[:], in1=vS[:, qb, :], op=ALU.mult), reads=[pimp_b, b_cm], writes=[imp_b])
                        cx.op("dve", lambda: nc.vector.tensor_tensor(out=imp[:], in0=imp[:], in1=aS[:, qb, :], op=ALU.add), reads=[imp_b, b_cm], writes=[imp_b])
                        cx.op("dve", lambda: nc.vector.max(out=m16[:, 0:8], in_=imp[:]), reads=[imp_b], writes=[m16_b])
                        cx.op("dve", lambda: nc.vector.match_replace(out=wk[:], in_to_replace=m16[:, 0:8], in_values=imp[:], imm_value=-3e38),
                              reads=[imp_b, m16_b], writes=[wk_b])
                        cx.op("dve", lambda: nc.vector.max(out=m16[:, 8:16], in_=wk[:]), reads=[wk_b], writes=[m16_b])
                        cx.op("dve", lambda: nc.vector.tensor_scalar(out=sneg[:], in0=imp[:], scalar1=m16[:, 15:16], scalar2=30000.0, op0=ALU.is_lt, op1=ALU.mult),
                              reads=[imp_b, m16_b], writes=[sneg_b])
                        cx.op("dve", lambda: nc.vector.tensor_scalar(out=sneg[:], in0=sneg[:], scalar1=-1.0, scalar2=None, op0=ALU.mult), reads=[sneg_b], writes=[sneg_b])
                    nt = min(2 * m + 2, n_all)
                    for r in range(4):
                        h = g * 4 + r

                        def masks(kt):
                            ml = [(sneg[:, kt * 8:(kt + 1) * 8].unsqueeze(2).to_broadcast([128, 8, 64]), [sneg_b])]
                            if kt == 2 * m:
                                ml.append((mA[:, qb, :], [b_c]))
                            if kt == 2 * m + 1:
                                ml.append((mB[:, qb, :], [b_c]))
                            return ml
                        oi, li = at.run(lambda kt: [lambda o: nc.tensor.matmul(o, qg[qi][:, r, Q], ksel[0][:, kt * TT:(kt + 1) * TT], start=True, stop=True)],
                                        [qg_b[qi], kv_b], nt, 512, masks, SC, lambda kt, kb: vsel[0][:, kt * 4 + kb, :], [kv_b])
                        at.rsum(li, nt, 1)
                        cx.op("dve", lambda: nc.vector.tensor_tensor(out=gm[:, 1:2], in0=at.rl[:, 1:2], in1=gt[qi][:, qb, 3 * h + 1:3 * h + 2], op=ALU.mult),
                              reads=[at.rl_b, gt_b[qi]], writes=[gm_b])
                        cx.op("dve", lambda: nc.vector.scalar_tensor_tensor(out=oacc[:, r, :], in0=at.pO[oi], scalar=gm[:, 1:2], in1=oacc[:, r, :], op0=ALU.mult, op1=ALU.add),
                              reads=[at.pO_b[oi], gm_b, oacc_b], writes=[oacc_b])
                    wt = [(w, 2 * m - 1 + w) for w in range(3) if 0 <= 2 * m - 1 + w < n_all]
                    for r in range(4):
                        h = g * 4 + r
                        oi, li = at.run(lambda i: [lambda o: nc.tensor.matmul(o, qg[qi][:, r, Q], kwin[0][:, wt[i][1] * TT:(wt[i][1] + 1) * TT], start=True, stop=True)],
                                        [qg_b[qi], kv_b], len(wt), 512, lambda i: [(mW[:, wt[i][0], qb, :], [b_c])], SC,
                                        lambda i, kb: vwin[0][:, wt[i][1] * 4 + kb, :], [kv_b])
                        at.rsum(li, len(wt), 2)
                        cx.op("dve", lambda: nc.vector.tensor_tensor(out=gm[:, 2:3], in0=at.rl[:, 2:3], in1=gt[qi][:, qb, 3 * h + 2:3 * h + 3], op=ALU.mult),
                              reads=[at.rl_b, gt_b[qi]], writes=[gm_b])
                        cx.op("dve", lambda: nc.vector.scalar_tensor_tensor(out=oacc[:, r, :], in0=at.pO[oi], scalar=gm[:, 2:3], in1=oacc[:, r, :], op0=ALU.mult, op1=ALU.add),
                              reads=[at.pO_b[oi], gm_b, oacc_b], writes=[oacc_b])
                    for r in range(4):
                        ot.put(oacc[:, r, :], [oacc_b], qb, r)
                for r in range(4):
                    ot.flush(r, scr["oT"][g * 4 + r, :, m * TT:(m + 1) * TT])
        cx.barrier()


class LNorm:
    def __init__(self, k, st, pfx, g_d, b_d):
        cx = k.cx
        self.k = k
        self.g_d, self.b_d = g_d, b_d
        self.stt = sb(k, pfx + "_st", [128, 8], F32, st)
        self.stt_b = cx.buf(pfx + "_st")
        self.rows = [sb(k, f"{pfx}_row{i}", [128, 2, 512], F32, st) for i in range(2)]
        self.rows_b = cx.bufs_n(pfx + "_row", 2)
        self.n = 0

    def run(self, y, y_b, junk, junk_b):
        k = self.k
        nc, cx = k.nc, k.cx
        s = self.stt
        sb_ = self.stt_b
        cx.op("act", lambda: nc.scalar.activation(out=junk, in_=y, func=AF.Identity, accum_out=s[:, 0:1]), reads=[y_b], writes=[junk_b, sb_])
        cx.op("act", lambda: nc.scalar.activation(out=junk, in_=y, func=AF.Square, accum_out=s[:, 1:2]), reads=[y_b], writes=[junk_b, sb_])
        cx.op("dve", lambda: nc.vector.tensor_scalar(out=s[:, 2:3], in0=s[:, 0:1], scalar1=1.0 / D, scalar2=None, op0=ALU.mult), reads=[sb_], writes=[sb_])
        cx.op("dve", lambda: nc.vector.tensor_tensor(out=s[:, 3:4], in0=s[:, 2:3], in1=s[:, 2:3], op=ALU.mult), reads=[sb_], writes=[sb_])
        cx.op("dve", lambda: nc.vector.scalar_tensor_tensor(out=s[:, 4:5], in0=s[:, 1:2], scalar=1.0 / D, in1=s[:, 3:4], op0=ALU.mult, op1=ALU.subtract),
              reads=[sb_], writes=[sb_])
        cx.op("act", lambda: nc.scalar.activation(out=s[:, 5:6], in_=s[:, 4:5], func=AF.Sqrt, bias=k.eps5[:], scale=1.0), reads=[sb_, k.b_const], writes=[sb_])
        cx.op("dve", lambda: nc.vector.reciprocal(out=s[:, 5:6], in_=s[:, 5:6]), reads=[sb_], writes=[sb_])
        cx.op("dve", lambda: nc.vector.tensor_scalar(out=y, in0=y, scalar1=s[:, 2:3], scalar2=s[:, 5:6], op0=ALU.subtract, op1=ALU.mult),
              reads=[y_b, sb_], writes=[y_b])
        for fc in range(8):
            ri = self.n % 2
            self.n += 1
            C = slice(fc * 512, (fc + 1) * 512)
            cx.dma("sp", self.rows[ri][:, 0, :], self.g_d[0:1, C].to_broadcast([128, 512]), self.rows_b[ri], writes=[self.rows_b[ri]])
            cx.dma("sp", self.rows[ri][:, 1, :], self.b_d[0:1, C].to_broadcast([128, 512]), self.rows_b[ri], writes=[self.rows_b[ri]])
            cx.op("dve", lambda: nc.vector.tensor_tensor(out=y[:, C], in0=y[:, C], in1=self.rows[ri][:, 0, :], op=ALU.mult), reads=[y_b, self.rows_b[ri]], writes=[y_b])
            cx.op("dve", lambda: nc.vector.tensor_tensor(out=y[:, C], in0=y[:, C], in1=self.rows[ri][:, 1, :], op=ALU.add), reads=[y_b, self.rows_b[ri]], writes=[y_b])


def emit_p5(k, scr, xq, w_out, modrow, ln_g, ln_b, alpha, n_my=NT_MY):
    nc, cx = k.nc, k.cx
    with ExitStack() as st:
        oT = sb(k, "p5_oT", [128, 32, TT], BF16, st)
        oT_b = cx.buf("p5_oT")
        yb = sb(k, "p5_y", [128, 4, D], F32, st)
        y_b = cx.bufs_n("p5_y", 4)
        slabs = [sb(k, f"p5_slab{i}", [128, 32, 512], BF16, st) for i in range(2)]
        slab_b = cx.bufs_n("p5_slab", 2)
        grow = [sb(k, f"p5_grow{i}", [128, 512], F32, st) for i in range(2)]
        grow_b = cx.bufs_n("p5_grow", 2)
        tmp = [sb(k, f"p5_tmp{i}", [128, 512], F32, st) for i in range(2)]
        tmp_b = cx.bufs_n("p5_tmp", 2)
        junk = sb(k, "p5_junk", [128, D], BF16, st)
        junk_b = cx.buf("p5_junk")
        hst = [sb(k, f"p5_hst{i}", [128, 32, 128], BF16, st) for i in range(2)]
        hst_b = cx.bufs_n("p5_hst", 2)
        pa = [ps(k, f"p5_pa{i}", [128, 512], F32, st) for i in range(3)]
        pa_b = cx.bufs_n("p5_pa", 3)
        ptr = [ps(k, f"p5_ptr{i}", [128, 512], F32, st) for i in range(2)]
        ptr_b = cx.bufs_n("p5_ptr", 2)
        ln = LNorm(k, st, "p5_ln", ln_g, ln_b)
        n_sl = 0
        n_pa = 0
        n_tr = 0
        n_h = 0
        for m in range(n_my):
            cx.dma("sp", oT[:], scr["oT"][:, :, m * TT:(m + 1) * TT].rearrange("c p t -> p c t"), oT_b, writes=[oT_b])
            for tb in range(4):
                r0 = m * TT + tb * 128
                cx.dma("sp", yb[:, tb, :], xq[r0:r0 + 128, :], y_b[tb], writes=[y_b[tb]])
            for fc in range(8):
                si = n_sl % 2
                n_sl += 1
                C = slice(fc * 512, (fc + 1) * 512)
                cx.dma("pool", slabs[si][:], w_out[fc], slab_b[si], writes=[slab_b[si]])
                cx.dma("sp", grow[si][:], modrow[2:3, C].to_broadcast([128, 512]), grow_b[si], writes=[grow_b[si]])
                for tb in range(4):
                    pi = n_pa % 3
                    n_pa += 1
                    cx.op("pe", [(lambda j=j: nc.tensor.matmul(pa[pi][:], oT[:, j, tb * 128:(tb + 1) * 128], slabs[si][:, j, :], start=(j == 0), stop=(j == 31)))
                                 for j in range(32)], reads=[oT_b, slab_b[si]], writes=[pa_b[pi]])
                    ti = pi % 2
                    cx.op("dve", lambda: nc.vector.tensor_tensor(out=tmp[ti][:], in0=pa[pi][:], in1=grow[si][:], op=ALU.mult),
                          reads=[pa_b[pi], grow_b[si]], writes=[tmp_b[ti]])
                    cx.op("dve", lambda: nc.vector.scalar_tensor_tensor(out=yb[:, tb, C], in0=yb[:, tb, C], scalar=alpha, in1=tmp[ti][:], op0=ALU.mult, op1=ALU.add),
                          reads=[y_b[tb], tmp_b[ti]], writes=[y_b[tb]])
            for tb in range(4):
                r0 = m * TT + tb * 128
                ln.run(yb[:, tb, :], y_b[tb], junk[:], junk_b)
                cx.dma("sp", scr["x1"][r0:r0 + 128, :], yb[:, tb, :], y_b[tb], reads=[y_b[tb]])
                hi = n_h % 2
                n_h += 1
                for j4 in range(8):
                    pi = n_tr % 2
                    n_tr += 1
                    cx.op("pe", [(lambda i=i: nc.tensor.transpose(ptr[pi][:, i * 128:(i + 1) * 128], yb[:, tb, (j4 * 4 + i) * 128:(j4 * 4 + i + 1) * 128], k.ident_f[:]))
                                 for i in range(4)], reads=[y_b[tb], k.b_const], writes=[ptr_b[pi]])
                    for i in range(4):
                        j = j4 * 4 + i
                        cx.op("act", lambda: nc.scalar.activation(out=hst[hi][:, j, :], in_=ptr[pi][:, i * 128:(i + 1) * 128], func=AF.Identity,
                                                                  scale=k.modT[:, 128 + j:129 + j], bias=k.modT[:, 96 + j:97 + j]),
                              reads=[ptr_b[pi], k.b_modT], writes=[hst_b[hi]])
                cx.dma("sp", scr["h2T"][:, :, r0:r0 + 128].rearrange("c p t -> p c t"), hst[hi][:], hst_b[hi], reads=[hst_b[hi]])
        cx.barrier()


def emit_p6(k, scr, w1_d, w2_d, modrow, ln_g, ln_b, alpha, out_d, n_my=NT_MY, n_sc=DFF // 256):
    nc, cx = k.nc, k.cx
    with ExitStack() as st:
        h2 = sb(k, "p6_h2", [128, 32, TT], BF16, st)
        h2_b = cx.buf("p6_h2")
        acc = sb(k, "p6_acc", [128, 4, D], F32, st)
        acc_b = cx.bufs_n("p6_acc", 4)
        accf_b = [[cx.buf(f"p6_acc{tb}_{fc}") for fc in range(8)] for tb in range(4)]
        etmp = [sb(k, f"p6_etmp{i}", [128, 512], F32, st) for i in range(3)]
        etmp_b = cx.bufs_n("p6_etmp", 3)
        n_e = 0
        s1 = [sb(k, f"p6_s1{i}", [128, 32, 256], BF16, st) for i in range(2)]
        s1_b = cx.bufs_n("p6_s1", 2)
        s2 = [sb(k, f"p6_s2{i}", [128, 2, D], BF16, st) for i in range(2)]
        s2_b = cx.bufs_n("p6_s2", 2)
        uT = [sb(k, f"p6_uT{i}", [128, 2, TT], BF16, st) for i in range(2)]
        uT_b = cx.bufs_n("p6_uT", 2)
        rl = [sb(k, f"p6_rl{i}", [128, TT], F32, st) for i in range(2)]
        rl_b = cx.bufs_n("p6_rl", 2)
        xr = sb(k, "p6_xr", [128, D], F32, st)
        xr_b = cx.buf("p6_xr")
        grow = [sb(k, f"p6_grow{i}", [128, 512], F32, st) for i in range(2)]
        grow_b = cx.bufs_n("p6_grow", 2)
        pu = [ps(k, f"p6_pu{i}", [128, 512], F32, st) for i in range(2)]
        pu_b = cx.bufs_n("p6_pu", 2)
        po = [ps(k, f"p6_po{i}", [128, 512], F32, st) for i in range(6)]
        po_b = cx.bufs_n("p6_po", 6)
        ln = LNorm(k, st, "p6_ln", ln_g, ln_b)
        n_u = 0
        n_o = 0
        n_g = 0
        tot = n_my * n_sc

        def load1(idx):
            if idx < tot:
                cx.dma("pool", s1[idx % 2][:], w1_d[idx % n_sc], s1_b[idx % 2], writes=[s1_b[idx % 2]])

        def load2(idx):
            if idx < tot:
                cx.dma("pool", s2[idx % 2][:], w2_d[idx % n_sc], s2_b[idx % 2], writes=[s2_b[idx % 2]])

        def ff1_steps(idx):
            nonlocal n_u
            si = idx % 2
            ui = idx % 2
            for c in range(2):
                pi = n_u % 2
                n_u += 1
                h = cx.op_begin("pe", reads=[s1_b[si], h2_b], writes=[pu_b[pi]])
                for j in range(32):
                    fn = (lambda j=j: nc.tensor.matmul(pu[pi][:], s1[si][:, j, c * 128:(c + 1) * 128], h2[:, j, :], start=(j == 0), stop=(j == 31)))
                    if j < 31:
                        cx.op_piece(h, fn)
                    else:
                        cx.op_end(h, fn)
                        cx.op("act", lambda: nc.scalar.activation(out=rl[pi][:], in_=pu[pi][:], func=AF.Relu), reads=[pu_b[pi]], writes=[rl_b[pi]])
                        cx.op("pool", lambda: nc.gpsimd.tensor_tensor(out=uT[ui][:, c, :], in0=rl[pi][:], in1=rl[pi][:], op=ALU.mult),
                              reads=[rl_b[pi]], writes=[uT_b[ui]])
                    yield

        def ff1(idx):
            for _ in ff1_steps(idx):
                pass

        load1(0)
        load2(0)
        load1(1)
        load2(1)
        def start_tile(m):
            cx.dma("sp", h2[:], scr["h2T"][:, :, m * TT:(m + 1) * TT].rearrange("c p t -> p c t"), h2_b, writes=[h2_b])
            ff1(m * n_sc)

        for m in range(n_my):
            start_tile(m)
            for sc in range(n_sc):
                idx = m * n_sc + sc
                si = idx % 2
                ui = idx % 2
                load1(idx + 2)
                gen = ff1_steps(idx + 1) if sc + 1 < n_sc else None
                for tb in range(4):
                    for fc in range(8):
                        if gen is not None:
                            next(gen, None)
                            next(gen, None)
                        oi = n_o % 6
                        n_o += 1
                        C = slice(fc * 512, (fc + 1) * 512)
                        cx.op("pe", [(lambda c=c: nc.tensor.matmul(po[oi][:], uT[ui][:, c, tb * 128:(tb + 1) * 128], s2[si][:, c, C], start=(c == 0), stop=(c == 1)))
                                     for c in range(2)], reads=[uT_b[ui], s2_b[si]], writes=[po_b[oi]])
                        ab = accf_b[tb][fc]
                        if sc == 0:
                            cx.op("dve", lambda: nc.vector.tensor_copy(out=acc[:, tb, C], in_=po[oi][:]), reads=[po_b[oi]], writes=[ab, acc_b[tb]])
                        elif n_o % 2 == 0:
                            cx.op("dve", lambda: nc.vector.tensor_tensor(out=acc[:, tb, C], in0=acc[:, tb, C], in1=po[oi][:], op=ALU.add),
                                  reads=[po_b[oi], ab], writes=[ab])
                        else:
                            ei = n_e % 3
                            n_e += 1
                            cx.op("act", lambda: nc.scalar.copy(out=etmp[ei][:], in_=po[oi][:]), reads=[po_b[oi]], writes=[etmp_b[ei]])
                            cx.op("pool", lambda: nc.gpsimd.tensor_tensor(out=acc[:, tb, C], in0=acc[:, tb, C], in1=etmp[ei][:], op=ALU.add),
                                  reads=[etmp_b[ei], ab], writes=[ab])
                if gen is not None:
                    for _ in gen:
                        pass
                load2(idx + 2)
            for tb in range(4):
                r0 = m * TT + tb * 128
                cx.dma("sp", xr[:], scr["x1"][r0:r0 + 128, :], xr_b, writes=[xr_b])
                cx.op("dve", lambda: nc.vector.tensor_copy(out=acc[:, tb, 0:1], in_=acc[:, tb, 0:1]), reads=accf_b[tb], writes=[acc_b[tb]])
                for fc in range(8):
                    gi = n_g % 2
                    n_g += 1
                    C = slice(fc * 512, (fc + 1) * 512)
                    cx.dma("sp", grow[gi][:], modrow[5:6, C].to_broadcast([128, 512]), grow_b[gi], writes=[grow_b[gi]])
                    cx.op("dve", lambda: nc.vector.tensor_tensor(out=acc[:, tb, C], in0=acc[:, tb, C], in1=grow[gi][:], op=ALU.mult),
                          reads=[acc_b[tb], grow_b[gi]], writes=[acc_b[tb]])
                cx.op("dve", lambda: nc.vector.scalar_tensor_tensor(out=acc[:, tb, :], in0=xr[:], scalar=alpha, in1=acc[:, tb, :], op0=ALU.mult, op1=ALU.add),
                      reads=[xr_b, acc_b[tb]], writes=[acc_b[tb]])
                ln.run(acc[:, tb, :], acc_b[tb], xr[:], xr_b)
                cx.dma("sp", out_d[r0:r0 + 128, :], acc[:, tb, :], acc_b[tb], reads=[acc_b[tb]])
        cx.barrier()


ALPHA_C = 2.0 ** 0.25
CONST_NAMES = ("ident_f", "prot_p", "prot_m", "invf", "mA", "mB", "mW", "mC", "vS", "aS", "ovl")


def setup(nc, stack):
    k = K()
    k.nc = nc
    k.stack = stack
    k.cx = Ctx(nc, stack)
    cx = k.cx
    I = "ExternalInput"
    k.cd = {}
    shapes = {"ident_f": [128, 128], "prot_p": [128, 128], "prot_m": [128, 128], "invf": [128, 2], "mA": [128, 4, 512], "mB": [128, 4, 512],
              "mW": [3, 128, 4, 512], "mC": [4, 128, 4, 512], "vS": [4, 128, 4, 64], "aS": [4, 128, 4, 64], "ovl": [128, 4, 64]}
    for n in CONST_NAMES:
        k.cd[n] = dram(nc, "c_" + n, shapes[n], F32, I)
    k.ident_f = sb(k, "ident_f", [128, 128], F32)
    k.prot_p = sb(k, "prot_p", [128, 128], F32)
    k.prot_m = sb(k, "prot_m", [128, 128], F32)
    k.invf = sb(k, "invf", [128, 2], F32)
    k.modT = sb(k, "modT", [128, 192], F32)
    k.negpi = sb(k, "negpi", [128, 1], F32)
    k.ones_bf = sb(k, "ones_bf", [128, 128], BF16)
    k.ident_b = sb(k, "ident_b", [128, 128], BF16)
    k.eps6 = sb(k, "eps6", [128, 1], F32)
    k.eps5 = sb(k, "eps5", [128, 1], F32)
    k.kcv_b = cx.buf("kcv")
    k.b_const = cx.buf("const")
    k.b_modT = cx.buf("modT")
    for dst, src in ((k.ident_f, "ident_f"), (k.prot_p, "prot_p"), (k.prot_m, "prot_m"), (k.invf, "invf")):
        cx.dma("sp", dst[:], k.cd[src], k.b_const, writes=[k.b_const])
    cx.op("dve", lambda: nc.vector.memset(k.negpi[:], -PI), writes=[k.b_const])
    cx.op("dve", lambda: nc.vector.memset(k.ones_bf[:], 1.0), writes=[k.b_const])
    cx.op("dve", lambda: nc.vector.memset(k.eps6[:], 1e-6), writes=[k.b_const])
    cx.op("dve", lambda: nc.vector.memset(k.eps5[:], 1e-5), writes=[k.b_const])
    cx.op("dve", lambda: nc.vector.tensor_copy(out=k.ident_b[:], in_=k.ident_f[:]), reads=[k.b_const], writes=[k.b_const])
    return k


def build_program():
    nc = bass.Bass("TRN2", target_bir_lowering=False)
    with ExitStack() as st:
        k = setup(nc, st)
        I = "ExternalInput"
        x_all = dram(nc, "x_all", [S, D], F32, I)
        xq = dram(nc, "xq", [NT_MY * TT, D], F32, I)
        cT = dram(nc, "cT", [128, 32], F32, I)
        pos_all = dram(nc, "pos_all", [1, S], I32, I)
        pos_q = dram(nc, "pos_q", [1, NT_MY * TT], I32, I)
        w_ada = dram(nc, "w_ada", [48, 128, 32, 512], F32, I)
        b_ada = dram(nc, "b_ada", [1, 6 * D], F32, I)
        w_in = dram(nc, "w_in", [len(W_IN_SLABS), 128, 32, 256], F32, I)
        k1 = dram(nc, "k1", [128, 32, 256], F32, I)
        k2 = dram(nc, "k2", [128, 2, 128], F32, I)
        v1 = dram(nc, "v1", [128, 32, 256], F32, I)
        v2 = dram(nc, "v2", [128, 2, 128], F32, I)
        posk = dram(nc, "posk", [128, 32], F32, I)
        posv = dram(nc, "posv", [128, 32], F32, I)
        qnT = dram(nc, "qnT", [128, 12], F32, I)
        kvnT = dram(nc, "kvnT", [128, 4], F32, I)
        w_uq = dram(nc, "w_uq", [16, 128, 12, 192], F32, I)
        w_ukv = dram(nc, "w_ukv", [16, 128, 4, 256], F32, I)
        w_out = dram(nc, "w_out", [8, 128, 32, 512], F32, I)
        l1g = dram(nc, "l1g", [1, D], F32, I)
        l1b = dram(nc, "l1b", [1, D], F32, I)
        l2g = dram(nc, "l2g", [1, D], F32, I)
        l2b = dram(nc, "l2b", [1, D], F32, I)
        w1 = dram(nc, "w_ff1", [64, 128, 32, 256], F32, I)
        w2 = dram(nc, "w_ff2", [64, 128, 2, 4096], F32, I)
        out = dram(nc, "out", [NT_MY * TT, D], F32, "ExternalOutput")
        modrow = dram(nc, "s_modrow", [6, D], F32)
        NQ = NT_MY * TT
        scr = dict(
            kcmpT=dram(nc, "s_kcmpT", [4, 128, S], BF16), vcmpT=dram(nc, "s_vcmpT", [4, 128, S], BF16),
            kselT=dram(nc, "s_kselT", [4, 128, S], BF16), kwinT=dram(nc, "s_kwinT", [4, 128, S], BF16),
            vsel=dram(nc, "s_vsel", [S, 512], BF16), vwin=dram(nc, "s_vwin", [S, 512], BF16),
            ckvf=dram(nc, "s_ckvf", [4, 128, S], F32), kpeT=dram(nc, "s_kpeT", [64, S], BF16),
            qT=dram(nc, "s_qT", [16, 128, NQ], BF16), gtok=dram(nc, "s_gtok", [NQ, 48], F32), cqf=dram(nc, "s_cqf", [12, 128, NQ], F32),
            qmnT=dram(nc, "s_qmnT", [16, 128, NQ], BF16), qmpT=dram(nc, "s_qmpT", [16, 64, NQ], BF16),
            ocmp=dram(nc, "s_ocmp", [4, NQ, 4, 128], F32), snegs=dram(nc, "s_snegs", [4, NQ, 64], F32),
            oT=dram(nc, "s_oT", [32, 128, NQ], BF16), x1=dram(nc, "s_x1", [NQ, D], F32), h2T=dram(nc, "s_h2T", [32, 128, NQ], BF16))
        emit_p0(k, cT, w_ada, b_ada, modrow)
        emit_p1(k, x_all, xq, pos_all, pos_q, w_in, scr)
        emit_p2(k, scr, w_uq, w_ukv, qnT, kvnT, pos_q, k.cd["mA"], k.cd["mB"])
        with ExitStack() as st34:
            k.kcT = sb(k, "kcT", [128, 4, 512], BF16, st34)
            k.vc = sb(k, "vc", [128, 4, 4, 128], BF16, st34)
            emit_p3(k, scr, k1, k2, v1, v2, posk, posv)
            emit_p4_sync(k, scr, k.cd)
        emit_p5(k, scr, xq, w_out, modrow, l1g, l1b, ALPHA_C)
        emit_p6(k, scr, w1, w2, modrow, l2g, l2b, ALPHA_C, out)
        k.cx.final_wait()
    return nc


def kernel(x, c, positions, w_ada, b_ada, w_in, nsa_pos_k, nsa_pos_v, nsa_cmp_k1, nsa_cmp_k2,
           nsa_cmp_v1, nsa_cmp_v2, mla_q_norm, mla_kv_norm, mla_w_uq, mla_w_ukv, w_out,
           ln1_g, ln1_b, w_ff1, w_ff2, ln2_g, ln2_b):
    f32 = np.float32
    A = lambda a: np.asarray(a)
    x = A(x); c = A(c); positions = A(positions)
    nc = build_program()
    tw = host_tile_weights(A(w_ada)[0], A(w_in)[0], A(nsa_cmp_k1)[0], A(nsa_cmp_k2)[0], A(nsa_cmp_v1)[0], A(nsa_cmp_v2)[0],
                           A(mla_w_uq)[0], A(mla_w_ukv)[0], A(w_out)[0], A(w_ff1)[0], A(w_ff2)[0])
    shared = {
        "b_ada": A(b_ada).reshape(1, -1),
        "posk": np.ascontiguousarray(A(nsa_pos_k)[0].T), "posv": np.ascontiguousarray(A(nsa_pos_v)[0].T),
        "qnT": np.ascontiguousarray(A(mla_q_norm)[0].reshape(12, 128).T), "kvnT": np.ascontiguousarray(A(mla_kv_norm)[0].reshape(4, 128).T),
        "l1g": A(ln1_g).reshape(1, -1), "l1b": A(ln1_b).reshape(1, -1), "l2g": A(ln2_g).reshape(1, -1), "l2b": A(ln2_b).reshape(1, -1),
    }
    shared.update(tw)
    consts = [host_constants(0), host_constants(1)]
    in_maps = []
    rows_of = []
    for core in range(8):
        b, hf = core // 2, core % 2
        rows = np.concatenate([np.arange((2 * m + hf) * TT, (2 * m + hf + 1) * TT) for m in range(NT_MY)])
        rows_of.append((b, rows))
        im = dict(shared)
        im["x_all"] = x[b]
        im["xq"] = np.ascontiguousarray(x[b][rows])
        im["cT"] = np.ascontiguousarray(c[b].reshape(32, 128).T)
        im["pos_all"] = np.ascontiguousarray(positions[b].reshape(1, -1)).astype(np.int32)
        im["pos_q"] = np.ascontiguousarray(positions[b][rows].reshape(1, -1)).astype(np.int32)
        for n in CONST_NAMES:
            im["c_" + n] = consts[hf][n]
        in_maps.append(im)
    res = run_bass_kernel_spmd(nc, in_maps, core_ids=list(range(8)))
    outp = np.empty((4, S, D), dtype=f32)
    for core in range(8):
        b, rows = rows_of[core]
        outp[b, rows] = np.asarray(res.results[core]["out"], dtype=f32)
    return outp
```
